# Optimizing a Trainium2 kernel written in Bass

```python
import functools
import jax, jax.numpy as jnp
from jax import lax
import numpy as np

D_MODEL = 1024
BATCH = 8
SEQ = 2048
DEPTH = 2

CHUNK = 64
Q_BLOCK = 128
MLA_HEADS = 8
MLA_NOPE = 64
MLA_ROPE = 32
MLA_V = 64
Q_LORA = 384
KV_LORA = 256
ROPE_THETA = 10000.0
SB_HEADS = 8
SB_DIM = 64
C_HEADS = 16
C_DIM = 64
LEFT_CHUNKS = 8
BAND = (LEFT_CHUNKS + 1) * CHUNK
REL_CLIP = 256
D_FF = -(-8 * D_MODEL // (3 * 256)) * 256
EVEN_IN = Q_LORA + KV_LORA + MLA_ROPE + 3 * SB_HEADS * SB_DIM
MIX_EVEN = MLA_HEADS * MLA_V + SB_HEADS * SB_DIM
MIX_ODD = C_HEADS * C_DIM
N_EVEN = (DEPTH + 1) // 2
N_ODD = DEPTH // 2
RMS_EPS = 1e-6

kernel_name = "hybrid_mla_stickbreak_chunkband_encoder"


def rms_norm(x, g):
    x32 = x.astype(jnp.float32)
    y = x32 * lax.rsqrt(jnp.mean(x32 * x32, axis=-1, keepdims=True) + RMS_EPS)
    return (y * g.astype(jnp.float32)).astype(x.dtype)


def rope_tables(seq, dim):
    pos = jnp.arange(seq, dtype=jnp.float32)
    inv_freq = ROPE_THETA ** (-jnp.arange(0, dim, 2, dtype=jnp.float32) / dim)
    ang = pos[:, None] * inv_freq[None, :]
    return jnp.cos(ang), jnp.sin(ang)


def apply_rope(x, cos, sin):
    half = x.shape[-1] // 2
    c, s = cos.astype(x.dtype), sin.astype(x.dtype)
    x1, x2 = x[..., :half], x[..., half:]
    return jnp.concatenate([x1 * c - x2 * s, x2 * c + x1 * s], axis=-1)


def swiglu(u, w_gate, w_up, w_down):
    return (jax.nn.silu(u @ w_gate) * (u @ w_up)) @ w_down


def mla_stick_breaking_mixer(u, w_in, g_cq, w_uq, g_ckv, w_ukv, w_out):
    bsz, seq, _ = u.shape
    proj = u @ w_in
    o1 = Q_LORA
    o2 = o1 + KV_LORA
    o3 = o2 + MLA_ROPE
    nb = SB_HEADS * SB_DIM
    c_q, c_kv, k_r = proj[..., :o1], proj[..., o1:o2], proj[..., o2:o3]
    q_b = proj[..., o3:o3 + nb].reshape(bsz, seq, SB_HEADS, SB_DIM)
    k_b = proj[..., o3 + nb:o3 + 2 * nb].reshape(bsz, seq, SB_HEADS, SB_DIM)
    v_b = proj[..., o3 + 2 * nb:].reshape(bsz, seq, SB_HEADS, SB_DIM)

    cos, sin = rope_tables(seq, MLA_ROPE)
    q_a = (rms_norm(c_q, g_cq) @ w_uq).reshape(bsz, seq, MLA_HEADS, MLA_NOPE + MLA_ROPE)
    q_a = jnp.concatenate([q_a[..., :MLA_NOPE],
                           apply_rope(q_a[..., MLA_NOPE:], cos[:, None, :], sin[:, None, :])], axis=-1)
    kv = (rms_norm(c_kv, g_ckv) @ w_ukv).reshape(bsz, seq, MLA_HEADS, MLA_NOPE + MLA_V)
    k_rope = apply_rope(k_r, cos, sin)
    k_a = jnp.concatenate([kv[..., :MLA_NOPE],
                           jnp.broadcast_to(k_rope[:, :, None, :], (bsz, seq, MLA_HEADS, MLA_ROPE))], axis=-1)
    v_a = kv[..., MLA_NOPE:]
    scale_a = (MLA_NOPE + MLA_ROPE) ** -0.5
    scale_b = SB_DIM ** -0.5

    outs_a, outs_b = [], []
    for blk in range(seq // Q_BLOCK):
        t0 = blk * Q_BLOCK
        kend = t0 + Q_BLOCK
        t_pos = t0 + jnp.arange(Q_BLOCK)[:, None]
        s_pos = jnp.arange(kend)[None, :]
        s_a = jnp.einsum('bqhd,bkhd->bhqk', q_a[:, t0:kend], k_a[:, :kend]).astype(jnp.float32) * scale_a
        chunk_ok = (s_pos // CHUNK) <= (t_pos // CHUNK)
        p_a = jax.nn.softmax(jnp.where(chunk_ok, s_a, -jnp.inf), axis=-1)
        outs_a.append(jnp.einsum('bhqk,bkhd->bqhd', p_a.astype(v_a.dtype), v_a[:, :kend]))
        z = jnp.einsum('bqhd,bkhd->bhqk', q_b[:, t0:kend], k_b[:, :kend]).astype(jnp.float32) * scale_b
        before = s_pos < t_pos
        log_keep = jnp.where(before, jax.nn.log_sigmoid(-z), 0.0)
        log_between = lax.cumsum(log_keep, axis=3, reverse=True) - log_keep
        w_b = jnp.where(before, jnp.exp(jax.nn.log_sigmoid(z) + log_between), 0.0)
        outs_b.append(jnp.einsum('bhqk,bkhd->bqhd', w_b.astype(v_b.dtype), v_b[:, :kend]))

    o_a = jnp.concatenate(outs_a, axis=1).reshape(bsz, seq, MLA_HEADS * MLA_V)
    o_b = jnp.concatenate(outs_b, axis=1).reshape(bsz, seq, SB_HEADS * SB_DIM)
    return jnp.concatenate([o_a, o_b], axis=-1) @ w_out


def chunk_band_mixer(u, w_qkv, rel_bias, w_out):
    bsz, seq, _ = u.shape
    n_chunks = seq // CHUNK
    qkv = (u @ w_qkv).reshape(bsz, n_chunks, CHUNK, 3, C_HEADS, C_DIM)
    q, k, v = qkv[:, :, :, 0], qkv[:, :, :, 1], qkv[:, :, :, 2]
    pad = ((0, 0), (LEFT_CHUNKS, 0), (0, 0), (0, 0), (0, 0))
    k_p = jnp.pad(k, pad)
    v_p = jnp.pad(v, pad)
    scores = jnp.concatenate(
        [jnp.einsum('bnqhd,bnkhd->bnhqk', q, k_p[:, i:i + n_chunks]) for i in range(LEFT_CHUNKS + 1)],
        axis=-1).astype(jnp.float32) * (C_DIM ** -0.5)
    q_in = jnp.arange(CHUNK)[:, None]
    j = jnp.arange(BAND)[None, :]
    rel = (LEFT_CHUNKS - j // CHUNK) * CHUNK + q_in - j % CHUNK
    bias = rel_bias[:, jnp.clip(rel, -REL_CLIP, REL_CLIP) + REL_CLIP]
    valid = (jnp.arange(n_chunks)[:, None] - LEFT_CHUNKS + j // CHUNK) >= 0
    scores = scores + bias.astype(jnp.float32)[None, None]
    p = jax.nn.softmax(jnp.where(valid[None, :, None, None, :], scores, -jnp.inf), axis=-1).astype(v.dtype)
    parts = [jnp.einsum('bnhqk,bnkhd->bnqhd', p[..., i * CHUNK:(i + 1) * CHUNK], v_p[:, i:i + n_chunks])
             for i in range(LEFT_CHUNKS + 1)]
    o = functools.reduce(jnp.add, parts)
    return o.reshape(bsz, seq, MIX_ODD) @ w_out


def setup_inputs(seed: int = 0) -> dict:
    key = jax.random.key(seed)
    ks = jax.random.split(key, 17)
    f32 = jnp.float32

    def nrm(k, shape, fan_in):
        return jax.random.normal(k, shape, f32) * (fan_in ** -0.5)

    def gain(k, shape):
        return 1.0 + 0.05 * jax.random.normal(k, shape, f32)

    return {
        "x": jax.random.normal(ks[0], (BATCH, SEQ, D_MODEL), f32),
        "ev_w_in": nrm(ks[1], (N_EVEN, D_MODEL, EVEN_IN), D_MODEL),
        "ev_g_cq": gain(ks[2], (N_EVEN, Q_LORA)),
        "ev_w_uq": nrm(ks[3], (N_EVEN, Q_LORA, MLA_HEADS * (MLA_NOPE + MLA_ROPE)), Q_LORA),
        "ev_g_ckv": gain(ks[4], (N_EVEN, KV_LORA)),
        "ev_w_ukv": nrm(ks[5], (N_EVEN, KV_LORA, MLA_HEADS * (MLA_NOPE + MLA_V)), KV_LORA),
        "ev_w_out": nrm(ks[6], (N_EVEN, MIX_EVEN, D_MODEL), MIX_EVEN),
        "od_w_qkv": nrm(ks[7], (N_ODD, D_MODEL, 3 * MIX_ODD), D_MODEL),
        "od_rel_bias": 0.1 * jax.random.normal(ks[8], (N_ODD, C_HEADS, 2 * REL_CLIP + 1), f32),
        "od_w_out": nrm(ks[9], (N_ODD, MIX_ODD, D_MODEL), MIX_ODD),
        "g_mix": gain(ks[10], (DEPTH, D_MODEL)),
        "g_ffn": gain(ks[11], (DEPTH, D_MODEL)),
        "w_gate": nrm(ks[12], (DEPTH, D_MODEL, D_FF), D_MODEL),
        "w_up": nrm(ks[13], (DEPTH, D_MODEL, D_FF), D_MODEL),
        "w_down": nrm(ks[14], (DEPTH, D_FF, D_MODEL), D_FF),
        "g_final": gain(ks[15], (D_MODEL,)),
    }


def reference(x, ev_w_in, ev_g_cq, ev_w_uq, ev_g_ckv, ev_w_ukv, ev_w_out,
              od_w_qkv, od_rel_bias, od_w_out, g_mix, g_ffn, w_gate, w_up, w_down, g_final):
    h = x
    for layer in range(DEPTH):
        u = rms_norm(h, g_mix[layer])
        if layer % 2 == 0:
            i = layer // 2
            h = h + mla_stick_breaking_mixer(u, ev_w_in[i], ev_g_cq[i], ev_w_uq[i],
                                             ev_g_ckv[i], ev_w_ukv[i], ev_w_out[i])
        else:
            i = layer // 2
            h = h + chunk_band_mixer(u, od_w_qkv[i], od_rel_bias[i], od_w_out[i])
        u = rms_norm(h, g_ffn[layer])
        h = h + swiglu(u, w_gate[layer], w_up[layer], w_down[layer])
    return rms_norm(h, g_final)
```

```python
import numpy as np
import concourse.bass as bass
import concourse.mybir as mybir
from concourse.bass_utils import run_bass_kernel_spmd

F32 = mybir.dt.float32
BF16 = mybir.dt.bfloat16
U8 = mybir.dt.uint8
ALU = mybir.AluOpType
AF = mybir.ActivationFunctionType

import os as _os
_SKIP_B = bool(_os.environ.get("K_SKIP_B"))
N_DUMMY_SB = int(_os.environ.get("K_DUMMY_SB", "3"))
N_DUMMY_M1 = int(_os.environ.get("K_DUMMY_M1", "0"))
_SKIP_C = bool(_os.environ.get("K_SKIP_C"))
S_LEN = 2048
D = 1024
DFF = 2816
EPS = 1e-6
NCORES = 8
ENGS = ("pe", "act", "dve", "pool", "sp")
NDMA = 24
NDMA_POOL = 16
SEM_LIMIT = 3000
NSEM_PER_ENG = 10


DEBUG_NAMES = {}


class Op:
    __slots__ = ("eng", "fn", "deps", "dmadeps", "marked", "sem", "cnt", "idx", "dma", "tag")

    def __init__(self, eng, fn):
        self.eng = eng
        self.fn = fn
        self.deps = {}
        self.dmadeps = {}
        self.marked = False
        self.sem = None
        self.cnt = 0
        self.idx = -1
        self.dma = None


class Sched:
    def __init__(self):
        self.ops = {e: [] for e in ENGS}
        self.last_w = {}
        self.readers = {}
        self.dma_use = [0] * NDMA
        self.dma_rr = 0
        self.dma_rr_pool = 0
        self.pending = {e: [] for e in ENGS}
        self.final_dma = []

    def _add_dep(self, o, tok):
        if tok is None:
            return
        if not isinstance(tok, tuple) and tok.dma is not None:
            tok = tok.dma
        if isinstance(tok, tuple):
            i, v = tok
            if o.dmadeps.get(i, 0) < v:
                o.dmadeps[i] = v
            return
        if tok.eng == o.eng and o.eng in ("pe", "sp"):
            return
        cur = o.deps.get(tok.eng)
        if cur is None or cur.idx < tok.idx:
            o.deps[tok.eng] = tok

    def op(self, eng, fn, reads=(), writes=(), dma=False, raw_same_only=True):
        o = Op(eng, fn)
        o.tag = (eng, tuple(reads), tuple(writes))
        o.idx = len(self.ops[eng])
        for tok in self.pending[eng]:
            self._add_dep(o, tok)
        self.pending[eng] = []
        for k in reads:
            self._add_dep(o, self.last_w.get(k))
        for k in writes:
            self._add_dep(o, self.last_w.get(k))
            for r in self.readers.get(k, ()):
                self._add_dep(o, r)
        tok = o
        if dma:
            if eng == "pool":
                i = self.dma_rr_pool
                self.dma_rr_pool = (self.dma_rr_pool + 1) % NDMA_POOL
            else:
                i = NDMA_POOL + self.dma_rr
                self.dma_rr = (self.dma_rr + 1) % (NDMA - NDMA_POOL)
            if self.dma_use[i] > 0:
                self._add_dep(o, (i, 16 * self.dma_use[i]))
            self.dma_use[i] += 1
            o.dma = (i, 16 * self.dma_use[i])
            tok = o.dma
        for k in reads:
            self.readers.setdefault(k, []).append(tok)
        for k in writes:
            self.last_w[k] = tok
            self.readers[k] = []
        self.ops[eng].append(o)
        return tok

    def barrier(self):
        toks = []
        for e in ENGS:
            if self.ops[e]:
                toks.append(self.ops[e][-1])
        for i in range(NDMA):
            if self.dma_use[i] > 0:
                toks.append((i, 16 * self.dma_use[i]))
        for e in ENGS:
            self.pending[e] = list(toks)
        self.last_w = {}
        self.readers = {}

    def finalize(self, nc):
        for e in ENGS:
            for o in self.ops[e]:
                for d in o.deps.values():
                    d.marked = True
        esems = {}
        self._ctx = []
        for e in ENGS:
            n = sum(1 for o in self.ops[e] if o.marked)
            need = max(1, (n + SEM_LIMIT - 1) // SEM_LIMIT)
            assert need <= NSEM_PER_ENG, (e, n)
            lst = []
            for k in range(need):
                cm = nc.semaphore(f"s_{e}_{k}")
                lst.append(cm.__enter__())
                self._ctx.append(cm)
            esems[e] = lst
            c = 0
            for o in self.ops[e]:
                if o.marked:
                    o.sem = lst[c // SEM_LIMIT]
                    o.cnt = c % SEM_LIMIT + 1
                    c += 1
        dsems = []
        for i in range(NDMA):
            cm = nc.semaphore(f"s_dma_{i}")
            dsems.append(cm.__enter__())
            self._ctx.append(cm)
        self.dsems = dsems

    def emit(self, eng_name, engobj):
        waited = {}
        dwaited = {}
        for o in self.ops[eng_name]:
            for src, d in o.deps.items():
                if waited.get(src, -1) >= d.idx:
                    continue
                engobj.wait_ge(d.sem, d.cnt)
                waited[src] = d.idx
            for i, v in o.dmadeps.items():
                if dwaited.get(i, 0) >= v:
                    continue
                engobj.wait_ge(self.dsems[i], v)
                dwaited[i] = v
            ins = o.fn(engobj)
            try:
                DEBUG_NAMES[ins.ins.name] = o.tag
            except Exception:
                pass
            if o.dma is not None:
                ins.then_inc(self.dsems[o.dma[0]], 16)
            elif o.marked:
                ins.then_inc(o.sem, 1)
        if eng_name == "sp":
            for (i, v) in self.final_dma:
                if dwaited.get(i, 0) < v:
                    engobj.wait_ge(self.dsems[i], v)
                    dwaited[i] = v

    def close(self):
        for cm in reversed(self._ctx):
            cm.__exit__(None, None, None)


class Arena:
    def __init__(self, nc, nbytes):
        self.t = nc.alloc_sbuf_tensor("arena", [128, nbytes], U8)
        self.n = nbytes
        self.off = 0
        self.peak = 0

    def alloc(self, shape, dtype):
        esz = 4 if dtype == F32 else 2
        n = 1
        for s in shape:
            n *= s
        nb = n * esz
        off = (self.off + 63) // 64 * 64
        assert off + nb <= self.n, ("arena overflow", off, nb, self.n)
        self.off = off + nb
        self.peak = max(self.peak, self.off)
        v = self.t[:, off:off + nb].bitcast(dtype)
        if len(shape) == 2:
            v = v.rearrange("p (a b) -> p a b", a=shape[0])
        elif len(shape) == 3:
            v = v.rearrange("p (a b c) -> p a b c", a=shape[0], b=shape[1])
        return v

    def mark(self):
        return self.off

    def release(self, m):
        self.off = m


class Rot:
    def __init__(self, items):
        self.items = list(items)
        self.i = 0

    def next(self):
        v = self.items[self.i % len(self.items)]
        self.i += 1
        return v


G_MIX0, G_FFN0, G_MIX1, G_FFN1, G_FINAL, G_CQ, G_CKV, NG = 0, 8, 16, 24, 32, 40, 43, 45
C_ONES, C_NEG, C_TRI, C_MSB, C_MMLA, C_ID, NCM = 0, 128, 256, 384, 512, 640, 768


def build(stages=("mix0", "ffn0", "mix1", "ffn1"), final_norm=True):
    nc = bass.Bass("TRN2", target_bir_lowering=False)

    def din(name, shape):
        return nc.dram_tensor(name, list(shape), F32, kind="ExternalInput").ap()

    xT_d = din("xT", [D, S_LEN])
    w_in_d = din("ev_w_in", [D, 2208]).rearrange("(c p) n -> p c n", p=128)
    w_uq_d = din("ev_w_uq", [384, 768]).rearrange("(c p) n -> p c n", p=128)
    w_ukv_d = din("ev_w_ukv", [256, 1024]).rearrange("(c p) n -> p c n", p=128)
    w_eo_d = din("ev_w_out", [D, D]).rearrange("(c p) n -> p c n", p=128)
    w_qkv_d = din("od_w_qkv", [D, 3072]).rearrange("(c p) n -> p c n", p=128)
    w_oo_d = din("od_w_out", [D, D]).rearrange("(c p) n -> p c n", p=128)
    w_gate_d = din("w_gate", [2, D, DFF])
    w_up_d = din("w_up", [2, D, DFF])
    w_down_d = din("w_down", [2, DFF, D])
    gvec_d = din("gvec", [128, NG])
    cmat_d = din("cmat", [128, NCM])
    cc_d = din("rope_cc", [128, S_LEN])
    ss_d = din("rope_ss", [128, S_LEN])
    btab_d = din("bias_tab", [16, 128, 640])
    mask1_d = din("mask1", [128, 640])
    nmask1_d = din("nmask1", [128, 640])
    outT_d = nc.dram_tensor("outT", [D, S_LEN], F32, kind="ExternalOutput").ap()
    outT_v = outT_d.rearrange("(c p) t -> p c t", p=128)
    xT_v = xT_d.rearrange("(c p) t -> p c t", p=128)

    S = Sched()
    A = Arena(nc, 211968)
    psall = nc.alloc_psum_tensor("psall", [128, 4096], F32)
    ps = [psall[:, b * 512:(b + 1) * 512] for b in range(8)]

    hT = A.alloc((8, S_LEN), F32)
    cb = A.alloc((NCM,), BF16)
    gvec = A.alloc((NG,), F32)
    sq = A.alloc((8, 512), BF16)
    rstd = A.alloc((512,), F32)
    ones_bf = cb[:, C_ONES:C_ONES + 128]
    neg_bf = cb[:, C_NEG:C_NEG + 128]
    tri_bf = cb[:, C_TRI:C_TRI + 128]
    msb_bf = cb[:, C_MSB:C_MSB + 128]
    mmla_bf = cb[:, C_MMLA:C_MMLA + 128]
    ident_bf = cb[:, C_ID:C_ID + 128]

    PS = Rot(range(8))
    state = {"PS": PS}

    def mm(out, lhsT, rhs, start, stop, reads, writes):
        S.op("pe", lambda e: e.matmul(out, lhsT, rhs, start=start, stop=stop, skip_group_check=True),
             reads=reads, writes=writes)

    def dve_tt(out, in0, in1, op, reads, writes):
        S.op("dve", lambda e: e.tensor_tensor(out, in0, in1, op), reads=reads, writes=writes)

    def dve_tss(out, in_, scalar, op, reads, writes):
        S.op("dve", lambda e: e.tensor_single_scalar(out, in_, scalar, op), reads=reads, writes=writes)

    def dve_copy(out, in_, reads, writes):
        S.op("dve", lambda e: e.tensor_copy(out, in_), reads=reads, writes=writes)

    def act(out, in_, func, reads, writes, bias=0.0, scale=1.0):
        S.op("act", lambda e: e.activation(out, in_, func, bias=bias, scale=scale), reads=reads, writes=writes)

    def act_copy(out, in_, reads, writes):
        S.op("act", lambda e: e.copy(out, in_), reads=reads, writes=writes)

    def act_mul(out, in_, m, reads, writes):
        S.op("act", lambda e: e.mul(out, in_, m), reads=reads, writes=writes)

    def pool_copy(out, in_, reads, writes):
        S.op("pool", lambda e: e.tensor_copy(out, in_), reads=reads, writes=writes)

    def dma(eng, out, in_, reads, writes):
        return S.op(eng, lambda e: e.dma_start(out=out, in_=in_), reads=reads, writes=writes, dma=True)

    dma("pool", cb[:, :], cmat_d[:, :], [], [("cb",)])
    dma("sp", gvec[:, :], gvec_d[:, :], [], [("gvec",)])
    for c in range(8):
        dma("sp", hT[:, c, :], xT_v[:, c, :], [], [("h", c, tb) for tb in range(4)])

    def rstd_from(bank, inv_n):
        S.op("act", lambda e: e.activation(rstd[:, :], ps[bank][:, :], AF.Sqrt, bias=EPS, scale=inv_n),
             reads=[("ps", bank)], writes=[("rstd",)])
        S.op("dve", lambda e: e.reciprocal(rstd[:, :], rstd[:, :]),
             reads=[("rstd",)], writes=[("rstd",)])

    def norm_main(tb, gcol, dst, dkeys):
        t0 = tb * 512
        act(sq[:, :, :], hT[:, :, t0:t0 + 512], AF.Square, [("h", c, tb) for c in range(8)], [("sq",)])
        bank = state["PS"].next()
        for c in range(8):
            mm(ps[bank][:, :], ones_bf, sq[:, c, :], c == 0, c == 7, [("sq",), ("cb",)], [("ps", bank)])
        rstd_from(bank, 1.0 / D)
        for c in range(8):
            S.op("dve", (lambda c: lambda e: e.scalar_tensor_tensor(
                dst[:, c, :], hT[:, c, t0:t0 + 512], gvec[:, gcol + c:gcol + c + 1], rstd[:, :],
                ALU.mult, ALU.mult))(c),
                reads=[("h", c, tb), ("rstd",), ("gvec",)], writes=[dkeys(c)])

    def ffn(l, gcol):
        S.barrier()
        state["PS"] = Rot(range(8))
        m0 = A.mark()
        uThs = [A.alloc((8, 1024), BF16) for _ in range(2)]
        actT = A.alloc((22, 1024), BF16)
        wgu = [A.alloc((2, 8, 256), BF16) for _ in range(3)]
        wdc = [A.alloc((22, 128), BF16) for _ in range(3)]
        sg = [A.alloc((512,), F32) for _ in range(2)]
        wg_v = w_gate_d[l].rearrange("(c p) f -> p c f", p=128)
        wu_v = w_up_d[l].rearrange("(c p) f -> p c f", p=128)
        wd_v = w_down_d[l].rearrange("(c p) d -> p c d", p=128)
        n_w = 0
        n_d = 0
        n_s = 0
        def ffn_norm(th):
            for tb2 in range(2):
                tb = th * 2 + tb2
                norm_main(tb, gcol, uThs[th][:, :, tb2 * 512:(tb2 + 1) * 512], lambda c, tb2=tb2, th=th: ("uTh", th, c, tb2))

        ffn_norm(0)
        for th in range(2):
            uTh = uThs[th]
            for fg in range(11):
                if th == 0 and fg == 3:
                    ffn_norm(1)
                slot = n_w % 3
                n_w += 1
                buf = wgu[slot]
                dma("pool", buf[:, 0, :, :], wg_v[:, :, fg * 256:(fg + 1) * 256], [], [("wgu", slot, 0)])
                dma("pool", buf[:, 1, :, :], wu_v[:, :, fg * 256:(fg + 1) * 256], [], [("wgu", slot, 1)])
                for fc in range(2):
                    f = fg * 2 + fc
                    for tb2 in range(2):
                        ts = slice(tb2 * 512, (tb2 + 1) * 512)
                        bg = state["PS"].next()
                        bu = state["PS"].next()
                        for c in range(8):
                            mm(ps[bg][:, :], buf[:, 0, c, fc * 128:(fc + 1) * 128], uTh[:, c, ts], c == 0, c == 7,
                               [("wgu", slot, 0), ("uTh", th, c, tb2)], [("ps", bg)])
                        for c in range(8):
                            mm(ps[bu][:, :], buf[:, 1, c, fc * 128:(fc + 1) * 128], uTh[:, c, ts], c == 0, c == 7,
                               [("wgu", slot, 1), ("uTh", th, c, tb2)], [("ps", bu)])
                        sgb = sg[n_s % 2]
                        sk = ("sg", n_s % 2)
                        n_s += 1
                        act(sgb[:, :], ps[bg][:, :], AF.Silu, [("ps", bg)], [sk])
                        dve_tt(actT[:, f, ts], ps[bu][:, :], sgb[:, :], ALU.mult, [("ps", bu), sk], [("actT", f, tb2)])
            for dc in range(8):
                slot = n_d % 3
                n_d += 1
                wb = wdc[slot]
                dma("pool", wb[:, 0:11, :], wd_v[:, 0:11, dc * 128:(dc + 1) * 128], [], [("wdc", slot)])
                dma("pool", wb[:, 11:22, :], wd_v[:, 11:22, dc * 128:(dc + 1) * 128], [], [("wdc", slot)])
                for tb2 in range(2):
                    tb = th * 2 + tb2
                    ts = slice(tb2 * 512, (tb2 + 1) * 512)
                    b = state["PS"].next()
                    for f in range(22):
                        mm(ps[b][:, :], wb[:, f, :], actT[:, f, ts], f == 0, f == 21,
                           [("wdc", slot), ("actT", f, tb2)], [("ps", b)])
                    hs = hT[:, dc, tb * 512:(tb + 1) * 512]
                    dve_tt(hs, ps[b][:, :], hs, ALU.add, [("ps", b), ("h", dc, tb)], [("h", dc, tb)])
        S.barrier()
        A.release(m0)

    def outproj(w_d, oT):
        wo = A.alloc((8, 1024), BF16)
        for half in range(2):
            dma("pool", wo[:, :, half * 512:(half + 1) * 512], w_d[:, :, half * 512:(half + 1) * 512], [], [("wo", half)])
        for dc in range(8):
            for tb in range(4):
                b = state["PS"].next()
                for kc in range(8):
                    mm(ps[b][:, :], wo[:, kc, dc * 128:(dc + 1) * 128], oT[:, kc, tb * 512:(tb + 1) * 512],
                       kc == 0, kc == 7, [("wo", dc // 4), ("oT", kc, tb)], [("ps", b)])
                hs = hT[:, dc, tb * 512:(tb + 1) * 512]
                dve_tt(hs, ps[b][:, :], hs, ALU.add, [("ps", b), ("h", dc, tb)], [("h", dc, tb)])

    def mix0():
        S.barrier()
        state["PS"] = Rot(range(8))
        m0 = A.mark()
        cqn = A.alloc((3, S_LEN), BF16)
        ckvn = A.alloc((2, S_LEN), BF16)
        krope = A.alloc((S_LEN,), BF16)
        oT = A.alloc((8, S_LEN), BF16)
        mU = A.mark()
        uT = A.alloc((8, S_LEN), BF16)
        mA = A.mark()
        wA = A.alloc((8, 704), BF16)
        CCb = [A.alloc((512,), F32) for _ in range(2)]
        SSb = [A.alloc((512,), F32) for _ in range(2)]
        t1 = A.alloc((512,), F32)
        t2 = A.alloc((512,), F32)
        dma("pool", wA[:, :, 0:672], w_in_d[:, :, 0:672], [], [("wA", 0)])
        dma("pool", wA[:, :, 672:688], w_in_d[:, :, 656:672], [], [("wA", 1)])
        dma("pool", wA[:, :, 688:704], w_in_d[:, :, 640:656], [], [("wA", 2)])
        norm_main(0, G_MIX0, uT[:, :, 0:512], lambda c: ("uT", c, 0))
        for tb in range(4):
            ts = slice(tb * 512, (tb + 1) * 512)
            if tb + 1 < 4:
                norm_main(tb + 1, G_MIX0, uT[:, :, (tb + 1) * 512:(tb + 2) * 512], lambda c, tb=tb: ("uT", c, tb + 1))
            for (nch, col0, gc, dst, dname, invn) in ((3, 0, G_CQ, cqn, "cqn", 1.0 / 384),
                                                       (2, 384, G_CKV, ckvn, "ckvn", 1.0 / 256)):
                banks = []
                for j in range(nch):
                    b = state["PS"].next()
                    banks.append(b)
                    for c in range(8):
                        mm(ps[b][:, :], wA[:, c, col0 + j * 128:col0 + (j + 1) * 128], uT[:, c, ts], c == 0, c == 7,
                           [("wA", 0), ("uT", c, tb)], [("ps", b)])
                    act(sq[:, j, :], ps[b][:, :], AF.Square, [("ps", b)], [("sq",)])
                bs = state["PS"].next()
                for j in range(nch):
                    mm(ps[bs][:, :], ones_bf, sq[:, j, :], j == 0, j == nch - 1, [("sq",), ("cb",)], [("ps", bs)])
                rstd_from(bs, invn)
                for j in range(nch):
                    b = banks[j]
                    S.op("dve", (lambda j, b, dst, gc, ts: lambda e: e.scalar_tensor_tensor(
                        dst[:, j, ts], ps[b][:, :], gvec[:, gc + j:gc + j + 1], rstd[:, :], ALU.mult, ALU.mult))(j, b, dst, gc, ts),
                        reads=[("ps", b), ("rstd",), ("gvec",)], writes=[(dname, j, tb)])
            b = state["PS"].next()
            for c in range(8):
                mm(ps[b][0:64, :], wA[:, c, 640:704], uT[:, c, ts], c == 0, c == 7,
                   [("wA", 0), ("wA", 1), ("wA", 2), ("uT", c, tb)], [("ps", b)])
            cc = CCb[tb % 2]
            ss = SSb[tb % 2]
            dma("sp", cc[:, :], cc_d[:, ts], [], [("CCb", tb % 2)])
            dma("sp", ss[:, :], ss_d[:, ts], [], [("SSb", tb % 2)])
            dve_tt(t1[0:32, :], ps[b][0:32, :], cc[0:32, :], ALU.mult, [("ps", b), ("CCb", tb % 2)], [("t1",)])
            dve_tt(t2[0:32, :], ps[b][32:64, :], ss[32:64, :], ALU.mult, [("ps", b), ("SSb", tb % 2)], [("t2",)])
            dve_tt(krope[0:32, ts], t1[0:32, :], t2[0:32, :], ALU.add, [("t1",), ("t2",)], [("krope", tb)])
        S.barrier()
        A.release(mA)

        if not _SKIP_B:
            mB = A.mark()
            state["PS"] = Rot([0, 1, 2, 3])
            wB = A.alloc((8, 384), BF16)
            qb = A.alloc((S_LEN,), BF16)
            kb = A.alloc((S_LEN,), BF16)
            vb = A.alloc((16, 128), BF16)
            ebuf = [A.alloc((2, 512), F32) for _ in range(1)]
            spb = [A.alloc((2, 512), BF16) for _ in range(3)]
            wbuf = [A.alloc((2, 512), BF16) for _ in range(3)]
            sacc = A.alloc((2, 512), F32)
            saccb = [A.alloc((2, 512), BF16) for _ in range(2)]
            ZZ = [psall[:, 0:1024].rearrange("p (a c) -> p a c", a=2), psall[:, 1024:2048].rearrange("p (a c) -> p a c", a=2)]
            RR = psall[:, 2048:3072].rearrange("p (a c) -> p a c", a=2)
            pcount = 0
            for j in range(4):
                dma("pool", wB[:, :, 0:128], w_in_d[:, :, 672 + j * 128:672 + (j + 1) * 128], [], [("wB", 0)])
                dma("pool", wB[:, :, 128:256], w_in_d[:, :, 1184 + j * 128:1184 + (j + 1) * 128], [], [("wB", 1)])
                dma("pool", wB[:, :, 256:384], w_in_d[:, :, 1696 + j * 128:1696 + (j + 1) * 128], [], [("wB", 2)])
                for tb in range(4):
                    ts = slice(tb * 512, (tb + 1) * 512)
                    b = 4
                    for c in range(8):
                        mm(ps[b][:, :], wB[:, c, 0:128], uT[:, c, ts], c == 0, c == 7,
                           [("wB", 0), ("uT", c, tb)], [("ps", b)])
                    dve_tss(qb[:, ts], ps[b][:, :], 0.125, ALU.mult, [("ps", b)], [("qb", tb)])
                    b = 4
                    for c in range(8):
                        mm(ps[b][:, :], wB[:, c, 128:256], uT[:, c, ts], c == 0, c == 7,
                           [("wB", 1), ("uT", c, tb)], [("ps", b)])
                    dve_copy(kb[:, ts], ps[b][:, :], [("ps", b)], [("kb", tb)])
                for tq in range(4):
                    b = 4
                    for q in range(4):
                        tile = tq * 4 + q
                        for c in range(8):
                            mm(ps[b][:, q * 128:(q + 1) * 128], uT[:, c, tile * 128:(tile + 1) * 128], wB[:, c, 256:384],
                               c == 0, c == 7, [("wB", 2), ("uT", c, tile // 4)], [("ps", b)])
                    dve_copy(vb[:, tq * 4:(tq + 1) * 4, :], ps[b][:, :].rearrange("p (a b) -> p a b", a=4),
                             [("ps", b)], [("vb", tq)])
                items = []
                for i in range(4):
                    for kt in range(4 * i + 3, -1, -1):
                        c0 = max(0, 128 * kt - 512 * i)
                        c0p = max(0, 128 * (kt + 1) - 512 * i)
                        items.append(dict(i=i, kt=kt, c0=c0, c0p=c0p, n=512 - c0, first=(kt == 4 * i + 3),
                                          last=(kt == 0), diag=(kt >= 4 * i), p=pcount))
                        pcount += 1

                def s1(it):
                    i, kt, c0, n = it["i"], it["kt"], it["c0"], it["n"]
                    zq = it["p"] % 2
                    zk = [("ps", 2 * zq), ("ps", 2 * zq + 1)]
                    for hh in range(2):
                        pb = hh * 64
                        mm(ps[2 * zq + hh][:, c0:512], kb[pb:pb + 64, kt * 128:(kt + 1) * 128],
                           qb[pb:pb + 64, 512 * i + c0:512 * i + 512], True, True,
                           [("kb", kt // 4), ("qb", i)], [("ps", 2 * zq + hh)])
                    ek = 0
                    sk = it["p"] % 3
                    act(ebuf[ek][:, :, 0:n], ZZ[zq][:, :, c0:512], AF.Exp, zk, [("ebuf", ek)])
                    act(spb[sk][:, :, 0:n], ebuf[ek][:, :, 0:n], AF.Ln, [("ebuf", ek)], [("spb", sk)], bias=1.0)
                    if it["diag"]:
                        for hh in range(2):
                            dve_tt(spb[sk][:, hh, 0:128], spb[sk][:, hh, 0:128], msb_bf, ALU.mult,
                                   [("spb", sk), ("cb",)], [("spb", sk)])

                def s2(it):
                    i, kt, c0, n = it["i"], it["kt"], it["c0"], it["n"]
                    zq = it["p"] % 2
                    zk = [("ps", 2 * zq), ("ps", 2 * zq + 1)]
                    sk3 = it["p"] % 3
                    sk = it["p"] % 3
                    pp = it["p"] % 2
                    for hh in range(2):
                        mm(ps[2 * zq + hh][:, c0:512], tri_bf, spb[sk3][:, hh, 0:n], False, it["first"],
                           [("spb", sk3), ("cb",)], [("ps", 2 * zq + hh)])
                    if not it["first"]:
                        for hh in range(2):
                            mm(ps[2 * zq + hh][:, c0:512], neg_bf, saccb[pp][:, hh, c0:512], False, True,
                               [("saccb", pp), ("cb",)], [("ps", 2 * zq + hh)])
                    act(wbuf[sk][:, :, 0:n], ZZ[zq][:, :, c0:512], AF.Exp, zk, [("wbuf", sk)])
                    for _d in range(N_DUMMY_SB):
                        mm(ps[5][:, :], ones_bf, uT[:, _d, 0:512], True, True, [("cb",)], [("psdummy",)])
                    if it["diag"]:
                        for hh in range(2):
                            dve_tt(wbuf[sk][:, hh, 0:128], wbuf[sk][:, hh, 0:128], msb_bf, ALU.mult,
                                   [("wbuf", sk), ("cb",)], [("wbuf", sk)])
                    if not it["last"]:
                        if it["first"]:
                            S.op("dve", lambda e: e.memset(sacc[:, :, :], 0.0), reads=[], writes=[("sacc",)])
                        dve_tt(sacc[:, :, c0:512], sacc[:, :, c0:512], spb[sk3][:, :, 0:n], ALU.add,
                               [("sacc",), ("spb", sk3)], [("sacc",)])
                        dve_copy(saccb[1 - pp][:, :, :], sacc[:, :, :], [("sacc",)], [("saccb", 1 - pp)])

                def s3(it):
                    i, kt, c0, n = it["i"], it["kt"], it["c0"], it["n"]
                    sk = it["p"] % 3
                    Ob = 6 + i % 2
                    for hh in range(2):
                        pb = hh * 64
                        mm(ps[Ob][pb:pb + 64, c0:512], vb[:, kt, pb:pb + 64], wbuf[sk][:, hh, 0:n], it["first"], it["last"],
                           [("vb", kt // 4), ("wbuf", sk)], [("ps", Ob)])
                    if it["last"]:
                        dve_copy(oT[:, 4 + j, 512 * i:512 * i + 512], ps[Ob][:, :], [("ps", Ob)], [("oT", 4 + j, i)])

                n_it = len(items)
                L1, L2 = 1, 3
                for step in range(n_it + L2):
                    if step < n_it:
                        s1(items[step])
                    if 0 <= step - L1 < n_it:
                        s2(items[step - L1])
                    if 0 <= step - L2 < n_it:
                        s3(items[step - L2])
            S.barrier()
            A.release(mU)

        if not _SKIP_C:
            mC = A.mark()
            state["PS"] = Rot([0, 1, 2, 3, 6, 7])
            wuq = A.alloc((3, 768), BF16)
            wsw = A.alloc((3, 8, 32), BF16)
            wukv = A.alloc((2, 1024), BF16)
            CC = A.alloc((S_LEN,), F32)
            SS = A.alloc((S_LEN,), F32)
            Qh = [A.alloc((S_LEN,), BF16) for _ in range(2)]
            Kh = [A.alloc((S_LEN,), BF16) for _ in range(2)]
            Vh = [A.alloc((16, 128), BF16) for _ in range(2)]
            pbuf = [A.alloc((512,), BF16) for _ in range(6)]
            rec = A.alloc((512,), F32)
            t1 = A.alloc((512,), F32)
            t2 = A.alloc((512,), F32)
            dma("pool", wuq[:, :, :], w_uq_d[:, :, :], [], [("wuq",)])
            wuq4 = w_uq_d.rearrange("p c (h f) -> p c h f", h=8)
            for kc in range(3):
                dma("pool", wsw[:, kc, :, 0:16], wuq4[:, kc, :, 80:96], [], [("wsw", 0)])
                dma("pool", wsw[:, kc, :, 16:32], wuq4[:, kc, :, 64:80], [], [("wsw", 1)])
            dma("pool", wukv[:, :, :], w_ukv_d[:, :, :], [], [("wukv",)])
            dma("sp", CC[:, :], cc_d[:, :], [], [("CC",)])
            dma("sp", SS[:, :], ss_d[:, :], [], [("SS",)])
            for sl in range(2):
                S.op("dve", (lambda sl: lambda e: e.memset(Vh[sl][:, :, 64:128], 1.0))(sl), reads=[], writes=[("Vh1", sl)])
            gcount = 0
            pcount = 0
            scale_a = 96.0 ** -0.5
            for h in range(8):
                sl = h % 2
                j = h // 2
                pb = (h % 2) * 64
                for tb in range(4):
                    ts = slice(tb * 512, (tb + 1) * 512)
                    bA = state["PS"].next()
                    for kc in range(3):
                        mm(ps[bA][0:96, :], wuq[:, kc, h * 96:(h + 1) * 96], cqn[:, kc, ts], kc == 0, kc == 2,
                           [("wuq",), ("cqn", kc, tb)], [("ps", bA)])
                    bB = state["PS"].next()
                    for kc in range(3):
                        mm(ps[bB][64:96, :], wsw[:, kc, h, :], cqn[:, kc, ts], kc == 0, kc == 2,
                           [("wsw", 0), ("wsw", 1), ("cqn", kc, tb)], [("ps", bB)])
                    act_copy(Qh[sl][0:64, ts], ps[bA][0:64, :], [("ps", bA)], [("Qh", sl, tb)])
                    dve_tt(t1[64:96, :], ps[bA][64:96, :], CC[64:96, ts], ALU.mult, [("ps", bA), ("CC",)], [("t1",)])
                    dve_tt(t2[64:96, :], ps[bB][64:96, :], SS[64:96, ts], ALU.mult, [("ps", bB), ("SS",)], [("t2",)])
                    dve_tt(Qh[sl][64:96, ts], t1[64:96, :], t2[64:96, :], ALU.add, [("t1",), ("t2",)], [("Qh", sl, tb)])
                    bK = state["PS"].next()
                    for kc in range(2):
                        mm(ps[bK][0:64, :], wukv[:, kc, h * 128:h * 128 + 64], ckvn[:, kc, ts], kc == 0, kc == 1,
                           [("wukv",), ("ckvn", kc, tb)], [("ps", bK)])
                    act_copy(Kh[sl][0:64, ts], ps[bK][0:64, :], [("ps", bK)], [("Kh", sl, tb)])
                    pool_copy(Kh[sl][64:96, ts], krope[0:32, ts], [("krope", tb)], [("Kh", sl, tb)])
                for half in range(2):
                    b = state["PS"].next()
                    for q in range(8):
                        tile = half * 8 + q
                        for kc in range(2):
                            mm(ps[b][:, q * 64:(q + 1) * 64], ckvn[:, kc, tile * 128:(tile + 1) * 128],
                               wukv[:, kc, h * 128 + 64:h * 128 + 128], kc == 0, kc == 1,
                               [("wukv",), ("ckvn", kc, tile // 4)], [("ps", b)])
                    dve_copy(Vh[sl][:, half * 8:(half + 1) * 8, 0:64], ps[b][:, :].rearrange("p (a b) -> p a b", a=8),
                             [("ps", b)], [("Vh", sl, half)])
                items = []
                for i in range(4):
                    g = gcount
                    gcount += 1
                    for kt in range(0, 4 * i + 4):
                        c0 = max(0, 128 * kt - 512 * i)
                        items.append(dict(i=i, kt=kt, c0=c0, n=512 - c0, first=(kt == 0), last=(kt == 4 * i + 3),
                                          diag=(kt >= 4 * i), g=g, p=pcount))
                        pcount += 1

                def c1(it):
                    i, kt, c0, n = it["i"], it["kt"], it["c0"], it["n"]
                    zb = state["PS"].next()
                    it["zb"] = zb
                    mm(ps[zb][:, c0:512], Kh[sl][0:96, kt * 128:(kt + 1) * 128], Qh[sl][0:96, 512 * i + c0:512 * i + 512],
                       True, True, [("Kh", sl, kt // 4), ("Qh", sl, i)], [("ps", zb)])
                    pk = it["p"] % 6
                    act(pbuf[pk][:, 0:n], ps[zb][:, c0:512], AF.Exp, [("ps", zb)], [("pbuf", pk)], scale=scale_a)
                    if it["diag"]:
                        dve_tt(pbuf[pk][:, 0:128], pbuf[pk][:, 0:128], mmla_bf, ALU.mult, [("pbuf", pk), ("cb",)], [("pbuf", pk)])

                def c2(it):
                    i, kt, c0, n, g = it["i"], it["kt"], it["c0"], it["n"], it["g"]
                    pk = it["p"] % 6
                    Ob = 4 + g % 2
                    mm(ps[Ob][:, c0:512], Vh[sl][:, kt, :], pbuf[pk][:, 0:n], it["first"], it["last"],
                       [("Vh", sl, kt // 8), ("Vh1", sl), ("pbuf", pk)], [("ps", Ob)])
                    if it["last"]:
                        S.op("dve", (lambda Ob: lambda e: e.reciprocal(rec[64:128, :], ps[Ob][64:128, :]))(Ob),
                             reads=[("ps", Ob)], writes=[("rec",)])
                        dve_tt(oT[pb:pb + 64, j, 512 * i:512 * i + 512], ps[Ob][0:64, :], rec[64:128, :], ALU.mult,
                               [("ps", Ob), ("rec",)], [("oT", j, i)])

                n_it = len(items)
                LC = 4
                for step in range(n_it + LC):
                    if step < n_it:
                        c1(items[step])
                    if 0 <= step - LC < n_it:
                        c2(items[step - LC])
            S.barrier()
            A.release(mC)
        state["PS"] = Rot(range(8))
        outproj(w_eo_d, oT)
        S.barrier()
        A.release(m0)

    def mix1():
        S.barrier()
        state["PS"] = Rot([0, 1, 2, 3])
        m0 = A.mark()
        oT = A.alloc((8, S_LEN), BF16)
        m1 = A.mark()
        uT = A.alloc((8, S_LEN), BF16)
        wq1 = [A.alloc((8, 384), BF16) for _ in range(2)]
        q1 = [A.alloc((S_LEN,), BF16) for _ in range(2)]
        k1 = [A.alloc((S_LEN,), BF16) for _ in range(2)]
        v1 = [A.alloc((16, 2, 128), BF16) for _ in range(2)]
        btab = [A.alloc((640,), F32) for _ in range(2)]
        expB = [A.alloc((640,), BF16) for _ in range(2)]
        mask1 = A.alloc((640,), F32)
        nmask1 = A.alloc((640,), F32)
        pbuf = [A.alloc((512,), BF16) for _ in range(6)]
        rec = A.alloc((512,), F32)
        dma("sp", mask1[:, :], mask1_d[:, :], [], [("mask1",)])
        dma("sp", nmask1[:, :], nmask1_d[:, :], [], [("nmask1",)])
        for sl in range(2):
            S.op("dve", (lambda sl: lambda e: e.memset(v1[sl][:, :, :, 64:128], 1.0))(sl), reads=[], writes=[("v11", sl)])
        for tb in range(4):
            norm_main(tb, G_MIX1, uT[:, :, tb * 512:(tb + 1) * 512], lambda c, tb=tb: ("uT", c, tb))
        gcount = 0
        pcount = 0
        for jp in range(8):
            sl = jp % 2
            for part in range(3):
                dma("pool", wq1[sl][:, :, part * 128:(part + 1) * 128],
                    w_qkv_d[:, :, part * 1024 + jp * 128:part * 1024 + (jp + 1) * 128], [], [("wq1", sl, part)])
            for tb in range(4):
                ts = slice(tb * 512, (tb + 1) * 512)
                b = state["PS"].next()
                for c in range(8):
                    mm(ps[b][:, :], wq1[sl][:, c, 0:128], uT[:, c, ts], c == 0, c == 7,
                       [("wq1", sl, 0), ("uT", c, tb)], [("ps", b)])
                act_mul(q1[sl][:, ts], ps[b][:, :], 0.125, [("ps", b)], [("q1", sl, tb)])
                b = state["PS"].next()
                for c in range(8):
                    mm(ps[b][:, :], wq1[sl][:, c, 128:256], uT[:, c, ts], c == 0, c == 7,
                       [("wq1", sl, 1), ("uT", c, tb)], [("ps", b)])
                act_copy(k1[sl][:, ts], ps[b][:, :], [("ps", b)], [("k1", sl, tb)])
            for tq in range(4):
                b = state["PS"].next()
                for q in range(4):
                    tile = tq * 4 + q
                    for c in range(8):
                        mm(ps[b][:, q * 128:(q + 1) * 128], uT[:, c, tile * 128:(tile + 1) * 128], wq1[sl][:, c, 256:384],
                           c == 0, c == 7, [("wq1", sl, 2), ("uT", c, tile // 4)], [("ps", b)])
                dve_copy(v1[sl][:, tq * 4:(tq + 1) * 4, :, 0:64],
                         ps[b][:, :].rearrange("p (a h b) -> p a h b", a=4, h=2), [("ps", b)], [("v1", sl, tq)])
            for hh in range(2):
                h = 2 * jp + hh
                bs = hh
                dma("sp", btab[bs][:, :], btab_d[h], [], [("btab", bs)])
                dve_tt(btab[bs][:, :], btab[bs][:, :], mask1[:, :], ALU.mult, [("btab", bs), ("mask1",)], [("btab", bs)])
                dve_tt(expB[bs][:, :], btab[bs][:, :], nmask1[:, :], ALU.add, [("btab", bs), ("nmask1",)], [("expB", bs)])
            if True:
                items = []
                for i in range(4):
                    kts = list(range(max(0, 4 * i - 4), 4 * i + 4))
                    for kt in kts:
                        for hh in range(2):
                            tlo = max(128 * kt, 512 * i)
                            thi = min(128 * kt + 640, 512 * i + 512)
                            items.append(dict(i=i, kt=kt, tlo=tlo, thi=thi, n=thi - tlo, tl0=tlo - 128 * kt, hh=hh,
                                              first=(kt == kts[0]), last=(kt == kts[-1]), g=hh, p=pcount))
                            pcount += 1

                def d1(it):
                    pb = it["hh"] * 64
                    bs = it["hh"]
                    i, kt, tlo, thi, n = it["i"], it["kt"], it["tlo"], it["thi"], it["n"]
                    zb = state["PS"].next()
                    it["zb"] = zb
                    mm(ps[zb][:, 0:n], k1[sl][pb:pb + 64, kt * 128:(kt + 1) * 128], q1[sl][pb:pb + 64, tlo:thi],
                       True, False, [("k1", sl, kt // 4), ("q1", sl, i)], [("ps", zb)])
                    mm(ps[zb][:, 0:n], ident_bf, expB[bs][:, it["tl0"]:it["tl0"] + n],
                       False, True, [("expB", bs), ("cb",)], [("ps", zb)])
                    pk = it["p"] % 6
                    act(pbuf[pk][:, 0:n], ps[zb][:, 0:n], AF.Exp, [("ps", zb)], [("pbuf", pk)])

                def d2(it):
                    hh = it["hh"]
                    pb = hh * 64
                    i, kt, tlo, thi, n, g = it["i"], it["kt"], it["tlo"], it["thi"], it["n"], it["g"]
                    pk = it["p"] % 6
                    Ob = 4 + hh + 2 * (i % 2)
                    mm(ps[Ob][:, tlo - 512 * i:thi - 512 * i], v1[sl][:, kt, hh, :], pbuf[pk][:, 0:n], it["first"], it["last"],
                       [("v1", sl, kt // 4), ("v11", sl), ("pbuf", pk)], [("ps", Ob)])
                    if it["last"]:
                        S.op("dve", (lambda Ob: lambda e: e.reciprocal(rec[64:128, :], ps[Ob][64:128, :]))(Ob),
                             reads=[("ps", Ob)], writes=[("rec",)])
                        dve_tt(oT[pb:pb + 64, jp, 512 * i:512 * i + 512], ps[Ob][0:64, :], rec[64:128, :], ALU.mult,
                               [("ps", Ob), ("rec",)], [("oT", jp, i)])

                n_it = len(items)
                LD = 4
                for step in range(n_it + LD):
                    if step < n_it:
                        d1(items[step])
                    if 0 <= step - LD < n_it:
                        d2(items[step - LD])
        S.barrier()
        A.release(m1)
        state["PS"] = Rot(range(8))
        outproj(w_oo_d, oT)
        S.barrier()
        A.release(m0)

    for st in stages:
        if st == "mix0":
            mix0()
        elif st == "ffn0":
            ffn(0, G_FFN0)
        elif st == "mix1":
            mix1()
        elif st == "ffn1":
            ffn(1, G_FFN1)
    S.barrier()
    state["PS"] = Rot(range(8))
    stage_o = [A.alloc((8, 512), F32) for _ in range(2)]
    for tb in range(4):
        so = stage_o[tb % 2]
        t0 = tb * 512
        if final_norm:
            act(sq[:, :, :], hT[:, :, t0:t0 + 512], AF.Square, [("h", c, tb) for c in range(8)], [("sq",)])
            bank = state["PS"].next()
            for c in range(8):
                mm(ps[bank][:, :], ones_bf, sq[:, c, :], c == 0, c == 7, [("sq",), ("cb",)], [("ps", bank)])
            rstd_from(bank, 1.0 / D)
            for c in range(8):
                S.op("dve", (lambda c, so, t0: lambda e: e.scalar_tensor_tensor(
                    so[:, c, :], hT[:, c, t0:t0 + 512], gvec[:, G_FINAL + c:G_FINAL + c + 1], rstd[:, :],
                    ALU.mult, ALU.mult))(c, so, t0),
                    reads=[("h", c, tb), ("rstd",), ("gvec",)], writes=[("so", tb % 2)])
            tok = dma("sp", outT_v[:, :, t0:t0 + 512], so[:, :, :], [("so", tb % 2)], [("out", tb)])
        else:
            tok = dma("sp", outT_v[:, :, t0:t0 + 512], hT[:, :, t0:t0 + 512], [("h", c, tb) for c in range(8)], [("out", tb)])
        S.final_dma.append(tok)

    S.finalize(nc)
    with nc.Block() as block:
        @block.tensor
        def _(e):
            S.emit("pe", e)

        @block.scalar
        def _(e):
            S.emit("act", e)

        @block.vector
        def _(e):
            S.emit("dve", e)

        @block.gpsimd
        def _(e):
            S.emit("pool", e)

        @block.sync
        def _(e):
            S.emit("sp", e)
    S.close()
    nc._arena_peak = A.peak if False else None
    return nc


def _consts():
    s = np.arange(128)[:, None]
    t = np.arange(128)[None, :]
    cm = np.zeros((128, NCM), np.float32)
    cm[:, C_ONES:C_ONES + 128] = 1.0
    cm[:, C_NEG:C_NEG + 128] = -1.0
    cm[:, C_TRI:C_TRI + 128] = np.where(s >= t, -1.0, 0.0)
    cm[:, C_MSB:C_MSB + 128] = np.where(s < t, 1.0, 0.0)
    cm[:, C_MMLA:C_MMLA + 128] = np.where((s // 64) <= (t // 64), 1.0, 0.0)
    cm[:, C_ID:C_ID + 128] = np.eye(128, dtype=np.float32)
    pos = np.arange(S_LEN, dtype=np.float32)
    inv_freq = (np.float32(10000.0) ** (-np.arange(0, 32, 2, dtype=np.float32) / np.float32(32))).astype(np.float32)
    ang = (pos[:, None] * inv_freq[None, :]).astype(np.float32)
    cos = np.cos(ang).astype(np.float32).T
    sin = np.sin(ang).astype(np.float32).T
    cc = np.concatenate([cos, cos], 0)
    ss = np.concatenate([-sin, sin], 0)
    cc4 = np.ascontiguousarray(np.tile(cc, (4, 1)))
    ss4 = np.ascontiguousarray(np.tile(ss, (4, 1)))
    sl = np.arange(128)[:, None]
    tl = np.arange(640)[None, :]
    d = tl // 64 - sl // 64
    mask1 = ((d >= 0) & (d <= 8)).astype(np.float32)
    bidx = np.clip(tl - sl, -256, 256) + 256
    return cm, cc4, ss4, mask1, bidx


def _gcol(g):
    return np.ascontiguousarray(np.asarray(g, np.float32).reshape(-1, 128).T)


_CACHE = {}


def run(inputs, stages=("mix0", "ffn0", "mix1", "ffn1"), final_norm=True, trace=False):
    f = lambda a: np.ascontiguousarray(np.asarray(a, dtype=np.float32))
    x = f(inputs["x"])
    cm, cc4, ss4, mask1, bidx = _consts()
    gvec = np.concatenate([
        _gcol(inputs["g_mix"][0]), _gcol(inputs["g_ffn"][0]), _gcol(inputs["g_mix"][1]), _gcol(inputs["g_ffn"][1]),
        _gcol(inputs["g_final"]), _gcol(inputs["ev_g_cq"][0]), _gcol(inputs["ev_g_ckv"][0])], axis=1)
    gvec = np.ascontiguousarray(gvec.astype(np.float32))
    assert gvec.shape == (128, NG)
    rb = f(inputs["od_rel_bias"])[0]
    bias_tab = np.ascontiguousarray(rb[:, bidx])
    shared = {
        "ev_w_in": f(inputs["ev_w_in"])[0], "ev_w_uq": f(inputs["ev_w_uq"])[0], "ev_w_ukv": f(inputs["ev_w_ukv"])[0],
        "ev_w_out": f(inputs["ev_w_out"])[0], "od_w_qkv": f(inputs["od_w_qkv"])[0], "od_w_out": f(inputs["od_w_out"])[0],
        "w_gate": f(inputs["w_gate"]), "w_up": f(inputs["w_up"]), "w_down": f(inputs["w_down"]),
        "gvec": gvec, "cmat": cm, "rope_cc": cc4, "rope_ss": ss4, "bias_tab": bias_tab, "mask1": mask1,
        "nmask1": np.ascontiguousarray((mask1 - 1.0) * 30000.0).astype(np.float32),
    }
    key = (tuple(stages), final_norm)
    if key not in _CACHE:
        _CACHE[key] = build(stages, final_norm)
    nc = _CACHE[key]
    in_maps = []
    for b in range(NCORES):
        m = dict(shared)
        m["xT"] = np.ascontiguousarray(x[b].T)
        in_maps.append(m)
    res = run_bass_kernel_spmd(nc, in_maps, core_ids=list(range(NCORES)), **({"trace": True} if trace else {}))
    out = np.stack([np.ascontiguousarray(np.asarray(r["outT"]).T) for r in res.results], axis=0)
    return out.astype(np.float32), res


def kernel(**inputs):
    out, _ = run(inputs)
    return out
```

```python
import numpy as np
import concourse.bass as bass
import concourse.mybir as mybir
from concourse.bass_utils import run_bass_kernel_spmd

F32 = mybir.dt.float32
BF16 = mybir.dt.bfloat16
U8 = mybir.dt.uint8
ALU = mybir.AluOpType
AF = mybir.ActivationFunctionType

import os as _os
_SKIP_B = bool(_os.environ.get("K_SKIP_B"))
N_DUMMY_SB = int(_os.environ.get("K_DUMMY_SB", "3"))
N_DUMMY_M1 = int(_os.environ.get("K_DUMMY_M1", "0"))
_SKIP_C = bool(_os.environ.get("K_SKIP_C"))
S_LEN = 2048
D = 1024
DFF = 2816
EPS = 1e-6
NCORES = 8
ENGS = ("pe", "act", "dve", "pool", "sp")
NDMA = 24
NDMA_POOL = 16
SEM_LIMIT = 3000
NSEM_PER_ENG = 10


DEBUG_NAMES = {}


class Op:
    __slots__ = ("eng", "fn", "deps", "dmadeps", "marked", "sem", "cnt", "idx", "dma", "tag")

    def __init__(self, eng, fn):
        self.eng = eng
        self.fn = fn
        self.deps = {}
        self.dmadeps = {}
        self.marked = False
        self.sem = None
        self.cnt = 0
        self.idx = -1
        self.dma = None


class Sched:
    def __init__(self):
        self.ops = {e: [] for e in ENGS}
        self.last_w = {}
        self.readers = {}
        self.dma_use = [0] * NDMA
        self.dma_rr = 0
        self.dma_rr_pool = 0
        self.pending = {e: [] for e in ENGS}
        self.final_dma = []

    def _add_dep(self, o, tok):
        if tok is None:
            return
        if not isinstance(tok, tuple) and tok.dma is not None:
            tok = tok.dma
        if isinstance(tok, tuple):
            i, v = tok
            if o.dmadeps.get(i, 0) < v:
                o.dmadeps[i] = v
            return
        if tok.eng == o.eng and o.eng in ("pe", "sp"):
            return
        cur = o.deps.get(tok.eng)
        if cur is None or cur.idx < tok.idx:
            o.deps[tok.eng] = tok

    def op(self, eng, fn, reads=(), writes=(), dma=False, raw_same_only=True):
        o = Op(eng, fn)
        o.tag = (eng, tuple(reads), tuple(writes))
        o.idx = len(self.ops[eng])
        for tok in self.pending[eng]:
            self._add_dep(o, tok)
        self.pending[eng] = []
        for k in reads:
            self._add_dep(o, self.last_w.get(k))
        for k in writes:
            self._add_dep(o, self.last_w.get(k))
            for r in self.readers.get(k, ()):
                self._add_dep(o, r)
        tok = o
        if dma:
            if eng == "pool":
                i = self.dma_rr_pool
                self.dma_rr_pool = (self.dma_rr_pool + 1) % NDMA_POOL
            else:
                i = NDMA_POOL + self.dma_rr
                self.dma_rr = (self.dma_rr + 1) % (NDMA - NDMA_POOL)
            if self.dma_use[i] > 0:
                self._add_dep(o, (i, 16 * self.dma_use[i]))
            self.dma_use[i] += 1
            o.dma = (i, 16 * self.dma_use[i])
            tok = o.dma
        for k in reads:
            self.readers.setdefault(k, []).append(tok)
        for k in writes:
            self.last_w[k] = tok
            self.readers[k] = []
        self.ops[eng].append(o)
        return tok

    def barrier(self):
        toks = []
        for e in ENGS:
            if self.ops[e]:
                toks.append(self.ops[e][-1])
        for i in range(NDMA):
            if self.dma_use[i] > 0:
                toks.append((i, 16 * self.dma_use[i]))
        for e in ENGS:
            self.pending[e] = list(toks)
        self.last_w = {}
        self.readers = {}

    def finalize(self, nc):
        for e in ENGS:
            for o in self.ops[e]:
                for d in o.deps.values():
                    d.marked = True
        esems = {}
        self._ctx = []
        for e in ENGS:
            n = sum(1 for o in self.ops[e] if o.marked)
            need = max(1, (n + SEM_LIMIT - 1) // SEM_LIMIT)
            assert need <= NSEM_PER_ENG, (e, n)
            lst = []
            for k in range(need):
                cm = nc.semaphore(f"s_{e}_{k}")
                lst.append(cm.__enter__())
                self._ctx.append(cm)
            esems[e] = lst
            c = 0
            for o in self.ops[e]:
                if o.marked:
                    o.sem = lst[c // SEM_LIMIT]
                    o.cnt = c % SEM_LIMIT + 1
                    c += 1
        dsems = []
        for i in range(NDMA):
            cm = nc.semaphore(f"s_dma_{i}")
            dsems.append(cm.__enter__())
            self._ctx.append(cm)
        self.dsems = dsems

    def emit(self, eng_name, engobj):
        waited = {}
        dwaited = {}
        for o in self.ops[eng_name]:
            for src, d in o.deps.items():
                if waited.get(src, -1) >= d.idx:
                    continue
                engobj.wait_ge(d.sem, d.cnt)
                waited[src] = d.idx
            for i, v in o.dmadeps.items():
                if dwaited.get(i, 0) >= v:
                    continue
                engobj.wait_ge(self.dsems[i], v)
                dwaited[i] = v
            ins = o.fn(engobj)
            try:
                DEBUG_NAMES[ins.ins.name] = o.tag
            except Exception:
                pass
            if o.dma is not None:
                ins.then_inc(self.dsems[o.dma[0]], 16)
            elif o.marked:
                ins.then_inc(o.sem, 1)
        if eng_name == "sp":
            for (i, v) in self.final_dma:
                if dwaited.get(i, 0) < v:
                    engobj.wait_ge(self.dsems[i], v)
                    dwaited[i] = v

    def close(self):
        for cm in reversed(self._ctx):
            cm.__exit__(None, None, None)


class Arena:
    def __init__(self, nc, nbytes):
        self.t = nc.alloc_sbuf_tensor("arena", [128, nbytes], U8)
        self.n = nbytes
        self.off = 0
        self.peak = 0

    def alloc(self, shape, dtype):
        esz = 4 if dtype == F32 else 2
        n = 1
        for s in shape:
            n *= s
        nb = n * esz
        off = (self.off + 63) // 64 * 64
        assert off + nb <= self.n, ("arena overflow", off, nb, self.n)
        self.off = off + nb
        self.peak = max(self.peak, self.off)
        v = self.t[:, off:off + nb].bitcast(dtype)
        if len(shape) == 2:
            v = v.rearrange("p (a b) -> p a b", a=shape[0])
        elif len(shape) == 3:
            v = v.rearrange("p (a b c) -> p a b c", a=shape[0], b=shape[1])
        return v

    def mark(self):
        return self.off

    def release(self, m):
        self.off = m


class Rot:
    def __init__(self, items):
        self.items = list(items)
        self.i = 0

    def next(self):
        v = self.items[self.i % len(self.items)]
        self.i += 1
        return v


G_MIX0, G_FFN0, G_MIX1, G_FFN1, G_FINAL, G_CQ, G_CKV, NG = 0, 8, 16, 24, 32, 40, 43, 45
C_ONES, C_NEG, C_TRI, C_MSB, C_MMLA, C_ID, NCM = 0, 128, 256, 384, 512, 640, 768


def build(stages=("mix0", "ffn0", "mix1", "ffn1"), final_norm=True):
    nc = bass.Bass("TRN2", target_bir_lowering=False)

    def din(name, shape):
        return nc.dram_tensor(name, list(shape), F32, kind="ExternalInput").ap()

    xT_d = din("xT", [D, S_LEN])
    w_in_d = din("ev_w_in", [D, 2208]).rearrange("(c p) n -> p c n", p=128)
    w_uq_d = din("ev_w_uq", [384, 768]).rearrange("(c p) n -> p c n", p=128)
    w_ukv_d = din("ev_w_ukv", [256, 1024]).rearrange("(c p) n -> p c n", p=128)
    w_eo_d = din("ev_w_out", [D, D]).rearrange("(c p) n -> p c n", p=128)
    w_qkv_d = din("od_w_qkv", [D, 3072]).rearrange("(c p) n -> p c n", p=128)
    w_oo_d = din("od_w_out", [D, D]).rearrange("(c p) n -> p c n", p=128)
    w_gate_d = din("w_gate", [2, D, DFF])
    w_up_d = din("w_up", [2, D, DFF])
    w_down_d = din("w_down", [2, DFF, D])
    gvec_d = din("gvec", [128, NG])
    cmat_d = din("cmat", [128, NCM])
    cc_d = din("rope_cc", [128, S_LEN])
    ss_d = din("rope_ss", [128, S_LEN])
    btab_d = din("bias_tab", [16, 128, 640])
    mask1_d = din("mask1", [128, 640])
    nmask1_d = din("nmask1", [128, 640])
    outT_d = nc.dram_tensor("outT", [D, S_LEN], F32, kind="ExternalOutput").ap()
    outT_v = outT_d.rearrange("(c p) t -> p c t", p=128)
    xT_v = xT_d.rearrange("(c p) t -> p c t", p=128)

    S = Sched()
    A = Arena(nc, 211968)
    psall = nc.alloc_psum_tensor("psall", [128, 4096], F32)
    ps = [psall[:, b * 512:(b + 1) * 512] for b in range(8)]

    hT = A.alloc((8, S_LEN), F32)
    cb = A.alloc((NCM,), BF16)
    gvec = A.alloc((NG,), F32)
    sq = A.alloc((8, 512), BF16)
    rstd = A.alloc((512,), F32)
    ones_bf = cb[:, C_ONES:C_ONES + 128]
    neg_bf = cb[:, C_NEG:C_NEG + 128]
    tri_bf = cb[:, C_TRI:C_TRI + 128]
    msb_bf = cb[:, C_MSB:C_MSB + 128]
    mmla_bf = cb[:, C_MMLA:C_MMLA + 128]
    ident_bf = cb[:, C_ID:C_ID + 128]

    PS = Rot(range(8))
    state = {"PS": PS}

    def mm(out, lhsT, rhs, start, stop, reads, writes):
        S.op("pe", lambda e: e.matmul(out, lhsT, rhs, start=start, stop=stop, skip_group_check=True),
             reads=reads, writes=writes)

    def dve_tt(out, in0, in1, op, reads, writes):
        S.op("dve", lambda e: e.tensor_tensor(out, in0, in1, op), reads=reads, writes=writes)

    def dve_tss(out, in_, scalar, op, reads, writes):
        S.op("dve", lambda e: e.tensor_single_scalar(out, in_, scalar, op), reads=reads, writes=writes)

    def dve_copy(out, in_, reads, writes):
        S.op("dve", lambda e: e.tensor_copy(out, in_), reads=reads, writes=writes)

    def act(out, in_, func, reads, writes, bias=0.0, scale=1.0):
        S.op("act", lambda e: e.activation(out, in_, func, bias=bias, scale=scale), reads=reads, writes=writes)

    def act_copy(out, in_, reads, writes):
        S.op("act", lambda e: e.copy(out, in_), reads=reads, writes=writes)

    def act_mul(out, in_, m, reads, writes):
        S.op("act", lambda e: e.mul(out, in_, m), reads=reads, writes=writes)

    def pool_copy(out, in_, reads, writes):
        S.op("pool", lambda e: e.tensor_copy(out, in_), reads=reads, writes=writes)

    def dma(eng, out, in_, reads, writes):
        return S.op(eng, lambda e: e.dma_start(out=out, in_=in_), reads=reads, writes=writes, dma=True)

    dma("pool", cb[:, :], cmat_d[:, :], [], [("cb",)])
    dma("sp", gvec[:, :], gvec_d[:, :], [], [("gvec",)])
    for c in range(8):
        dma("sp", hT[:, c, :], xT_v[:, c, :], [], [("h", c, tb) for tb in range(4)])

    def rstd_from(bank, inv_n):
        S.op("act", lambda e: e.activation(rstd[:, :], ps[bank][:, :], AF.Sqrt, bias=EPS, scale=inv_n),
             reads=[("ps", bank)], writes=[("rstd",)])
        S.op("dve", lambda e: e.reciprocal(rstd[:, :], rstd[:, :]),
             reads=[("rstd",)], writes=[("rstd",)])

    def norm_main(tb, gcol, dst, dkeys):
        t0 = tb * 512
        act(sq[:, :, :], hT[:, :, t0:t0 + 512], AF.Square, [("h", c, tb) for c in range(8)], [("sq",)])
        bank = state["PS"].next()
        for c in range(8):
            mm(ps[bank][:, :], ones_bf, sq[:, c, :], c == 0, c == 7, [("sq",), ("cb",)], [("ps", bank)])
        rstd_from(bank, 1.0 / D)
        for c in range(8):
            S.op("dve", (lambda c: lambda e: e.scalar_tensor_tensor(
                dst[:, c, :], hT[:, c, t0:t0 + 512], gvec[:, gcol + c:gcol + c + 1], rstd[:, :],
                ALU.mult, ALU.mult))(c),
                reads=[("h", c, tb), ("rstd",), ("gvec",)], writes=[dkeys(c)])

    def ffn(l, gcol):
        S.barrier()
        state["PS"] = Rot(range(8))
        m0 = A.mark()
        uThs = [A.alloc((8, 1024), BF16) for _ in range(2)]
        actT = A.alloc((22, 1024), BF16)
        wgu = [A.alloc((2, 8, 256), BF16) for _ in range(3)]
        wdc = [A.alloc((22, 128), BF16) for _ in range(3)]
        sg = [A.alloc((512,), F32) for _ in range(2)]
        wg_v = w_gate_d[l].rearrange("(c p) f -> p c f", p=128)
        wu_v = w_up_d[l].rearrange("(c p) f -> p c f", p=128)
        wd_v = w_down_d[l].rearrange("(c p) d -> p c d", p=128)
        n_w = 0
        n_d = 0
        n_s = 0
        def ffn_norm(th):
            for tb2 in range(2):
                tb = th * 2 + tb2
                norm_main(tb, gcol, uThs[th][:, :, tb2 * 512:(tb2 + 1) * 512], lambda c, tb2=tb2, th=th: ("uTh", th, c, tb2))

        ffn_norm(0)
        for th in range(2):
            uTh = uThs[th]
            for fg in range(11):
                if th == 0 and fg == 3:
                    ffn_norm(1)
                slot = n_w % 3
                n_w += 1
                buf = wgu[slot]
                dma("pool", buf[:, 0, :, :], wg_v[:, :, fg * 256:(fg + 1) * 256], [], [("wgu", slot, 0)])
                dma("pool", buf[:, 1, :, :], wu_v[:, :, fg * 256:(fg + 1) * 256], [], [("wgu", slot, 1)])
                for fc in range(2):
                    f = fg * 2 + fc
                    for tb2 in range(2):
                        ts = slice(tb2 * 512, (tb2 + 1) * 512)
                        bg = state["PS"].next()
                        bu = state["PS"].next()
                        for c in range(8):
                            mm(ps[bg][:, :], buf[:, 0, c, fc * 128:(fc + 1) * 128], uTh[:, c, ts], c == 0, c == 7,
                               [("wgu", slot, 0), ("uTh", th, c, tb2)], [("ps", bg)])
                        for c in range(8):
                            mm(ps[bu][:, :], buf[:, 1, c, fc * 128:(fc + 1) * 128], uTh[:, c, ts], c == 0, c == 7,
                               [("wgu", slot, 1), ("uTh", th, c, tb2)], [("ps", bu)])
                        sgb = sg[n_s % 2]
                        sk = ("sg", n_s % 2)
                        n_s += 1
                        act(sgb[:, :], ps[bg][:, :], AF.Silu, [("ps", bg)], [sk])
                        dve_tt(actT[:, f, ts], ps[bu][:, :], sgb[:, :], ALU.mult, [("ps", bu), sk], [("actT", f, tb2)])
            for dc in range(8):
                slot = n_d % 3
                n_d += 1
                wb = wdc[slot]
                dma("pool", wb[:, 0:11, :], wd_v[:, 0:11, dc * 128:(dc + 1) * 128], [], [("wdc", slot)])
                dma("pool", wb[:, 11:22, :], wd_v[:, 11:22, dc * 128:(dc + 1) * 128], [], [("wdc", slot)])
                for tb2 in range(2):
                    tb = th * 2 + tb2
                    ts = slice(tb2 * 512, (tb2 + 1) * 512)
                    b = state["PS"].next()
                    for f in range(22):
                        mm(ps[b][:, :], wb[:, f, :], actT[:, f, ts], f == 0, f == 21,
                           [("wdc", slot), ("actT", f, tb2)], [("ps", b)])
                    hs = hT[:, dc, tb * 512:(tb + 1) * 512]
                    dve_tt(hs, ps[b][:, :], hs, ALU.add, [("ps", b), ("h", dc, tb)], [("h", dc, tb)])
        S.barrier()
        A.release(m0)

    def outproj(w_d, oT):
        wo = A.alloc((8, 1024), BF16)
        for half in range(2):
            dma("pool", wo[:, :, half * 512:(half + 1) * 512], w_d[:, :, half * 512:(half + 1) * 512], [], [("wo", half)])
        for dc in range(8):
            for tb in range(4):
                b = state["PS"].next()
                for kc in range(8):
                    mm(ps[b][:, :], wo[:, kc, dc * 128:(dc + 1) * 128], oT[:, kc, tb * 512:(tb + 1) * 512],
                       kc == 0, kc == 7, [("wo", dc // 4), ("oT", kc, tb)], [("ps", b)])
                hs = hT[:, dc, tb * 512:(tb + 1) * 512]
                dve_tt(hs, ps[b][:, :], hs, ALU.add, [("ps", b), ("h", dc, tb)], [("h", dc, tb)])

    def mix0():
        S.barrier()
        state["PS"] = Rot(range(8))
        m0 = A.mark()
        cqn = A.alloc((3, S_LEN), BF16)
        ckvn = A.alloc((2, S_LEN), BF16)
        krope = A.alloc((S_LEN,), BF16)
        oT = A.alloc((8, S_LEN), BF16)
        mU = A.mark()
        uT = A.alloc((8, S_LEN), BF16)
        mA = A.mark()
        wA = A.alloc((8, 704), BF16)
        CCb = [A.alloc((512,), F32) for _ in range(2)]
        SSb = [A.alloc((512,), F32) for _ in range(2)]
        t1 = A.alloc((512,), F32)
        t2 = A.alloc((512,), F32)
        dma("pool", wA[:, :, 0:672], w_in_d[:, :, 0:672], [], [("wA", 0)])
        dma("pool", wA[:, :, 672:688], w_in_d[:, :, 656:672], [], [("wA", 1)])
        dma("pool", wA[:, :, 688:704], w_in_d[:, :, 640:656], [], [("wA", 2)])
        norm_main(0, G_MIX0, uT[:, :, 0:512], lambda c: ("uT", c, 0))
        for tb in range(4):
            ts = slice(tb * 512, (tb + 1) * 512)
            if tb + 1 < 4:
                norm_main(tb + 1, G_MIX0, uT[:, :, (tb + 1) * 512:(tb + 2) * 512], lambda c, tb=tb: ("uT", c, tb + 1))
            for (nch, col0, gc, dst, dname, invn) in ((3, 0, G_CQ, cqn, "cqn", 1.0 / 384),
                                                       (2, 384, G_CKV, ckvn, "ckvn", 1.0 / 256)):
                banks = []
                for j in range(nch):
                    b = state["PS"].next()
                    banks.append(b)
                    for c in range(8):
                        mm(ps[b][:, :], wA[:, c, col0 + j * 128:col0 + (j + 1) * 128], uT[:, c, ts], c == 0, c == 7,
                           [("wA", 0), ("uT", c, tb)], [("ps", b)])
                    act(sq[:, j, :], ps[b][:, :], AF.Square, [("ps", b)], [("sq",)])
                bs = state["PS"].next()
                for j in range(nch):
                    mm(ps[bs][:, :], ones_bf, sq[:, j, :], j == 0, j == nch - 1, [("sq",), ("cb",)], [("ps", bs)])
                rstd_from(bs, invn)
                for j in range(nch):
                    b = banks[j]
                    S.op("dve", (lambda j, b, dst, gc, ts: lambda e: e.scalar_tensor_tensor(
                        dst[:, j, ts], ps[b][:, :], gvec[:, gc + j:gc + j + 1], rstd[:, :], ALU.mult, ALU.mult))(j, b, dst, gc, ts),
                        reads=[("ps", b), ("rstd",), ("gvec",)], writes=[(dname, j, tb)])
            b = state["PS"].next()
            for c in range(8):
                mm(ps[b][0:64, :], wA[:, c, 640:704], uT[:, c, ts], c == 0, c == 7,
                   [("wA", 0), ("wA", 1), ("wA", 2), ("uT", c, tb)], [("ps", b)])
            cc = CCb[tb % 2]
            ss = SSb[tb % 2]
            dma("sp", cc[:, :], cc_d[:, ts], [], [("CCb", tb % 2)])
            dma("sp", ss[:, :], ss_d[:, ts], [], [("SSb", tb % 2)])
            dve_tt(t1[0:32, :], ps[b][0:32, :], cc[0:32, :], ALU.mult, [("ps", b), ("CCb", tb % 2)], [("t1",)])
            dve_tt(t2[0:32, :], ps[b][32:64, :], ss[32:64, :], ALU.mult, [("ps", b), ("SSb", tb % 2)], [("t2",)])
            dve_tt(krope[0:32, ts], t1[0:32, :], t2[0:32, :], ALU.add, [("t1",), ("t2",)], [("krope", tb)])
        S.barrier()
        A.release(mA)

        if not _SKIP_B:
            mB = A.mark()
            state["PS"] = Rot([7])
            wB = A.alloc((8, 384), BF16)
            qb = A.alloc((S_LEN,), BF16)
            kb = A.alloc((S_LEN,), BF16)
            vb = A.alloc((16, 128), BF16)
            ebuf = [A.alloc((2, 512), F32) for _ in range(1)]
            spb = [A.alloc((2, 512), BF16) for _ in range(3)]
            wbuf = [A.alloc((2, 512), BF16) for _ in range(3)]
            sacc = A.alloc((2, 512), F32)
            saccb = [A.alloc((2, 512), BF16) for _ in range(2)]
            ZZ = [psall[:, q * 1024:(q + 1) * 1024].rearrange("p (a c) -> p a c", a=2) for q in range(3)]
            RR = psall[:, 2048:3072].rearrange("p (a c) -> p a c", a=2)
            pcount = 0
            for j in range(4):
                dma("pool", wB[:, :, 0:128], w_in_d[:, :, 672 + j * 128:672 + (j + 1) * 128], [], [("wB", 0)])
                dma("pool", wB[:, :, 128:256], w_in_d[:, :, 1184 + j * 128:1184 + (j + 1) * 128], [], [("wB", 1)])
                dma("pool", wB[:, :, 256:384], w_in_d[:, :, 1696 + j * 128:1696 + (j + 1) * 128], [], [("wB", 2)])
                for tb in range(4):
                    ts = slice(tb * 512, (tb + 1) * 512)
                    b = 7
                    for c in range(8):
                        mm(ps[b][:, :], wB[:, c, 0:128], uT[:, c, ts], c == 0, c == 7,
                           [("wB", 0), ("uT", c, tb)], [("ps", b)])
                    dve_tss(qb[:, ts], ps[b][:, :], 0.125, ALU.mult, [("ps", b)], [("qb", tb)])
                    b = 7
                    for c in range(8):
                        mm(ps[b][:, :], wB[:, c, 128:256], uT[:, c, ts], c == 0, c == 7,
                           [("wB", 1), ("uT", c, tb)], [("ps", b)])
                    dve_copy(kb[:, ts], ps[b][:, :], [("ps", b)], [("kb", tb)])
                for tq in range(4):
                    b = 7
                    for q in range(4):
                        tile = tq * 4 + q
                        for c in range(8):
                            mm(ps[b][:, q * 128:(q + 1) * 128], uT[:, c, tile * 128:(tile + 1) * 128], wB[:, c, 256:384],
                               c == 0, c == 7, [("wB", 2), ("uT", c, tile // 4)], [("ps", b)])
                    dve_copy(vb[:, tq * 4:(tq + 1) * 4, :], ps[b][:, :].rearrange("p (a b) -> p a b", a=4),
                             [("ps", b)], [("vb", tq)])
                items = []
                for i in range(4):
                    for kt in range(4 * i + 3, -1, -1):
                        c0 = max(0, 128 * kt - 512 * i)
                        c0p = max(0, 128 * (kt + 1) - 512 * i)
                        items.append(dict(i=i, kt=kt, c0=c0, c0p=c0p, n=512 - c0, first=(kt == 4 * i + 3),
                                          last=(kt == 0), diag=(kt >= 4 * i), p=pcount))
                        pcount += 1

                def s1(it):
                    i, kt, c0, n = it["i"], it["kt"], it["c0"], it["n"]
                    zq = it["p"] % 3
                    zk = [("ps", 2 * zq), ("ps", 2 * zq + 1)]
                    for hh in range(2):
                        pb = hh * 64
                        mm(ps[2 * zq + hh][:, c0:512], kb[pb:pb + 64, kt * 128:(kt + 1) * 128],
                           qb[pb:pb + 64, 512 * i + c0:512 * i + 512], True, True,
                           [("kb", kt // 4), ("qb", i)], [("ps", 2 * zq + hh)])
                    ek = 0
                    sk = it["p"] % 3
                    act(ebuf[ek][:, :, 0:n], ZZ[zq][:, :, c0:512], AF.Exp, zk, [("ebuf", ek)])
                    act(spb[sk][:, :, 0:n], ebuf[ek][:, :, 0:n], AF.Ln, [("ebuf", ek)], [("spb", sk)], bias=1.0)
                    if it["diag"]:
                        for hh in range(2):
                            dve_tt(spb[sk][:, hh, 0:128], spb[sk][:, hh, 0:128], msb_bf, ALU.mult,
                                   [("spb", sk), ("cb",)], [("spb", sk)])

                def s2(it):
                    i, kt, c0, n = it["i"], it["kt"], it["c0"], it["n"]
                    zq = it["p"] % 3
                    zk = [("ps", 2 * zq), ("ps", 2 * zq + 1)]
                    sk3 = it["p"] % 3
                    sk = it["p"] % 3
                    pp = it["p"] % 2
                    for hh in range(2):
                        mm(ps[2 * zq + hh][:, c0:512], tri_bf, spb[sk3][:, hh, 0:n], False, it["first"],
                           [("spb", sk3), ("cb",)], [("ps", 2 * zq + hh)])
                    if not it["first"]:
                        for hh in range(2):
                            mm(ps[2 * zq + hh][:, c0:512], neg_bf, saccb[pp][:, hh, c0:512], False, True,
                               [("saccb", pp), ("cb",)], [("ps", 2 * zq + hh)])
                    act(wbuf[sk][:, :, 0:n], ZZ[zq][:, :, c0:512], AF.Exp, zk, [("wbuf", sk)])
                    for _d in range(N_DUMMY_SB):
                        mm(ps[7][:, :], ones_bf, uT[:, _d, 0:512], True, True, [("cb",)], [("ps", 7)])
                    if it["diag"]:
                        for hh in range(2):
                            dve_tt(wbuf[sk][:, hh, 0:128], wbuf[sk][:, hh, 0:128], msb_bf, ALU.mult,
                                   [("wbuf", sk), ("cb",)], [("wbuf", sk)])
                    if not it["last"]:
                        if it["first"]:
                            S.op("dve", lambda e: e.memset(sacc[:, :, :], 0.0), reads=[], writes=[("sacc",)])
                        dve_tt(sacc[:, :, c0:512], sacc[:, :, c0:512], spb[sk3][:, :, 0:n], ALU.add,
                               [("sacc",), ("spb", sk3)], [("sacc",)])
                        dve_copy(saccb[1 - pp][:, :, :], sacc[:, :, :], [("sacc",)], [("saccb", 1 - pp)])

                def s3(it):
                    i, kt, c0, n = it["i"], it["kt"], it["c0"], it["n"]
                    sk = it["p"] % 3
                    Ob = 6
                    for hh in range(2):
                        pb = hh * 64
                        mm(ps[Ob][pb:pb + 64, c0:512], vb[:, kt, pb:pb + 64], wbuf[sk][:, hh, 0:n], it["first"], it["last"],
                           [("vb", kt // 4), ("wbuf", sk)], [("ps", Ob)])
                    if it["last"]:
                        dve_copy(oT[:, 4 + j, 512 * i:512 * i + 512], ps[Ob][:, :], [("ps", Ob)], [("oT", 4 + j, i)])

                n_it = len(items)
                L1, L2 = 2, 4
                for step in range(n_it + L2):
                    if step < n_it:
                        s1(items[step])
                    if 0 <= step - L1 < n_it:
                        s2(items[step - L1])
                    if 0 <= step - L2 < n_it:
                        s3(items[step - L2])
            S.barrier()
            A.release(mU)

        if not _SKIP_C:
            mC = A.mark()
            state["PS"] = Rot([0, 1, 2, 3, 6, 7])
            wuq = A.alloc((3, 768), BF16)
            wsw = A.alloc((3, 8, 32), BF16)
            wukv = A.alloc((2, 1024), BF16)
            CC = A.alloc((S_LEN,), F32)
            SS = A.alloc((S_LEN,), F32)
            Qh = [A.alloc((S_LEN,), BF16) for _ in range(2)]
            Kh = [A.alloc((S_LEN,), BF16) for _ in range(2)]
            Vh = [A.alloc((16, 128), BF16) for _ in range(2)]
            pbuf = [A.alloc((512,), BF16) for _ in range(6)]
            rec = A.alloc((512,), F32)
            t1 = A.alloc((512,), F32)
            t2 = A.alloc((512,), F32)
            dma("pool", wuq[:, :, :], w_uq_d[:, :, :], [], [("wuq",)])
            wuq4 = w_uq_d.rearrange("p c (h f) -> p c h f", h=8)
            for kc in range(3):
                dma("pool", wsw[:, kc, :, 0:16], wuq4[:, kc, :, 80:96], [], [("wsw", 0)])
                dma("pool", wsw[:, kc, :, 16:32], wuq4[:, kc, :, 64:80], [], [("wsw", 1)])
            dma("pool", wukv[:, :, :], w_ukv_d[:, :, :], [], [("wukv",)])
            dma("sp", CC[:, :], cc_d[:, :], [], [("CC",)])
            dma("sp", SS[:, :], ss_d[:, :], [], [("SS",)])
            for sl in range(2):
                S.op("dve", (lambda sl: lambda e: e.memset(Vh[sl][:, :, 64:128], 1.0))(sl), reads=[], writes=[("Vh1", sl)])
            gcount = 0
            pcount = 0
            scale_a = 96.0 ** -0.5
            for h in range(8):
                sl = h % 2
                j = h // 2
                pb = (h % 2) * 64
                for tb in range(4):
                    ts = slice(tb * 512, (tb + 1) * 512)
                    bA = state["PS"].next()
                    for kc in range(3):
                        mm(ps[bA][0:96, :], wuq[:, kc, h * 96:(h + 1) * 96], cqn[:, kc, ts], kc == 0, kc == 2,
                           [("wuq",), ("cqn", kc, tb)], [("ps", bA)])
                    bB = state["PS"].next()
                    for kc in range(3):
                        mm(ps[bB][64:96, :], wsw[:, kc, h, :], cqn[:, kc, ts], kc == 0, kc == 2,
                           [("wsw", 0), ("wsw", 1), ("cqn", kc, tb)], [("ps", bB)])
                    act_copy(Qh[sl][0:64, ts], ps[bA][0:64, :], [("ps", bA)], [("Qh", sl, tb)])
                    dve_tt(t1[64:96, :], ps[bA][64:96, :], CC[64:96, ts], ALU.mult, [("ps", bA), ("CC",)], [("t1",)])
                    dve_tt(t2[64:96, :], ps[bB][64:96, :], SS[64:96, ts], ALU.mult, [("ps", bB), ("SS",)], [("t2",)])
                    dve_tt(Qh[sl][64:96, ts], t1[64:96, :], t2[64:96, :], ALU.add, [("t1",), ("t2",)], [("Qh", sl, tb)])
                    bK = state["PS"].next()
                    for kc in range(2):
                        mm(ps[bK][0:64, :], wukv[:, kc, h * 128:h * 128 + 64], ckvn[:, kc, ts], kc == 0, kc == 1,
                           [("wukv",), ("ckvn", kc, tb)], [("ps", bK)])
                    act_copy(Kh[sl][0:64, ts], ps[bK][0:64, :], [("ps", bK)], [("Kh", sl, tb)])
                    pool_copy(Kh[sl][64:96, ts], krope[0:32, ts], [("krope", tb)], [("Kh", sl, tb)])
                for half in range(2):
                    b = state["PS"].next()
                    for q in range(8):
                        tile = half * 8 + q
                        for kc in range(2):
                            mm(ps[b][:, q * 64:(q + 1) * 64], ckvn[:, kc, tile * 128:(tile + 1) * 128],
                               wukv[:, kc, h * 128 + 64:h * 128 + 128], kc == 0, kc == 1,
                               [("wukv",), ("ckvn", kc, tile // 4)], [("ps", b)])
                    dve_copy(Vh[sl][:, half * 8:(half + 1) * 8, 0:64], ps[b][:, :].rearrange("p (a b) -> p a b", a=8),
                             [("ps", b)], [("Vh", sl, half)])
                items = []
                for i in range(4):
                    g = gcount
                    gcount += 1
                    for kt in range(0, 4 * i + 4):
                        c0 = max(0, 128 * kt - 512 * i)
                        items.append(dict(i=i, kt=kt, c0=c0, n=512 - c0, first=(kt == 0), last=(kt == 4 * i + 3),
                                          diag=(kt >= 4 * i), g=g, p=pcount))
                        pcount += 1

                def c1(it):
                    i, kt, c0, n = it["i"], it["kt"], it["c0"], it["n"]
                    zb = state["PS"].next()
                    it["zb"] = zb
                    mm(ps[zb][:, c0:512], Kh[sl][0:96, kt * 128:(kt + 1) * 128], Qh[sl][0:96, 512 * i + c0:512 * i + 512],
                       True, True, [("Kh", sl, kt // 4), ("Qh", sl, i)], [("ps", zb)])
                    pk = it["p"] % 6
                    act(pbuf[pk][:, 0:n], ps[zb][:, c0:512], AF.Exp, [("ps", zb)], [("pbuf", pk)], scale=scale_a)
                    if it["diag"]:
                        dve_tt(pbuf[pk][:, 0:128], pbuf[pk][:, 0:128], mmla_bf, ALU.mult, [("pbuf", pk), ("cb",)], [("pbuf", pk)])

                def c2(it):
                    i, kt, c0, n, g = it["i"], it["kt"], it["c0"], it["n"], it["g"]
                    pk = it["p"] % 6
                    Ob = 4 + g % 2
                    mm(ps[Ob][:, c0:512], Vh[sl][:, kt, :], pbuf[pk][:, 0:n], it["first"], it["last"],
                       [("Vh", sl, kt // 8), ("Vh1", sl), ("pbuf", pk)], [("ps", Ob)])
                    if it["last"]:
                        S.op("dve", (lambda Ob: lambda e: e.reciprocal(rec[64:128, :], ps[Ob][64:128, :]))(Ob),
                             reads=[("ps", Ob)], writes=[("rec",)])
                        dve_tt(oT[pb:pb + 64, j, 512 * i:512 * i + 512], ps[Ob][0:64, :], rec[64:128, :], ALU.mult,
                               [("ps", Ob), ("rec",)], [("oT", j, i)])

                n_it = len(items)
                LC = 4
                for step in range(n_it + LC):
                    if step < n_it:
                        c1(items[step])
                    if 0 <= step - LC < n_it:
                        c2(items[step - LC])
            S.barrier()
            A.release(mC)
        state["PS"] = Rot(range(8))
        outproj(w_eo_d, oT)
        S.barrier()
        A.release(m0)

    def mix1():
        S.barrier()
        state["PS"] = Rot([0, 1, 2, 3])
        m0 = A.mark()
        oT = A.alloc((8, S_LEN), BF16)
        m1 = A.mark()
        uT = A.alloc((8, S_LEN), BF16)
        wq1 = [A.alloc((8, 384), BF16) for _ in range(2)]
        q1 = [A.alloc((S_LEN,), BF16) for _ in range(2)]
        k1 = [A.alloc((S_LEN,), BF16) for _ in range(2)]
        v1 = [A.alloc((16, 2, 128), BF16) for _ in range(2)]
        btab = [A.alloc((640,), F32) for _ in range(2)]
        expB = [A.alloc((640,), BF16) for _ in range(2)]
        mask1 = A.alloc((640,), F32)
        nmask1 = A.alloc((640,), F32)
        pbuf = [A.alloc((512,), BF16) for _ in range(6)]
        rec = A.alloc((512,), F32)
        dma("sp", mask1[:, :], mask1_d[:, :], [], [("mask1",)])
        dma("sp", nmask1[:, :], nmask1_d[:, :], [], [("nmask1",)])
        for sl in range(2):
            S.op("dve", (lambda sl: lambda e: e.memset(v1[sl][:, :, :, 64:128], 1.0))(sl), reads=[], writes=[("v11", sl)])
        for tb in range(4):
            norm_main(tb, G_MIX1, uT[:, :, tb * 512:(tb + 1) * 512], lambda c, tb=tb: ("uT", c, tb))
        gcount = 0
        pcount = 0
        for jp in range(8):
            sl = jp % 2
            for part in range(3):
                dma("pool", wq1[sl][:, :, part * 128:(part + 1) * 128],
                    w_qkv_d[:, :, part * 1024 + jp * 128:part * 1024 + (jp + 1) * 128], [], [("wq1", sl, part)])
            for tb in range(4):
                ts = slice(tb * 512, (tb + 1) * 512)
                b = state["PS"].next()
                for c in range(8):
                    mm(ps[b][:, :], wq1[sl][:, c, 0:128], uT[:, c, ts], c == 0, c == 7,
                       [("wq1", sl, 0), ("uT", c, tb)], [("ps", b)])
                act_mul(q1[sl][:, ts], ps[b][:, :], 0.125, [("ps", b)], [("q1", sl, tb)])
                b = state["PS"].next()
                for c in range(8):
                    mm(ps[b][:, :], wq1[sl][:, c, 128:256], uT[:, c, ts], c == 0, c == 7,
                       [("wq1", sl, 1), ("uT", c, tb)], [("ps", b)])
                act_copy(k1[sl][:, ts], ps[b][:, :], [("ps", b)], [("k1", sl, tb)])
            for tq in range(4):
                b = state["PS"].next()
                for q in range(4):
                    tile = tq * 4 + q
                    for c in range(8):
                        mm(ps[b][:, q * 128:(q + 1) * 128], uT[:, c, tile * 128:(tile + 1) * 128], wq1[sl][:, c, 256:384],
                           c == 0, c == 7, [("wq1", sl, 2), ("uT", c, tile // 4)], [("ps", b)])
                dve_copy(v1[sl][:, tq * 4:(tq + 1) * 4, :, 0:64],
                         ps[b][:, :].rearrange("p (a h b) -> p a h b", a=4, h=2), [("ps", b)], [("v1", sl, tq)])
            for hh in range(2):
                h = 2 * jp + hh
                bs = hh
                dma("sp", btab[bs][:, :], btab_d[h], [], [("btab", bs)])
                dve_tt(btab[bs][:, :], btab[bs][:, :], mask1[:, :], ALU.mult, [("btab", bs), ("mask1",)], [("btab", bs)])
                dve_tt(expB[bs][:, :], btab[bs][:, :], nmask1[:, :], ALU.add, [("btab", bs), ("nmask1",)], [("expB", bs)])
            if True:
                items = []
                for i in range(4):
                    kts = list(range(max(0, 4 * i - 4), 4 * i + 4))
                    for kt in kts:
                        for hh in range(2):
                            tlo = max(128 * kt, 512 * i)
                            thi = min(128 * kt + 640, 512 * i + 512)
                            items.append(dict(i=i, kt=kt, tlo=tlo, thi=thi, n=thi - tlo, tl0=tlo - 128 * kt, hh=hh,
                                              first=(kt == kts[0]), last=(kt == kts[-1]), g=hh, p=pcount))
                            pcount += 1

                def d1(it):
                    pb = it["hh"] * 64
                    bs = it["hh"]
                    i, kt, tlo, thi, n = it["i"], it["kt"], it["tlo"], it["thi"], it["n"]
                    zb = state["PS"].next()
                    it["zb"] = zb
                    mm(ps[zb][:, 0:n], k1[sl][pb:pb + 64, kt * 128:(kt + 1) * 128], q1[sl][pb:pb + 64, tlo:thi],
                       True, False, [("k1", sl, kt // 4), ("q1", sl, i)], [("ps", zb)])
                    mm(ps[zb][:, 0:n], ident_bf, expB[bs][:, it["tl0"]:it["tl0"] + n],
                       False, True, [("expB", bs), ("cb",)], [("ps", zb)])
                    pk = it["p"] % 6
                    act(pbuf[pk][:, 0:n], ps[zb][:, 0:n], AF.Exp, [("ps", zb)], [("pbuf", pk)])

                def d2(it):
                    hh = it["hh"]
                    pb = hh * 64
                    i, kt, tlo, thi, n, g = it["i"], it["kt"], it["tlo"], it["thi"], it["n"], it["g"]
                    pk = it["p"] % 6
                    Ob = 4 + hh + 2 * (i % 2)
                    mm(ps[Ob][:, tlo - 512 * i:thi - 512 * i], v1[sl][:, kt, hh, :], pbuf[pk][:, 0:n], it["first"], it["last"],
                       [("v1", sl, kt // 4), ("v11", sl), ("pbuf", pk)], [("ps", Ob)])
                    if it["last"]:
                        S.op("dve", (lambda Ob: lambda e: e.reciprocal(rec[64:128, :], ps[Ob][64:128, :]))(Ob),
                             reads=[("ps", Ob)], writes=[("rec",)])
                        dve_tt(oT[pb:pb + 64, jp, 512 * i:512 * i + 512], ps[Ob][0:64, :], rec[64:128, :], ALU.mult,
                               [("ps", Ob), ("rec",)], [("oT", jp, i)])

                n_it = len(items)
                LD = 4
                for step in range(n_it + LD):
                    if step < n_it:
                        d1(items[step])
                    if 0 <= step - LD < n_it:
                        d2(items[step - LD])
        S.barrier()
        A.release(m1)
        state["PS"] = Rot(range(8))
        outproj(w_oo_d, oT)
        S.barrier()
        A.release(m0)

    for st in stages:
        if st == "mix0":
            mix0()
        elif st == "ffn0":
            ffn(0, G_FFN0)
        elif st == "mix1":
            mix1()
        elif st == "ffn1":
            ffn(1, G_FFN1)
    S.barrier()
    state["PS"] = Rot(range(8))
    stage_o = [A.alloc((8, 512), F32) for _ in range(2)]
    for tb in range(4):
        so = stage_o[tb % 2]
        t0 = tb * 512
        if final_norm:
            act(sq[:, :, :], hT[:, :, t0:t0 + 512], AF.Square, [("h", c, tb) for c in range(8)], [("sq",)])
            bank = state["PS"].next()
            for c in range(8):
                mm(ps[bank][:, :], ones_bf, sq[:, c, :], c == 0, c == 7, [("sq",), ("cb",)], [("ps", bank)])
            rstd_from(bank, 1.0 / D)
            for c in range(8):
                S.op("dve", (lambda c, so, t0: lambda e: e.scalar_tensor_tensor(
                    so[:, c, :], hT[:, c, t0:t0 + 512], gvec[:, G_FINAL + c:G_FINAL + c + 1], rstd[:, :],
                    ALU.mult, ALU.mult))(c, so, t0),
                    reads=[("h", c, tb), ("rstd",), ("gvec",)], writes=[("so", tb % 2)])
            tok = dma("sp", outT_v[:, :, t0:t0 + 512], so[:, :, :], [("so", tb % 2)], [("out", tb)])
        else:
            tok = dma("sp", outT_v[:, :, t0:t0 + 512], hT[:, :, t0:t0 + 512], [("h", c, tb) for c in range(8)], [("out", tb)])
        S.final_dma.append(tok)

    S.finalize(nc)
    with nc.Block() as block:
        @block.tensor
        def _(e):
            S.emit("pe", e)

        @block.scalar
        def _(e):
            S.emit("act", e)

        @block.vector
        def _(e):
            S.emit("dve", e)

        @block.gpsimd
        def _(e):
            S.emit("pool", e)

        @block.sync
        def _(e):
            S.emit("sp", e)
    S.close()
    nc._arena_peak = A.peak if False else None
    return nc


def _consts():
    s = np.arange(128)[:, None]
    t = np.arange(128)[None, :]
    cm = np.zeros((128, NCM), np.float32)
    cm[:, C_ONES:C_ONES + 128] = 1.0
    cm[:, C_NEG:C_NEG + 128] = -1.0
    cm[:, C_TRI:C_TRI + 128] = np.where(s >= t, -1.0, 0.0)
    cm[:, C_MSB:C_MSB + 128] = np.where(s < t, 1.0, 0.0)
    cm[:, C_MMLA:C_MMLA + 128] = np.where((s // 64) <= (t // 64), 1.0, 0.0)
    cm[:, C_ID:C_ID + 128] = np.eye(128, dtype=np.float32)
    pos = np.arange(S_LEN, dtype=np.float32)
    inv_freq = (np.float32(10000.0) ** (-np.arange(0, 32, 2, dtype=np.float32) / np.float32(32))).astype(np.float32)
    ang = (pos[:, None] * inv_freq[None, :]).astype(np.float32)
    cos = np.cos(ang).astype(np.float32).T
    sin = np.sin(ang).astype(np.float32).T
    cc = np.concatenate([cos, cos], 0)
    ss = np.concatenate([-sin, sin], 0)
    cc4 = np.ascontiguousarray(np.tile(cc, (4, 1)))
    ss4 = np.ascontiguousarray(np.tile(ss, (4, 1)))
    sl = np.arange(128)[:, None]
    tl = np.arange(640)[None, :]
    d = tl // 64 - sl // 64
    mask1 = ((d >= 0) & (d <= 8)).astype(np.float32)
    bidx = np.clip(tl - sl, -256, 256) + 256
    return cm, cc4, ss4, mask1, bidx


def _gcol(g):
    return np.ascontiguousarray(np.asarray(g, np.float32).reshape(-1, 128).T)


_CACHE = {}


def run(inputs, stages=("mix0", "ffn0", "mix1", "ffn1"), final_norm=True, trace=False):
    f = lambda a: np.ascontiguousarray(np.asarray(a, dtype=np.float32))
    x = f(inputs["x"])
    cm, cc4, ss4, mask1, bidx = _consts()
    gvec = np.concatenate([
        _gcol(inputs["g_mix"][0]), _gcol(inputs["g_ffn"][0]), _gcol(inputs["g_mix"][1]), _gcol(inputs["g_ffn"][1]),
        _gcol(inputs["g_final"]), _gcol(inputs["ev_g_cq"][0]), _gcol(inputs["ev_g_ckv"][0])], axis=1)
    gvec = np.ascontiguousarray(gvec.astype(np.float32))
    assert gvec.shape == (128, NG)
    rb = f(inputs["od_rel_bias"])[0]
    bias_tab = np.ascontiguousarray(rb[:, bidx])
    shared = {
        "ev_w_in": f(inputs["ev_w_in"])[0], "ev_w_uq": f(inputs["ev_w_uq"])[0], "ev_w_ukv": f(inputs["ev_w_ukv"])[0],
        "ev_w_out": f(inputs["ev_w_out"])[0], "od_w_qkv": f(inputs["od_w_qkv"])[0], "od_w_out": f(inputs["od_w_out"])[0],
        "w_gate": f(inputs["w_gate"]), "w_up": f(inputs["w_up"]), "w_down": f(inputs["w_down"]),
        "gvec": gvec, "cmat": cm, "rope_cc": cc4, "rope_ss": ss4, "bias_tab": bias_tab, "mask1": mask1,
        "nmask1": np.ascontiguousarray((mask1 - 1.0) * 30000.0).astype(np.float32),
    }
    key = (tuple(stages), final_norm)
    if key not in _CACHE:
        _CACHE[key] = build(stages, final_norm)
    nc = _CACHE[key]
    in_maps = []
    for b in range(NCORES):
        m = dict(shared)
        m["xT"] = np.ascontiguousarray(x[b].T)
        in_maps.append(m)
    res = run_bass_kernel_spmd(nc, in_maps, core_ids=list(range(NCORES)), **({"trace": True} if trace else {}))
    out = np.stack([np.ascontiguousarray(np.asarray(r["outT"]).T) for r in res.results], axis=0)
    return out.astype(np.float32), res


def kernel(**inputs):
    out, _ = run(inputs)
    return out
```

```python
import numpy as np
import concourse.bass as bass
import concourse.mybir as mybir
from concourse.bass_utils import run_bass_kernel_spmd

F32 = mybir.dt.float32
BF16 = mybir.dt.bfloat16
U8 = mybir.dt.uint8
ALU = mybir.AluOpType
AF = mybir.ActivationFunctionType

import os as _os
_SKIP_B = bool(_os.environ.get("K_SKIP_B"))
N_DUMMY_SB = int(_os.environ.get("K_DUMMY_SB", "3"))
N_DUMMY_M1 = int(_os.environ.get("K_DUMMY_M1", "0"))
_SKIP_C = bool(_os.environ.get("K_SKIP_C"))
S_LEN = 2048
D = 1024
DFF = 2816
EPS = 1e-6
NCORES = 8
ENGS = ("pe", "act", "dve", "pool", "sp")
NDMA = 24
NDMA_POOL = 16
SEM_LIMIT = 3000
NSEM_PER_ENG = 10


DEBUG_NAMES = {}


class Op:
    __slots__ = ("eng", "fn", "deps", "dmadeps", "marked", "sem", "cnt", "idx", "dma", "tag")

    def __init__(self, eng, fn):
        self.eng = eng
        self.fn = fn
        self.deps = {}
        self.dmadeps = {}
        self.marked = False
        self.sem = None
        self.cnt = 0
        self.idx = -1
        self.dma = None


class Sched:
    def __init__(self):
        self.ops = {e: [] for e in ENGS}
        self.last_w = {}
        self.readers = {}
        self.dma_use = [0] * NDMA
        self.dma_rr = 0
        self.dma_rr_pool = 0
        self.pending = {e: [] for e in ENGS}
        self.final_dma = []

    def _add_dep(self, o, tok):
        if tok is None:
            return
        if not isinstance(tok, tuple) and tok.dma is not None:
            tok = tok.dma
        if isinstance(tok, tuple):
            i, v = tok
            if o.dmadeps.get(i, 0) < v:
                o.dmadeps[i] = v
            return
        if tok.eng == o.eng and o.eng in ("pe", "sp"):
            return
        cur = o.deps.get(tok.eng)
        if cur is None or cur.idx < tok.idx:
            o.deps[tok.eng] = tok

    def op(self, eng, fn, reads=(), writes=(), dma=False, raw_same_only=True):
        o = Op(eng, fn)
        o.tag = (eng, tuple(reads), tuple(writes))
        o.idx = len(self.ops[eng])
        for tok in self.pending[eng]:
            self._add_dep(o, tok)
        self.pending[eng] = []
        for k in reads:
            self._add_dep(o, self.last_w.get(k))
        for k in writes:
            self._add_dep(o, self.last_w.get(k))
            for r in self.readers.get(k, ()):
                self._add_dep(o, r)
        tok = o
        if dma:
            if eng == "pool":
                i = self.dma_rr_pool
                self.dma_rr_pool = (self.dma_rr_pool + 1) % NDMA_POOL
            else:
                i = NDMA_POOL + self.dma_rr
                self.dma_rr = (self.dma_rr + 1) % (NDMA - NDMA_POOL)
            if self.dma_use[i] > 0:
                self._add_dep(o, (i, 16 * self.dma_use[i]))
            self.dma_use[i] += 1
            o.dma = (i, 16 * self.dma_use[i])
            tok = o.dma
        for k in reads:
            self.readers.setdefault(k, []).append(tok)
        for k in writes:
            self.last_w[k] = tok
            self.readers[k] = []
        self.ops[eng].append(o)
        return tok

    def barrier(self):
        toks = []
        for e in ENGS:
            if self.ops[e]:
                toks.append(self.ops[e][-1])
        for i in range(NDMA):
            if self.dma_use[i] > 0:
                toks.append((i, 16 * self.dma_use[i]))
        for e in ENGS:
            self.pending[e] = list(toks)
        self.last_w = {}
        self.readers = {}

    def finalize(self, nc):
        for e in ENGS:
            for o in self.ops[e]:
                for d in o.deps.values():
                    d.marked = True
        esems = {}
        self._ctx = []
        for e in ENGS:
            n = sum(1 for o in self.ops[e] if o.marked)
            need = max(1, (n + SEM_LIMIT - 1) // SEM_LIMIT)
            assert need <= NSEM_PER_ENG, (e, n)
            lst = []
            for k in range(need):
                cm = nc.semaphore(f"s_{e}_{k}")
                lst.append(cm.__enter__())
                self._ctx.append(cm)
            esems[e] = lst
            c = 0
            for o in self.ops[e]:
                if o.marked:
                    o.sem = lst[c // SEM_LIMIT]
                    o.cnt = c % SEM_LIMIT + 1
                    c += 1
        dsems = []
        for i in range(NDMA):
            cm = nc.semaphore(f"s_dma_{i}")
            dsems.append(cm.__enter__())
            self._ctx.append(cm)
        self.dsems = dsems

    def emit(self, eng_name, engobj):
        waited = {}
        dwaited = {}
        for o in self.ops[eng_name]:
            for src, d in o.deps.items():
                if waited.get(src, -1) >= d.idx:
                    continue
                engobj.wait_ge(d.sem, d.cnt)
                waited[src] = d.idx
            for i, v in o.dmadeps.items():
                if dwaited.get(i, 0) >= v:
                    continue
                engobj.wait_ge(self.dsems[i], v)
                dwaited[i] = v
            ins = o.fn(engobj)
            try:
                DEBUG_NAMES[ins.ins.name] = o.tag
            except Exception:
                pass
            if o.dma is not None:
                ins.then_inc(self.dsems[o.dma[0]], 16)
            elif o.marked:
                ins.then_inc(o.sem, 1)
        if eng_name == "sp":
            for (i, v) in self.final_dma:
                if dwaited.get(i, 0) < v:
                    engobj.wait_ge(self.dsems[i], v)
                    dwaited[i] = v

    def close(self):
        for cm in reversed(self._ctx):
            cm.__exit__(None, None, None)


class Arena:
    def __init__(self, nc, nbytes):
        self.t = nc.alloc_sbuf_tensor("arena", [128, nbytes], U8)
        self.n = nbytes
        self.off = 0
        self.peak = 0

    def alloc(self, shape, dtype):
        esz = 4 if dtype == F32 else 2
        n = 1
        for s in shape:
            n *= s
        nb = n * esz
        off = (self.off + 63) // 64 * 64
        assert off + nb <= self.n, ("arena overflow", off, nb, self.n)
        self.off = off + nb
        self.peak = max(self.peak, self.off)
        v = self.t[:, off:off + nb].bitcast(dtype)
        if len(shape) == 2:
            v = v.rearrange("p (a b) -> p a b", a=shape[0])
        elif len(shape) == 3:
            v = v.rearrange("p (a b c) -> p a b c", a=shape[0], b=shape[1])
        return v

    def mark(self):
        return self.off

    def release(self, m):
        self.off = m


class Rot:
    def __init__(self, items):
        self.items = list(items)
        self.i = 0

    def next(self):
        v = self.items[self.i % len(self.items)]
        self.i += 1
        return v


G_MIX0, G_FFN0, G_MIX1, G_FFN1, G_FINAL, G_CQ, G_CKV, NG = 0, 8, 16, 24, 32, 40, 43, 45
C_ONES, C_NEG, C_TRI, C_MSB, C_MMLA, C_ID, NCM = 0, 128, 256, 384, 512, 640, 768


def build(stages=("mix0", "ffn0", "mix1", "ffn1"), final_norm=True):
    nc = bass.Bass("TRN2", target_bir_lowering=False)

    def din(name, shape):
        return nc.dram_tensor(name, list(shape), F32, kind="ExternalInput").ap()

    xT_d = din("xT", [D, S_LEN])
    w_in_d = din("ev_w_in", [D, 2208]).rearrange("(c p) n -> p c n", p=128)
    w_uq_d = din("ev_w_uq", [384, 768]).rearrange("(c p) n -> p c n", p=128)
    w_ukv_d = din("ev_w_ukv", [256, 1024]).rearrange("(c p) n -> p c n", p=128)
    w_eo_d = din("ev_w_out", [D, D]).rearrange("(c p) n -> p c n", p=128)
    w_qkv_d = din("od_w_qkv", [D, 3072]).rearrange("(c p) n -> p c n", p=128)
    w_oo_d = din("od_w_out", [D, D]).rearrange("(c p) n -> p c n", p=128)
    w_gate_d = din("w_gate", [2, D, DFF])
    w_up_d = din("w_up", [2, D, DFF])
    w_down_d = din("w_down", [2, DFF, D])
    gvec_d = din("gvec", [128, NG])
    cmat_d = din("cmat", [128, NCM])
    cc_d = din("rope_cc", [128, S_LEN])
    ss_d = din("rope_ss", [128, S_LEN])
    btab_d = din("bias_tab", [16, 128, 640])
    mask1_d = din("mask1", [128, 640])
    nmask1_d = din("nmask1", [128, 640])
    outT_d = nc.dram_tensor("outT", [D, S_LEN], F32, kind="ExternalOutput").ap()
    outT_v = outT_d.rearrange("(c p) t -> p c t", p=128)
    xT_v = xT_d.rearrange("(c p) t -> p c t", p=128)

    S = Sched()
    A = Arena(nc, 211968)
    psall = nc.alloc_psum_tensor("psall", [128, 4096], F32)
    ps = [psall[:, b * 512:(b + 1) * 512] for b in range(8)]

    hT = A.alloc((8, S_LEN), F32)
    cb = A.alloc((NCM,), BF16)
    gvec = A.alloc((NG,), F32)
    sq = A.alloc((8, 512), BF16)
    rstd = A.alloc((512,), F32)
    ones_bf = cb[:, C_ONES:C_ONES + 128]
    neg_bf = cb[:, C_NEG:C_NEG + 128]
    tri_bf = cb[:, C_TRI:C_TRI + 128]
    msb_bf = cb[:, C_MSB:C_MSB + 128]
    mmla_bf = cb[:, C_MMLA:C_MMLA + 128]
    ident_bf = cb[:, C_ID:C_ID + 128]

    PS = Rot(range(8))
    state = {"PS": PS}

    def mm(out, lhsT, rhs, start, stop, reads, writes):
        S.op("pe", lambda e: e.matmul(out, lhsT, rhs, start=start, stop=stop, skip_group_check=True),
             reads=reads, writes=writes)

    def dve_tt(out, in0, in1, op, reads, writes):
        S.op("dve", lambda e: e.tensor_tensor(out, in0, in1, op), reads=reads, writes=writes)

    def dve_tss(out, in_, scalar, op, reads, writes):
        S.op("dve", lambda e: e.tensor_single_scalar(out, in_, scalar, op), reads=reads, writes=writes)

    def dve_copy(out, in_, reads, writes):
        S.op("dve", lambda e: e.tensor_copy(out, in_), reads=reads, writes=writes)

    def act(out, in_, func, reads, writes, bias=0.0, scale=1.0):
        S.op("act", lambda e: e.activation(out, in_, func, bias=bias, scale=scale), reads=reads, writes=writes)

    def act_copy(out, in_, reads, writes):
        S.op("act", lambda e: e.copy(out, in_), reads=reads, writes=writes)

    def act_mul(out, in_, m, reads, writes):
        S.op("act", lambda e: e.mul(out, in_, m), reads=reads, writes=writes)

    def pool_copy(out, in_, reads, writes):
        S.op("pool", lambda e: e.tensor_copy(out, in_), reads=reads, writes=writes)

    def dma(eng, out, in_, reads, writes):
        return S.op(eng, lambda e: e.dma_start(out=out, in_=in_), reads=reads, writes=writes, dma=True)

    dma("pool", cb[:, :], cmat_d[:, :], [], [("cb",)])
    dma("sp", gvec[:, :], gvec_d[:, :], [], [("gvec",)])
    for c in range(8):
        dma("sp", hT[:, c, :], xT_v[:, c, :], [], [("h", c, tb) for tb in range(4)])

    def rstd_from(bank, inv_n):
        S.op("act", lambda e: e.activation(rstd[:, :], ps[bank][:, :], AF.Sqrt, bias=EPS, scale=inv_n),
             reads=[("ps", bank)], writes=[("rstd",)])
        S.op("dve", lambda e: e.reciprocal(rstd[:, :], rstd[:, :]),
             reads=[("rstd",)], writes=[("rstd",)])

    def norm_main(tb, gcol, dst, dkeys):
        t0 = tb * 512
        act(sq[:, :, :], hT[:, :, t0:t0 + 512], AF.Square, [("h", c, tb) for c in range(8)], [("sq",)])
        bank = state["PS"].next()
        for c in range(8):
            mm(ps[bank][:, :], ones_bf, sq[:, c, :], c == 0, c == 7, [("sq",), ("cb",)], [("ps", bank)])
        rstd_from(bank, 1.0 / D)
        for c in range(8):
            S.op("dve", (lambda c: lambda e: e.scalar_tensor_tensor(
                dst[:, c, :], hT[:, c, t0:t0 + 512], gvec[:, gcol + c:gcol + c + 1], rstd[:, :],
                ALU.mult, ALU.mult))(c),
                reads=[("h", c, tb), ("rstd",), ("gvec",)], writes=[dkeys(c)])

    def ffn(l, gcol):
        S.barrier()
        state["PS"] = Rot(range(8))
        m0 = A.mark()
        uThs = [A.alloc((8, 1024), BF16) for _ in range(2)]
        actT = A.alloc((22, 1024), BF16)
        wgu = [A.alloc((2, 8, 256), BF16) for _ in range(3)]
        wdc = [A.alloc((22, 128), BF16) for _ in range(3)]
        sg = [A.alloc((512,), F32) for _ in range(2)]
        wg_v = w_gate_d[l].rearrange("(c p) f -> p c f", p=128)
        wu_v = w_up_d[l].rearrange("(c p) f -> p c f", p=128)
        wd_v = w_down_d[l].rearrange("(c p) d -> p c d", p=128)
        n_w = 0
        n_d = 0
        n_s = 0
        def ffn_norm(th):
            for tb2 in range(2):
                tb = th * 2 + tb2
                norm_main(tb, gcol, uThs[th][:, :, tb2 * 512:(tb2 + 1) * 512], lambda c, tb2=tb2, th=th: ("uTh", th, c, tb2))

        ffn_norm(0)
        for th in range(2):
            uTh = uThs[th]
            for fg in range(11):
                if th == 0 and fg == 3:
                    ffn_norm(1)
                slot = n_w % 3
                n_w += 1
                buf = wgu[slot]
                dma("pool", buf[:, 0, :, :], wg_v[:, :, fg * 256:(fg + 1) * 256], [], [("wgu", slot, 0)])
                dma("pool", buf[:, 1, :, :], wu_v[:, :, fg * 256:(fg + 1) * 256], [], [("wgu", slot, 1)])
                for fc in range(2):
                    f = fg * 2 + fc
                    for tb2 in range(2):
                        ts = slice(tb2 * 512, (tb2 + 1) * 512)
                        bg = state["PS"].next()
                        bu = state["PS"].next()
                        for c in range(8):
                            mm(ps[bg][:, :], buf[:, 0, c, fc * 128:(fc + 1) * 128], uTh[:, c, ts], c == 0, c == 7,
                               [("wgu", slot, 0), ("uTh", th, c, tb2)], [("ps", bg)])
                        for c in range(8):
                            mm(ps[bu][:, :], buf[:, 1, c, fc * 128:(fc + 1) * 128], uTh[:, c, ts], c == 0, c == 7,
                               [("wgu", slot, 1), ("uTh", th, c, tb2)], [("ps", bu)])
                        sgb = sg[n_s % 2]
                        sk = ("sg", n_s % 2)
                        n_s += 1
                        act(sgb[:, :], ps[bg][:, :], AF.Silu, [("ps", bg)], [sk])
                        dve_tt(actT[:, f, ts], ps[bu][:, :], sgb[:, :], ALU.mult, [("ps", bu), sk], [("actT", f, tb2)])
            for dc in range(8):
                slot = n_d % 3
                n_d += 1
                wb = wdc[slot]
                dma("pool", wb[:, 0:11, :], wd_v[:, 0:11, dc * 128:(dc + 1) * 128], [], [("wdc", slot)])
                dma("pool", wb[:, 11:22, :], wd_v[:, 11:22, dc * 128:(dc + 1) * 128], [], [("wdc", slot)])
                for tb2 in range(2):
                    tb = th * 2 + tb2
                    ts = slice(tb2 * 512, (tb2 + 1) * 512)
                    b = state["PS"].next()
                    for f in range(22):
                        mm(ps[b][:, :], wb[:, f, :], actT[:, f, ts], f == 0, f == 21,
                           [("wdc", slot), ("actT", f, tb2)], [("ps", b)])
                    hs = hT[:, dc, tb * 512:(tb + 1) * 512]
                    dve_tt(hs, ps[b][:, :], hs, ALU.add, [("ps", b), ("h", dc, tb)], [("h", dc, tb)])
        S.barrier()
        A.release(m0)

    def outproj(w_d, oT):
        wo = A.alloc((8, 1024), BF16)
        for half in range(2):
            dma("pool", wo[:, :, half * 512:(half + 1) * 512], w_d[:, :, half * 512:(half + 1) * 512], [], [("wo", half)])
        for dc in range(8):
            for tb in range(4):
                b = state["PS"].next()
                for kc in range(8):
                    mm(ps[b][:, :], wo[:, kc, dc * 128:(dc + 1) * 128], oT[:, kc, tb * 512:(tb + 1) * 512],
                       kc == 0, kc == 7, [("wo", dc // 4), ("oT", kc, tb)], [("ps", b)])
                hs = hT[:, dc, tb * 512:(tb + 1) * 512]
                dve_tt(hs, ps[b][:, :], hs, ALU.add, [("ps", b), ("h", dc, tb)], [("h", dc, tb)])

    def mix0():
        S.barrier()
        state["PS"] = Rot(range(8))
        m0 = A.mark()
        cqn = A.alloc((3, S_LEN), BF16)
        ckvn = A.alloc((2, S_LEN), BF16)
        krope = A.alloc((S_LEN,), BF16)
        oT = A.alloc((8, S_LEN), BF16)
        mU = A.mark()
        uT = A.alloc((8, S_LEN), BF16)
        mA = A.mark()
        wA = A.alloc((8, 704), BF16)
        CCb = [A.alloc((512,), F32) for _ in range(2)]
        SSb = [A.alloc((512,), F32) for _ in range(2)]
        t1 = A.alloc((512,), F32)
        t2 = A.alloc((512,), F32)
        dma("pool", wA[:, :, 0:672], w_in_d[:, :, 0:672], [], [("wA", 0)])
        dma("pool", wA[:, :, 672:688], w_in_d[:, :, 656:672], [], [("wA", 1)])
        dma("pool", wA[:, :, 688:704], w_in_d[:, :, 640:656], [], [("wA", 2)])
        norm_main(0, G_MIX0, uT[:, :, 0:512], lambda c: ("uT", c, 0))
        for tb in range(4):
            ts = slice(tb * 512, (tb + 1) * 512)
            if tb + 1 < 4:
                norm_main(tb + 1, G_MIX0, uT[:, :, (tb + 1) * 512:(tb + 2) * 512], lambda c, tb=tb: ("uT", c, tb + 1))
            for (nch, col0, gc, dst, dname, invn) in ((3, 0, G_CQ, cqn, "cqn", 1.0 / 384),
                                                       (2, 384, G_CKV, ckvn, "ckvn", 1.0 / 256)):
                banks = []
                for j in range(nch):
                    b = state["PS"].next()
                    banks.append(b)
                    for c in range(8):
                        mm(ps[b][:, :], wA[:, c, col0 + j * 128:col0 + (j + 1) * 128], uT[:, c, ts], c == 0, c == 7,
                           [("wA", 0), ("uT", c, tb)], [("ps", b)])
                    act(sq[:, j, :], ps[b][:, :], AF.Square, [("ps", b)], [("sq",)])
                bs = state["PS"].next()
                for j in range(nch):
                    mm(ps[bs][:, :], ones_bf, sq[:, j, :], j == 0, j == nch - 1, [("sq",), ("cb",)], [("ps", bs)])
                rstd_from(bs, invn)
                for j in range(nch):
                    b = banks[j]
                    S.op("dve", (lambda j, b, dst, gc, ts: lambda e: e.scalar_tensor_tensor(
                        dst[:, j, ts], ps[b][:, :], gvec[:, gc + j:gc + j + 1], rstd[:, :], ALU.mult, ALU.mult))(j, b, dst, gc, ts),
                        reads=[("ps", b), ("rstd",), ("gvec",)], writes=[(dname, j, tb)])
            b = state["PS"].next()
            for c in range(8):
                mm(ps[b][0:64, :], wA[:, c, 640:704], uT[:, c, ts], c == 0, c == 7,
                   [("wA", 0), ("wA", 1), ("wA", 2), ("uT", c, tb)], [("ps", b)])
            cc = CCb[tb % 2]
            ss = SSb[tb % 2]
            dma("sp", cc[:, :], cc_d[:, ts], [], [("CCb", tb % 2)])
            dma("sp", ss[:, :], ss_d[:, ts], [], [("SSb", tb % 2)])
            dve_tt(t1[0:32, :], ps[b][0:32, :], cc[0:32, :], ALU.mult, [("ps", b), ("CCb", tb % 2)], [("t1",)])
            dve_tt(t2[0:32, :], ps[b][32:64, :], ss[32:64, :], ALU.mult, [("ps", b), ("SSb", tb % 2)], [("t2",)])
            dve_tt(krope[0:32, ts], t1[0:32, :], t2[0:32, :], ALU.add, [("t1",), ("t2",)], [("krope", tb)])
        S.barrier()
        A.release(mA)

        if not _SKIP_B:
            mB = A.mark()
            state["PS"] = Rot([7])
            wB = A.alloc((8, 384), BF16)
            qb = A.alloc((S_LEN,), BF16)
            kb = A.alloc((S_LEN,), BF16)
            vb = A.alloc((16, 128), BF16)
            ebuf = [A.alloc((2, 512), F32) for _ in range(1)]
            spb = [A.alloc((2, 512), BF16) for _ in range(3)]
            wbuf = [A.alloc((2, 512), BF16) for _ in range(3)]
            sacc = A.alloc((2, 512), F32)
            saccb = [A.alloc((2, 512), BF16) for _ in range(2)]
            ZZ = [psall[:, q * 1024:(q + 1) * 1024].rearrange("p (a c) -> p a c", a=2) for q in range(3)]
            RR = psall[:, 2048:3072].rearrange("p (a c) -> p a c", a=2)
            pcount = 0
            for j in range(4):
                dma("pool", wB[:, :, 0:128], w_in_d[:, :, 672 + j * 128:672 + (j + 1) * 128], [], [("wB", 0)])
                dma("pool", wB[:, :, 128:256], w_in_d[:, :, 1184 + j * 128:1184 + (j + 1) * 128], [], [("wB", 1)])
                dma("pool", wB[:, :, 256:384], w_in_d[:, :, 1696 + j * 128:1696 + (j + 1) * 128], [], [("wB", 2)])
                for tb in range(4):
                    ts = slice(tb * 512, (tb + 1) * 512)
                    b = 7
                    for c in range(8):
                        mm(ps[b][:, :], wB[:, c, 0:128], uT[:, c, ts], c == 0, c == 7,
                           [("wB", 0), ("uT", c, tb)], [("ps", b)])
                    dve_tss(qb[:, ts], ps[b][:, :], 0.125, ALU.mult, [("ps", b)], [("qb", tb)])
                    b = 7
                    for c in range(8):
                        mm(ps[b][:, :], wB[:, c, 128:256], uT[:, c, ts], c == 0, c == 7,
                           [("wB", 1), ("uT", c, tb)], [("ps", b)])
                    dve_copy(kb[:, ts], ps[b][:, :], [("ps", b)], [("kb", tb)])
                for tq in range(4):
                    b = 7
                    for q in range(4):
                        tile = tq * 4 + q
                        for c in range(8):
                            mm(ps[b][:, q * 128:(q + 1) * 128], uT[:, c, tile * 128:(tile + 1) * 128], wB[:, c, 256:384],
                               c == 0, c == 7, [("wB", 2), ("uT", c, tile // 4)], [("ps", b)])
                    dve_copy(vb[:, tq * 4:(tq + 1) * 4, :], ps[b][:, :].rearrange("p (a b) -> p a b", a=4),
                             [("ps", b)], [("vb", tq)])
                items = []
                for i in range(4):
                    for kt in range(4 * i + 3, -1, -1):
                        c0 = max(0, 128 * kt - 512 * i)
                        c0p = max(0, 128 * (kt + 1) - 512 * i)
                        items.append(dict(i=i, kt=kt, c0=c0, c0p=c0p, n=512 - c0, first=(kt == 4 * i + 3),
                                          last=(kt == 0), diag=(kt >= 4 * i), p=pcount))
                        pcount += 1

                def s1(it):
                    i, kt, c0, n = it["i"], it["kt"], it["c0"], it["n"]
                    zq = it["p"] % 3
                    zk = [("ps", 2 * zq), ("ps", 2 * zq + 1)]
                    for hh in range(2):
                        pb = hh * 64
                        mm(ps[2 * zq + hh][:, c0:512], kb[pb:pb + 64, kt * 128:(kt + 1) * 128],
                           qb[pb:pb + 64, 512 * i + c0:512 * i + 512], True, True,
                           [("kb", kt // 4), ("qb", i)], [("ps", 2 * zq + hh)])
                    ek = 0
                    sk = it["p"] % 3
                    act(ebuf[ek][:, :, 0:n], ZZ[zq][:, :, c0:512], AF.Exp, zk, [("ebuf", ek)])
                    act(spb[sk][:, :, 0:n], ebuf[ek][:, :, 0:n], AF.Ln, [("ebuf", ek)], [("spb", sk)], bias=1.0)
                    if it["diag"]:
                        for hh in range(2):
                            dve_tt(spb[sk][:, hh, 0:128], spb[sk][:, hh, 0:128], msb_bf, ALU.mult,
                                   [("spb", sk), ("cb",)], [("spb", sk)])

                def s2(it):
                    i, kt, c0, n = it["i"], it["kt"], it["c0"], it["n"]
                    zq = it["p"] % 3
                    zk = [("ps", 2 * zq), ("ps", 2 * zq + 1)]
                    sk3 = it["p"] % 3
                    sk = it["p"] % 3
                    pp = it["p"] % 2
                    for hh in range(2):
                        mm(ps[2 * zq + hh][:, c0:512], tri_bf, spb[sk3][:, hh, 0:n], False, it["first"],
                           [("spb", sk3), ("cb",)], [("ps", 2 * zq + hh)])
                    if not it["first"]:
                        for hh in range(2):
                            mm(ps[2 * zq + hh][:, c0:512], neg_bf, saccb[pp][:, hh, c0:512], False, True,
                               [("saccb", pp), ("cb",)], [("ps", 2 * zq + hh)])
                    act(wbuf[sk][:, :, 0:n], ZZ[zq][:, :, c0:512], AF.Exp, zk, [("wbuf", sk)])
                    for _d in range(N_DUMMY_SB):
                        mm(ps[7][:, :], ones_bf, uT[:, _d, 0:512], True, True, [("cb",)], [("ps", 7)])
                    if it["diag"]:
                        for hh in range(2):
                            dve_tt(wbuf[sk][:, hh, 0:128], wbuf[sk][:, hh, 0:128], msb_bf, ALU.mult,
                                   [("wbuf", sk), ("cb",)], [("wbuf", sk)])
                    if not it["last"]:
                        if it["first"]:
                            S.op("dve", lambda e: e.memset(sacc[:, :, :], 0.0), reads=[], writes=[("sacc",)])
                        dve_tt(sacc[:, :, c0:512], sacc[:, :, c0:512], spb[sk3][:, :, 0:n], ALU.add,
                               [("sacc",), ("spb", sk3)], [("sacc",)])
                        dve_copy(saccb[1 - pp][:, :, :], sacc[:, :, :], [("sacc",)], [("saccb", 1 - pp)])

                def s3(it):
                    i, kt, c0, n = it["i"], it["kt"], it["c0"], it["n"]
                    sk = it["p"] % 3
                    Ob = 6
                    for hh in range(2):
                        pb = hh * 64
                        mm(ps[Ob][pb:pb + 64, c0:512], vb[:, kt, pb:pb + 64], wbuf[sk][:, hh, 0:n], it["first"], it["last"],
                           [("vb", kt // 4), ("wbuf", sk)], [("ps", Ob)])
                    if it["last"]:
                        dve_copy(oT[:, 4 + j, 512 * i:512 * i + 512], ps[Ob][:, :], [("ps", Ob)], [("oT", 4 + j, i)])

                n_it = len(items)
                L1, L2 = 1, 3
                for step in range(n_it + L2):
                    if step < n_it:
                        s1(items[step])
                    if 0 <= step - L1 < n_it:
                        s2(items[step - L1])
                    if 0 <= step - L2 < n_it:
                        s3(items[step - L2])
            S.barrier()
            A.release(mU)

        if not _SKIP_C:
            mC = A.mark()
            state["PS"] = Rot([0, 1, 2, 3, 6, 7])
            wuq = A.alloc((3, 768), BF16)
            wsw = A.alloc((3, 8, 32), BF16)
            wukv = A.alloc((2, 1024), BF16)
            CC = A.alloc((S_LEN,), F32)
            SS = A.alloc((S_LEN,), F32)
            Qh = [A.alloc((S_LEN,), BF16) for _ in range(2)]
            Kh = [A.alloc((S_LEN,), BF16) for _ in range(2)]
            Vh = [A.alloc((16, 128), BF16) for _ in range(2)]
            pbuf = [A.alloc((512,), BF16) for _ in range(6)]
            rec = A.alloc((512,), F32)
            t1 = A.alloc((512,), F32)
            t2 = A.alloc((512,), F32)
            dma("pool", wuq[:, :, :], w_uq_d[:, :, :], [], [("wuq",)])
            wuq4 = w_uq_d.rearrange("p c (h f) -> p c h f", h=8)
            for kc in range(3):
                dma("pool", wsw[:, kc, :, 0:16], wuq4[:, kc, :, 80:96], [], [("wsw", 0)])
                dma("pool", wsw[:, kc, :, 16:32], wuq4[:, kc, :, 64:80], [], [("wsw", 1)])
            dma("pool", wukv[:, :, :], w_ukv_d[:, :, :], [], [("wukv",)])
            dma("sp", CC[:, :], cc_d[:, :], [], [("CC",)])
            dma("sp", SS[:, :], ss_d[:, :], [], [("SS",)])
            for sl in range(2):
                S.op("dve", (lambda sl: lambda e: e.memset(Vh[sl][:, :, 64:128], 1.0))(sl), reads=[], writes=[("Vh1", sl)])
            gcount = 0
            pcount = 0
            scale_a = 96.0 ** -0.5
            for h in range(8):
                sl = h % 2
                j = h // 2
                pb = (h % 2) * 64
                for tb in range(4):
                    ts = slice(tb * 512, (tb + 1) * 512)
                    bA = state["PS"].next()
                    for kc in range(3):
                        mm(ps[bA][0:96, :], wuq[:, kc, h * 96:(h + 1) * 96], cqn[:, kc, ts], kc == 0, kc == 2,
                           [("wuq",), ("cqn", kc, tb)], [("ps", bA)])
                    bB = state["PS"].next()
                    for kc in range(3):
                        mm(ps[bB][64:96, :], wsw[:, kc, h, :], cqn[:, kc, ts], kc == 0, kc == 2,
                           [("wsw", 0), ("wsw", 1), ("cqn", kc, tb)], [("ps", bB)])
                    act_copy(Qh[sl][0:64, ts], ps[bA][0:64, :], [("ps", bA)], [("Qh", sl, tb)])
                    dve_tt(t1[64:96, :], ps[bA][64:96, :], CC[64:96, ts], ALU.mult, [("ps", bA), ("CC",)], [("t1",)])
                    dve_tt(t2[64:96, :], ps[bB][64:96, :], SS[64:96, ts], ALU.mult, [("ps", bB), ("SS",)], [("t2",)])
                    dve_tt(Qh[sl][64:96, ts], t1[64:96, :], t2[64:96, :], ALU.add, [("t1",), ("t2",)], [("Qh", sl, tb)])
                    bK = state["PS"].next()
                    for kc in range(2):
                        mm(ps[bK][0:64, :], wukv[:, kc, h * 128:h * 128 + 64], ckvn[:, kc, ts], kc == 0, kc == 1,
                           [("wukv",), ("ckvn", kc, tb)], [("ps", bK)])
                    act_copy(Kh[sl][0:64, ts], ps[bK][0:64, :], [("ps", bK)], [("Kh", sl, tb)])
                    pool_copy(Kh[sl][64:96, ts], krope[0:32, ts], [("krope", tb)], [("Kh", sl, tb)])
                for half in range(2):
                    b = state["PS"].next()
                    for q in range(8):
                        tile = half * 8 + q
                        for kc in range(2):
                            mm(ps[b][:, q * 64:(q + 1) * 64], ckvn[:, kc, tile * 128:(tile + 1) * 128],
                               wukv[:, kc, h * 128 + 64:h * 128 + 128], kc == 0, kc == 1,
                               [("wukv",), ("ckvn", kc, tile // 4)], [("ps", b)])
                    dve_copy(Vh[sl][:, half * 8:(half + 1) * 8, 0:64], ps[b][:, :].rearrange("p (a b) -> p a b", a=8),
                             [("ps", b)], [("Vh", sl, half)])
                items = []
                for i in range(4):
                    g = gcount
                    gcount += 1
                    for kt in range(0, 4 * i + 4):
                        c0 = max(0, 128 * kt - 512 * i)
                        items.append(dict(i=i, kt=kt, c0=c0, n=512 - c0, first=(kt == 0), last=(kt == 4 * i + 3),
                                          diag=(kt >= 4 * i), g=g, p=pcount))
                        pcount += 1

                def c1(it):
                    i, kt, c0, n = it["i"], it["kt"], it["c0"], it["n"]
                    zb = state["PS"].next()
                    it["zb"] = zb
                    mm(ps[zb][:, c0:512], Kh[sl][0:96, kt * 128:(kt + 1) * 128], Qh[sl][0:96, 512 * i + c0:512 * i + 512],
                       True, True, [("Kh", sl, kt // 4), ("Qh", sl, i)], [("ps", zb)])
                    pk = it["p"] % 6
                    act(pbuf[pk][:, 0:n], ps[zb][:, c0:512], AF.Exp, [("ps", zb)], [("pbuf", pk)], scale=scale_a)
                    if it["diag"]:
                        dve_tt(pbuf[pk][:, 0:128], pbuf[pk][:, 0:128], mmla_bf, ALU.mult, [("pbuf", pk), ("cb",)], [("pbuf", pk)])

                def c2(it):
                    i, kt, c0, n, g = it["i"], it["kt"], it["c0"], it["n"], it["g"]
                    pk = it["p"] % 6
                    Ob = 4 + g % 2
                    mm(ps[Ob][:, c0:512], Vh[sl][:, kt, :], pbuf[pk][:, 0:n], it["first"], it["last"],
                       [("Vh", sl, kt // 8), ("Vh1", sl), ("pbuf", pk)], [("ps", Ob)])
                    if it["last"]:
                        S.op("dve", (lambda Ob: lambda e: e.reciprocal(rec[64:128, :], ps[Ob][64:128, :]))(Ob),
                             reads=[("ps", Ob)], writes=[("rec",)])
                        dve_tt(oT[pb:pb + 64, j, 512 * i:512 * i + 512], ps[Ob][0:64, :], rec[64:128, :], ALU.mult,
                               [("ps", Ob), ("rec",)], [("oT", j, i)])

                n_it = len(items)
                LC = 4
                for step in range(n_it + LC):
                    if step < n_it:
                        c1(items[step])
                    if 0 <= step - LC < n_it:
                        c2(items[step - LC])
            S.barrier()
            A.release(mC)
        state["PS"] = Rot(range(8))
        outproj(w_eo_d, oT)
        S.barrier()
        A.release(m0)

    def mix1():
        S.barrier()
        state["PS"] = Rot([0, 1, 2, 3])
        m0 = A.mark()
        oT = A.alloc((8, S_LEN), BF16)
        m1 = A.mark()
        uT = A.alloc((8, S_LEN), BF16)
        wq1 = [A.alloc((8, 384), BF16) for _ in range(2)]
        q1 = [A.alloc((S_LEN,), BF16) for _ in range(2)]
        k1 = [A.alloc((S_LEN,), BF16) for _ in range(2)]
        v1 = [A.alloc((16, 2, 128), BF16) for _ in range(2)]
        btab = [A.alloc((640,), F32) for _ in range(2)]
        expB = [A.alloc((640,), BF16) for _ in range(2)]
        mask1 = A.alloc((640,), F32)
        nmask1 = A.alloc((640,), F32)
        pbuf = [A.alloc((512,), BF16) for _ in range(6)]
        rec = A.alloc((512,), F32)
        dma("sp", mask1[:, :], mask1_d[:, :], [], [("mask1",)])
        dma("sp", nmask1[:, :], nmask1_d[:, :], [], [("nmask1",)])
        for sl in range(2):
            S.op("dve", (lambda sl: lambda e: e.memset(v1[sl][:, :, :, 64:128], 1.0))(sl), reads=[], writes=[("v11", sl)])
        for tb in range(4):
            norm_main(tb, G_MIX1, uT[:, :, tb * 512:(tb + 1) * 512], lambda c, tb=tb: ("uT", c, tb))
        gcount = 0
        pcount = 0
        for jp in range(8):
            sl = jp % 2
            for part in range(3):
                dma("pool", wq1[sl][:, :, part * 128:(part + 1) * 128],
                    w_qkv_d[:, :, part * 1024 + jp * 128:part * 1024 + (jp + 1) * 128], [], [("wq1", sl, part)])
            for tb in range(4):
                ts = slice(tb * 512, (tb + 1) * 512)
                b = state["PS"].next()
                for c in range(8):
                    mm(ps[b][:, :], wq1[sl][:, c, 0:128], uT[:, c, ts], c == 0, c == 7,
                       [("wq1", sl, 0), ("uT", c, tb)], [("ps", b)])
                act_mul(q1[sl][:, ts], ps[b][:, :], 0.125, [("ps", b)], [("q1", sl, tb)])
                b = state["PS"].next()
                for c in range(8):
                    mm(ps[b][:, :], wq1[sl][:, c, 128:256], uT[:, c, ts], c == 0, c == 7,
                       [("wq1", sl, 1), ("uT", c, tb)], [("ps", b)])
                act_copy(k1[sl][:, ts], ps[b][:, :], [("ps", b)], [("k1", sl, tb)])
            for tq in range(4):
                b = state["PS"].next()
                for q in range(4):
                    tile = tq * 4 + q
                    for c in range(8):
                        mm(ps[b][:, q * 128:(q + 1) * 128], uT[:, c, tile * 128:(tile + 1) * 128], wq1[sl][:, c, 256:384],
                           c == 0, c == 7, [("wq1", sl, 2), ("uT", c, tile // 4)], [("ps", b)])
                dve_copy(v1[sl][:, tq * 4:(tq + 1) * 4, :, 0:64],
                         ps[b][:, :].rearrange("p (a h b) -> p a h b", a=4, h=2), [("ps", b)], [("v1", sl, tq)])
            for hh in range(2):
                h = 2 * jp + hh
                bs = hh
                dma("sp", btab[bs][:, :], btab_d[h], [], [("btab", bs)])
                dve_tt(btab[bs][:, :], btab[bs][:, :], mask1[:, :], ALU.mult, [("btab", bs), ("mask1",)], [("btab", bs)])
                dve_tt(expB[bs][:, :], btab[bs][:, :], nmask1[:, :], ALU.add, [("btab", bs), ("nmask1",)], [("expB", bs)])
            if True:
                items = []
                for i in range(4):
                    kts = list(range(max(0, 4 * i - 4), 4 * i + 4))
                    for kt in kts:
                        for hh in range(2):
                            tlo = max(128 * kt, 512 * i)
                            thi = min(128 * kt + 640, 512 * i + 512)
                            items.append(dict(i=i, kt=kt, tlo=tlo, thi=thi, n=thi - tlo, tl0=tlo - 128 * kt, hh=hh,
                                              first=(kt == kts[0]), last=(kt == kts[-1]), g=hh, p=pcount))
                            pcount += 1

                def d1(it):
                    pb = it["hh"] * 64
                    bs = it["hh"]
                    i, kt, tlo, thi, n = it["i"], it["kt"], it["tlo"], it["thi"], it["n"]
                    zb = state["PS"].next()
                    it["zb"] = zb
                    mm(ps[zb][:, 0:n], k1[sl][pb:pb + 64, kt * 128:(kt + 1) * 128], q1[sl][pb:pb + 64, tlo:thi],
                       True, False, [("k1", sl, kt // 4), ("q1", sl, i)], [("ps", zb)])
                    mm(ps[zb][:, 0:n], ident_bf, expB[bs][:, it["tl0"]:it["tl0"] + n],
                       False, True, [("expB", bs), ("cb",)], [("ps", zb)])
                    pk = it["p"] % 6
                    act(pbuf[pk][:, 0:n], ps[zb][:, 0:n], AF.Exp, [("ps", zb)], [("pbuf", pk)])

                def d2(it):
                    hh = it["hh"]
                    pb = hh * 64
                    i, kt, tlo, thi, n, g = it["i"], it["kt"], it["tlo"], it["thi"], it["n"], it["g"]
                    pk = it["p"] % 6
                    Ob = 4 + hh + 2 * (i % 2)
                    mm(ps[Ob][:, tlo - 512 * i:thi - 512 * i], v1[sl][:, kt, hh, :], pbuf[pk][:, 0:n], it["first"], it["last"],
                       [("v1", sl, kt // 4), ("v11", sl), ("pbuf", pk)], [("ps", Ob)])
                    if it["last"]:
                        S.op("dve", (lambda Ob: lambda e: e.reciprocal(rec[64:128, :], ps[Ob][64:128, :]))(Ob),
                             reads=[("ps", Ob)], writes=[("rec",)])
                        dve_tt(oT[pb:pb + 64, jp, 512 * i:512 * i + 512], ps[Ob][0:64, :], rec[64:128, :], ALU.mult,
                               [("ps", Ob), ("rec",)], [("oT", jp, i)])

                n_it = len(items)
                LD = 4
                for step in range(n_it + LD):
                    if step < n_it:
                        d1(items[step])
                    if 0 <= step - LD < n_it:
                        d2(items[step - LD])
        S.barrier()
        A.release(m1)
        state["PS"] = Rot(range(8))
        outproj(w_oo_d, oT)
        S.barrier()
        A.release(m0)

    for st in stages:
        if st == "mix0":
            mix0()
        elif st == "ffn0":
            ffn(0, G_FFN0)
        elif st == "mix1":
            mix1()
        elif st == "ffn1":
            ffn(1, G_FFN1)
    S.barrier()
    state["PS"] = Rot(range(8))
    stage_o = [A.alloc((8, 512), F32) for _ in range(2)]
    for tb in range(4):
        so = stage_o[tb % 2]
        t0 = tb * 512
        if final_norm:
            act(sq[:, :, :], hT[:, :, t0:t0 + 512], AF.Square, [("h", c, tb) for c in range(8)], [("sq",)])
            bank = state["PS"].next()
            for c in range(8):
                mm(ps[bank][:, :], ones_bf, sq[:, c, :], c == 0, c == 7, [("sq",), ("cb",)], [("ps", bank)])
            rstd_from(bank, 1.0 / D)
            for c in range(8):
                S.op("dve", (lambda c, so, t0: lambda e: e.scalar_tensor_tensor(
                    so[:, c, :], hT[:, c, t0:t0 + 512], gvec[:, G_FINAL + c:G_FINAL + c + 1], rstd[:, :],
                    ALU.mult, ALU.mult))(c, so, t0),
                    reads=[("h", c, tb), ("rstd",), ("gvec",)], writes=[("so", tb % 2)])
            tok = dma("sp", outT_v[:, :, t0:t0 + 512], so[:, :, :], [("so", tb % 2)], [("out", tb)])
        else:
            tok = dma("sp", outT_v[:, :, t0:t0 + 512], hT[:, :, t0:t0 + 512], [("h", c, tb) for c in range(8)], [("out", tb)])
        S.final_dma.append(tok)

    S.finalize(nc)
    with nc.Block() as block:
        @block.tensor
        def _(e):
            S.emit("pe", e)

        @block.scalar
        def _(e):
            S.emit("act", e)

        @block.vector
        def _(e):
            S.emit("dve", e)

        @block.gpsimd
        def _(e):
            S.emit("pool", e)

        @block.sync
        def _(e):
            S.emit("sp", e)
    S.close()
    nc._arena_peak = A.peak if False else None
    return nc


def _consts():
    s = np.arange(128)[:, None]
    t = np.arange(128)[None, :]
    cm = np.zeros((128, NCM), np.float32)
    cm[:, C_ONES:C_ONES + 128] = 1.0
    cm[:, C_NEG:C_NEG + 128] = -1.0
    cm[:, C_TRI:C_TRI + 128] = np.where(s >= t, -1.0, 0.0)
    cm[:, C_MSB:C_MSB + 128] = np.where(s < t, 1.0, 0.0)
    cm[:, C_MMLA:C_MMLA + 128] = np.where((s // 64) <= (t // 64), 1.0, 0.0)
    cm[:, C_ID:C_ID + 128] = np.eye(128, dtype=np.float32)
    pos = np.arange(S_LEN, dtype=np.float32)
    inv_freq = (np.float32(10000.0) ** (-np.arange(0, 32, 2, dtype=np.float32) / np.float32(32))).astype(np.float32)
    ang = (pos[:, None] * inv_freq[None, :]).astype(np.float32)
    cos = np.cos(ang).astype(np.float32).T
    sin = np.sin(ang).astype(np.float32).T
    cc = np.concatenate([cos, cos], 0)
    ss = np.concatenate([-sin, sin], 0)
    cc4 = np.ascontiguousarray(np.tile(cc, (4, 1)))
    ss4 = np.ascontiguousarray(np.tile(ss, (4, 1)))
    sl = np.arange(128)[:, None]
    tl = np.arange(640)[None, :]
    d = tl // 64 - sl // 64
    mask1 = ((d >= 0) & (d <= 8)).astype(np.float32)
    bidx = np.clip(tl - sl, -256, 256) + 256
    return cm, cc4, ss4, mask1, bidx


def _gcol(g):
    return np.ascontiguousarray(np.asarray(g, np.float32).reshape(-1, 128).T)


_CACHE = {}


def run(inputs, stages=("mix0", "ffn0", "mix1", "ffn1"), final_norm=True, trace=False):
    f = lambda a: np.ascontiguousarray(np.asarray(a, dtype=np.float32))
    x = f(inputs["x"])
    cm, cc4, ss4, mask1, bidx = _consts()
    gvec = np.concatenate([
        _gcol(inputs["g_mix"][0]), _gcol(inputs["g_ffn"][0]), _gcol(inputs["g_mix"][1]), _gcol(inputs["g_ffn"][1]),
        _gcol(inputs["g_final"]), _gcol(inputs["ev_g_cq"][0]), _gcol(inputs["ev_g_ckv"][0])], axis=1)
    gvec = np.ascontiguousarray(gvec.astype(np.float32))
    assert gvec.shape == (128, NG)
    rb = f(inputs["od_rel_bias"])[0]
    bias_tab = np.ascontiguousarray(rb[:, bidx])
    shared = {
        "ev_w_in": f(inputs["ev_w_in"])[0], "ev_w_uq": f(inputs["ev_w_uq"])[0], "ev_w_ukv": f(inputs["ev_w_ukv"])[0],
        "ev_w_out": f(inputs["ev_w_out"])[0], "od_w_qkv": f(inputs["od_w_qkv"])[0], "od_w_out": f(inputs["od_w_out"])[0],
        "w_gate": f(inputs["w_gate"]), "w_up": f(inputs["w_up"]), "w_down": f(inputs["w_down"]),
        "gvec": gvec, "cmat": cm, "rope_cc": cc4, "rope_ss": ss4, "bias_tab": bias_tab, "mask1": mask1,
        "nmask1": np.ascontiguousarray((mask1 - 1.0) * 30000.0).astype(np.float32),
    }
    key = (tuple(stages), final_norm)
    if key not in _CACHE:
        _CACHE[key] = build(stages, final_norm)
    nc = _CACHE[key]
    in_maps = []
    for b in range(NCORES):
        m = dict(shared)
        m["xT"] = np.ascontiguousarray(x[b].T)
        in_maps.append(m)
    res = run_bass_kernel_spmd(nc, in_maps, core_ids=list(range(NCORES)), **({"trace": True} if trace else {}))
    out = np.stack([np.ascontiguousarray(np.asarray(r["outT"]).T) for r in res.results], axis=0)
    return out.astype(np.float32), res


def kernel(**inputs):
    out, _ = run(inputs)
    return out
```

```python
import numpy as np
import concourse.bass as bass
import concourse.mybir as mybir
from concourse.bass_utils import run_bass_kernel_spmd

F32 = mybir.dt.float32
BF16 = mybir.dt.bfloat16
U8 = mybir.dt.uint8
ALU = mybir.AluOpType
AF = mybir.ActivationFunctionType

import os as _os
_SKIP_B = bool(_os.environ.get("K_SKIP_B"))
N_DUMMY_SB = int(_os.environ.get("K_DUMMY_SB", "3"))
N_DUMMY_M1 = int(_os.environ.get("K_DUMMY_M1", "0"))
_SKIP_C = bool(_os.environ.get("K_SKIP_C"))
S_LEN = 2048
D = 1024
DFF = 2816
EPS = 1e-6
NCORES = 8
ENGS = ("pe", "act", "dve", "pool", "sp")
NDMA = 24
NDMA_POOL = 16
SEM_LIMIT = 3000
NSEM_PER_ENG = 10


DEBUG_NAMES = {}


class Op:
    __slots__ = ("eng", "fn", "deps", "dmadeps", "marked", "sem", "cnt", "idx", "dma", "tag")

    def __init__(self, eng, fn):
        self.eng = eng
        self.fn = fn
        self.deps = {}
        self.dmadeps = {}
        self.marked = False
        self.sem = None
        self.cnt = 0
        self.idx = -1
        self.dma = None


class Sched:
    def __init__(self):
        self.ops = {e: [] for e in ENGS}
        self.last_w = {}
        self.readers = {}
        self.dma_use = [0] * NDMA
        self.dma_rr = 0
        self.dma_rr_pool = 0
        self.pending = {e: [] for e in ENGS}
        self.final_dma = []

    def _add_dep(self, o, tok):
        if tok is None:
            return
        if not isinstance(tok, tuple) and tok.dma is not None:
            tok = tok.dma
        if isinstance(tok, tuple):
            i, v = tok
            if o.dmadeps.get(i, 0) < v:
                o.dmadeps[i] = v
            return
        if tok.eng == o.eng and o.eng in ("pe", "sp"):
            return
        cur = o.deps.get(tok.eng)
        if cur is None or cur.idx < tok.idx:
            o.deps[tok.eng] = tok

    def op(self, eng, fn, reads=(), writes=(), dma=False, raw_same_only=True):
        o = Op(eng, fn)
        o.tag = (eng, tuple(reads), tuple(writes))
        o.idx = len(self.ops[eng])
        for tok in self.pending[eng]:
            self._add_dep(o, tok)
        self.pending[eng] = []
        for k in reads:
            self._add_dep(o, self.last_w.get(k))
        for k in writes:
            self._add_dep(o, self.last_w.get(k))
            for r in self.readers.get(k, ()):
                self._add_dep(o, r)
        tok = o
        if dma:
            if eng == "pool":
                i = self.dma_rr_pool
                self.dma_rr_pool = (self.dma_rr_pool + 1) % NDMA_POOL
            else:
                i = NDMA_POOL + self.dma_rr
                self.dma_rr = (self.dma_rr + 1) % (NDMA - NDMA_POOL)
            if self.dma_use[i] > 0:
                self._add_dep(o, (i, 16 * self.dma_use[i]))
            self.dma_use[i] += 1
            o.dma = (i, 16 * self.dma_use[i])
            tok = o.dma
        for k in reads:
            self.readers.setdefault(k, []).append(tok)
        for k in writes:
            self.last_w[k] = tok
            self.readers[k] = []
        self.ops[eng].append(o)
        return tok

    def barrier(self):
        toks = []
        for e in ENGS:
            if self.ops[e]:
                toks.append(self.ops[e][-1])
        for i in range(NDMA):
            if self.dma_use[i] > 0:
                toks.append((i, 16 * self.dma_use[i]))
        for e in ENGS:
            self.pending[e] = list(toks)
        self.last_w = {}
        self.readers = {}

    def finalize(self, nc):
        for e in ENGS:
            for o in self.ops[e]:
                for d in o.deps.values():
                    d.marked = True
        esems = {}
        self._ctx = []
        for e in ENGS:
            n = sum(1 for o in self.ops[e] if o.marked)
            need = max(1, (n + SEM_LIMIT - 1) // SEM_LIMIT)
            assert need <= NSEM_PER_ENG, (e, n)
            lst = []
            for k in range(need):
                cm = nc.semaphore(f"s_{e}_{k}")
                lst.append(cm.__enter__())
                self._ctx.append(cm)
            esems[e] = lst
            c = 0
            for o in self.ops[e]:
                if o.marked:
                    o.sem = lst[c // SEM_LIMIT]
                    o.cnt = c % SEM_LIMIT + 1
                    c += 1
        dsems = []
        for i in range(NDMA):
            cm = nc.semaphore(f"s_dma_{i}")
            dsems.append(cm.__enter__())
            self._ctx.append(cm)
        self.dsems = dsems

    def emit(self, eng_name, engobj):
        waited = {}
        dwaited = {}
        for o in self.ops[eng_name]:
            for src, d in o.deps.items():
                if waited.get(src, -1) >= d.idx:
                    continue
                engobj.wait_ge(d.sem, d.cnt)
                waited[src] = d.idx
            for i, v in o.dmadeps.items():
                if dwaited.get(i, 0) >= v:
                    continue
                engobj.wait_ge(self.dsems[i], v)
                dwaited[i] = v
            ins = o.fn(engobj)
            try:
                DEBUG_NAMES[ins.ins.name] = o.tag
            except Exception:
                pass
            if o.dma is not None:
                ins.then_inc(self.dsems[o.dma[0]], 16)
            elif o.marked:
                ins.then_inc(o.sem, 1)
        if eng_name == "sp":
            for (i, v) in self.final_dma:
                if dwaited.get(i, 0) < v:
                    engobj.wait_ge(self.dsems[i], v)
                    dwaited[i] = v

    def close(self):
        for cm in reversed(self._ctx):
            cm.__exit__(None, None, None)


class Arena:
    def __init__(self, nc, nbytes):
        self.t = nc.alloc_sbuf_tensor("arena", [128, nbytes], U8)
        self.n = nbytes
        self.off = 0
        self.peak = 0

    def alloc(self, shape, dtype):
        esz = 4 if dtype == F32 else 2
        n = 1
        for s in shape:
            n *= s
        nb = n * esz
        off = (self.off + 63) // 64 * 64
        assert off + nb <= self.n, ("arena overflow", off, nb, self.n)
        self.off = off + nb
        self.peak = max(self.peak, self.off)
        v = self.t[:, off:off + nb].bitcast(dtype)
        if len(shape) == 2:
            v = v.rearrange("p (a b) -> p a b", a=shape[0])
        elif len(shape) == 3:
            v = v.rearrange("p (a b c) -> p a b c", a=shape[0], b=shape[1])
        return v

    def mark(self):
        return self.off

    def release(self, m):
        self.off = m


class Rot:
    def __init__(self, items):
        self.items = list(items)
        self.i = 0

    def next(self):
        v = self.items[self.i % len(self.items)]
        self.i += 1
        return v


G_MIX0, G_FFN0, G_MIX1, G_FFN1, G_FINAL, G_CQ, G_CKV, NG = 0, 8, 16, 24, 32, 40, 43, 45
C_ONES, C_NEG, C_TRI, C_MSB, C_MMLA, C_ID, NCM = 0, 128, 256, 384, 512, 640, 768


def build(stages=("mix0", "ffn0", "mix1", "ffn1"), final_norm=True):
    nc = bass.Bass("TRN2", target_bir_lowering=False)

    def din(name, shape):
        return nc.dram_tensor(name, list(shape), F32, kind="ExternalInput").ap()

    xT_d = din("xT", [D, S_LEN])
    w_in_d = din("ev_w_in", [D, 2208]).rearrange("(c p) n -> p c n", p=128)
    w_uq_d = din("ev_w_uq", [384, 768]).rearrange("(c p) n -> p c n", p=128)
    w_ukv_d = din("ev_w_ukv", [256, 1024]).rearrange("(c p) n -> p c n", p=128)
    w_eo_d = din("ev_w_out", [D, D]).rearrange("(c p) n -> p c n", p=128)
    w_qkv_d = din("od_w_qkv", [D, 3072]).rearrange("(c p) n -> p c n", p=128)
    w_oo_d = din("od_w_out", [D, D]).rearrange("(c p) n -> p c n", p=128)
    w_gate_d = din("w_gate", [2, D, DFF])
    w_up_d = din("w_up", [2, D, DFF])
    w_down_d = din("w_down", [2, DFF, D])
    gvec_d = din("gvec", [128, NG])
    cmat_d = din("cmat", [128, NCM])
    cc_d = din("rope_cc", [128, S_LEN])
    ss_d = din("rope_ss", [128, S_LEN])
    btab_d = din("bias_tab", [16, 128, 640])
    mask1_d = din("mask1", [128, 640])
    nmask1_d = din("nmask1", [128, 640])
    outT_d = nc.dram_tensor("outT", [D, S_LEN], F32, kind="ExternalOutput").ap()
    outT_v = outT_d.rearrange("(c p) t -> p c t", p=128)
    xT_v = xT_d.rearrange("(c p) t -> p c t", p=128)

    S = Sched()
    A = Arena(nc, 211968)
    psall = nc.alloc_psum_tensor("psall", [128, 4096], F32)
    ps = [psall[:, b * 512:(b + 1) * 512] for b in range(8)]

    hT = A.alloc((8, S_LEN), F32)
    cb = A.alloc((NCM,), BF16)
    gvec = A.alloc((NG,), F32)
    sq = A.alloc((8, 512), BF16)
    rstd = A.alloc((512,), F32)
    ones_bf = cb[:, C_ONES:C_ONES + 128]
    neg_bf = cb[:, C_NEG:C_NEG + 128]
    tri_bf = cb[:, C_TRI:C_TRI + 128]
    msb_bf = cb[:, C_MSB:C_MSB + 128]
    mmla_bf = cb[:, C_MMLA:C_MMLA + 128]
    ident_bf = cb[:, C_ID:C_ID + 128]

    PS = Rot(range(8))
    state = {"PS": PS}

    def mm(out, lhsT, rhs, start, stop, reads, writes):
        S.op("pe", lambda e: e.matmul(out, lhsT, rhs, start=start, stop=stop, skip_group_check=True),
             reads=reads, writes=writes)

    def dve_tt(out, in0, in1, op, reads, writes):
        S.op("dve", lambda e: e.tensor_tensor(out, in0, in1, op), reads=reads, writes=writes)

    def dve_tss(out, in_, scalar, op, reads, writes):
        S.op("dve", lambda e: e.tensor_single_scalar(out, in_, scalar, op), reads=reads, writes=writes)

    def dve_copy(out, in_, reads, writes):
        S.op("dve", lambda e: e.tensor_copy(out, in_), reads=reads, writes=writes)

    def act(out, in_, func, reads, writes, bias=0.0, scale=1.0):
        S.op("act", lambda e: e.activation(out, in_, func, bias=bias, scale=scale), reads=reads, writes=writes)

    def act_copy(out, in_, reads, writes):
        S.op("act", lambda e: e.copy(out, in_), reads=reads, writes=writes)

    def act_mul(out, in_, m, reads, writes):
        S.op("act", lambda e: e.mul(out, in_, m), reads=reads, writes=writes)

    def pool_copy(out, in_, reads, writes):
        S.op("pool", lambda e: e.tensor_copy(out, in_), reads=reads, writes=writes)

    def dma(eng, out, in_, reads, writes):
        return S.op(eng, lambda e: e.dma_start(out=out, in_=in_), reads=reads, writes=writes, dma=True)

    dma("pool", cb[:, :], cmat_d[:, :], [], [("cb",)])
    dma("sp", gvec[:, :], gvec_d[:, :], [], [("gvec",)])
    for c in range(8):
        dma("sp", hT[:, c, :], xT_v[:, c, :], [], [("h", c, tb) for tb in range(4)])

    def rstd_from(bank, inv_n):
        S.op("act", lambda e: e.activation(rstd[:, :], ps[bank][:, :], AF.Sqrt, bias=EPS, scale=inv_n),
             reads=[("ps", bank)], writes=[("rstd",)])
        S.op("dve", lambda e: e.reciprocal(rstd[:, :], rstd[:, :]),
             reads=[("rstd",)], writes=[("rstd",)])

    def norm_main(tb, gcol, dst, dkeys):
        t0 = tb * 512
        act(sq[:, :, :], hT[:, :, t0:t0 + 512], AF.Square, [("h", c, tb) for c in range(8)], [("sq",)])
        bank = state["PS"].next()
        for c in range(8):
            mm(ps[bank][:, :], ones_bf, sq[:, c, :], c == 0, c == 7, [("sq",), ("cb",)], [("ps", bank)])
        rstd_from(bank, 1.0 / D)
        for c in range(8):
            S.op("dve", (lambda c: lambda e: e.scalar_tensor_tensor(
                dst[:, c, :], hT[:, c, t0:t0 + 512], gvec[:, gcol + c:gcol + c + 1], rstd[:, :],
                ALU.mult, ALU.mult))(c),
                reads=[("h", c, tb), ("rstd",), ("gvec",)], writes=[dkeys(c)])

    def ffn(l, gcol):
        S.barrier()
        state["PS"] = Rot(range(8))
        m0 = A.mark()
        uThs = [A.alloc((8, 1024), BF16) for _ in range(2)]
        actT = A.alloc((22, 1024), BF16)
        wgu = [A.alloc((2, 8, 256), BF16) for _ in range(3)]
        wdc = [A.alloc((22, 128), BF16) for _ in range(3)]
        sg = [A.alloc((512,), F32) for _ in range(2)]
        wg_v = w_gate_d[l].rearrange("(c p) f -> p c f", p=128)
        wu_v = w_up_d[l].rearrange("(c p) f -> p c f", p=128)
        wd_v = w_down_d[l].rearrange("(c p) d -> p c d", p=128)
        n_w = 0
        n_d = 0
        n_s = 0
        def ffn_norm(th):
            for tb2 in range(2):
                tb = th * 2 + tb2
                norm_main(tb, gcol, uThs[th][:, :, tb2 * 512:(tb2 + 1) * 512], lambda c, tb2=tb2, th=th: ("uTh", th, c, tb2))

        ffn_norm(0)
        for th in range(2):
            uTh = uThs[th]
            for fg in range(11):
                if th == 0 and fg == 3:
                    ffn_norm(1)
                slot = n_w % 3
                n_w += 1
                buf = wgu[slot]
                dma("pool", buf[:, 0, :, :], wg_v[:, :, fg * 256:(fg + 1) * 256], [], [("wgu", slot, 0)])
                dma("pool", buf[:, 1, :, :], wu_v[:, :, fg * 256:(fg + 1) * 256], [], [("wgu", slot, 1)])
                for fc in range(2):
                    f = fg * 2 + fc
                    for tb2 in range(2):
                        ts = slice(tb2 * 512, (tb2 + 1) * 512)
                        bg = state["PS"].next()
                        bu = state["PS"].next()
                        for c in range(8):
                            mm(ps[bg][:, :], buf[:, 0, c, fc * 128:(fc + 1) * 128], uTh[:, c, ts], c == 0, c == 7,
                               [("wgu", slot, 0), ("uTh", th, c, tb2)], [("ps", bg)])
                        for c in range(8):
                            mm(ps[bu][:, :], buf[:, 1, c, fc * 128:(fc + 1) * 128], uTh[:, c, ts], c == 0, c == 7,
                               [("wgu", slot, 1), ("uTh", th, c, tb2)], [("ps", bu)])
                        sgb = sg[n_s % 2]
                        sk = ("sg", n_s % 2)
                        n_s += 1
                        act(sgb[:, :], ps[bg][:, :], AF.Silu, [("ps", bg)], [sk])
                        dve_tt(actT[:, f, ts], ps[bu][:, :], sgb[:, :], ALU.mult, [("ps", bu), sk], [("actT", f, tb2)])
            for dc in range(8):
                slot = n_d % 3
                n_d += 1
                wb = wdc[slot]
                dma("pool", wb[:, 0:11, :], wd_v[:, 0:11, dc * 128:(dc + 1) * 128], [], [("wdc", slot)])
                dma("pool", wb[:, 11:22, :], wd_v[:, 11:22, dc * 128:(dc + 1) * 128], [], [("wdc", slot)])
                for tb2 in range(2):
                    tb = th * 2 + tb2
                    ts = slice(tb2 * 512, (tb2 + 1) * 512)
                    b = state["PS"].next()
                    for f in range(22):
                        mm(ps[b][:, :], wb[:, f, :], actT[:, f, ts], f == 0, f == 21,
                           [("wdc", slot), ("actT", f, tb2)], [("ps", b)])
                    hs = hT[:, dc, tb * 512:(tb + 1) * 512]
                    dve_tt(hs, ps[b][:, :], hs, ALU.add, [("ps", b), ("h", dc, tb)], [("h", dc, tb)])
        S.barrier()
        A.release(m0)

    def outproj(w_d, oT):
        wo = A.alloc((8, 1024), BF16)
        for half in range(2):
            dma("pool", wo[:, :, half * 512:(half + 1) * 512], w_d[:, :, half * 512:(half + 1) * 512], [], [("wo", half)])
        for dc in range(8):
            for tb in range(4):
                b = state["PS"].next()
                for kc in range(8):
                    mm(ps[b][:, :], wo[:, kc, dc * 128:(dc + 1) * 128], oT[:, kc, tb * 512:(tb + 1) * 512],
                       kc == 0, kc == 7, [("wo", dc // 4), ("oT", kc, tb)], [("ps", b)])
                hs = hT[:, dc, tb * 512:(tb + 1) * 512]
                dve_tt(hs, ps[b][:, :], hs, ALU.add, [("ps", b), ("h", dc, tb)], [("h", dc, tb)])

    def mix0():
        S.barrier()
        state["PS"] = Rot(range(8))
        m0 = A.mark()
        cqn = A.alloc((3, S_LEN), BF16)
        ckvn = A.alloc((2, S_LEN), BF16)
        krope = A.alloc((S_LEN,), BF16)
        oT = A.alloc((8, S_LEN), BF16)
        mU = A.mark()
        uT = A.alloc((8, S_LEN), BF16)
        mA = A.mark()
        wA = A.alloc((8, 704), BF16)
        CCb = [A.alloc((512,), F32) for _ in range(2)]
        SSb = [A.alloc((512,), F32) for _ in range(2)]
        t1 = A.alloc((512,), F32)
        t2 = A.alloc((512,), F32)
        dma("pool", wA[:, :, 0:672], w_in_d[:, :, 0:672], [], [("wA", 0)])
        dma("pool", wA[:, :, 672:688], w_in_d[:, :, 656:672], [], [("wA", 1)])
        dma("pool", wA[:, :, 688:704], w_in_d[:, :, 640:656], [], [("wA", 2)])
        norm_main(0, G_MIX0, uT[:, :, 0:512], lambda c: ("uT", c, 0))
        for tb in range(4):
            ts = slice(tb * 512, (tb + 1) * 512)
            if tb + 1 < 4:
                norm_main(tb + 1, G_MIX0, uT[:, :, (tb + 1) * 512:(tb + 2) * 512], lambda c, tb=tb: ("uT", c, tb + 1))
            for (nch, col0, gc, dst, dname, invn) in ((3, 0, G_CQ, cqn, "cqn", 1.0 / 384),
                                                       (2, 384, G_CKV, ckvn, "ckvn", 1.0 / 256)):
                banks = []
                for j in range(nch):
                    b = state["PS"].next()
                    banks.append(b)
                    for c in range(8):
                        mm(ps[b][:, :], wA[:, c, col0 + j * 128:col0 + (j + 1) * 128], uT[:, c, ts], c == 0, c == 7,
                           [("wA", 0), ("uT", c, tb)], [("ps", b)])
                    act(sq[:, j, :], ps[b][:, :], AF.Square, [("ps", b)], [("sq",)])
                bs = state["PS"].next()
                for j in range(nch):
                    mm(ps[bs][:, :], ones_bf, sq[:, j, :], j == 0, j == nch - 1, [("sq",), ("cb",)], [("ps", bs)])
                rstd_from(bs, invn)
                for j in range(nch):
                    b = banks[j]
                    S.op("dve", (lambda j, b, dst, gc, ts: lambda e: e.scalar_tensor_tensor(
                        dst[:, j, ts], ps[b][:, :], gvec[:, gc + j:gc + j + 1], rstd[:, :], ALU.mult, ALU.mult))(j, b, dst, gc, ts),
                        reads=[("ps", b), ("rstd",), ("gvec",)], writes=[(dname, j, tb)])
            b = state["PS"].next()
            for c in range(8):
                mm(ps[b][0:64, :], wA[:, c, 640:704], uT[:, c, ts], c == 0, c == 7,
                   [("wA", 0), ("wA", 1), ("wA", 2), ("uT", c, tb)], [("ps", b)])
            cc = CCb[tb % 2]
            ss = SSb[tb % 2]
            dma("sp", cc[:, :], cc_d[:, ts], [], [("CCb", tb % 2)])
            dma("sp", ss[:, :], ss_d[:, ts], [], [("SSb", tb % 2)])
            dve_tt(t1[0:32, :], ps[b][0:32, :], cc[0:32, :], ALU.mult, [("ps", b), ("CCb", tb % 2)], [("t1",)])
            dve_tt(t2[0:32, :], ps[b][32:64, :], ss[32:64, :], ALU.mult, [("ps", b), ("SSb", tb % 2)], [("t2",)])
            dve_tt(krope[0:32, ts], t1[0:32, :], t2[0:32, :], ALU.add, [("t1",), ("t2",)], [("krope", tb)])
        S.barrier()
        A.release(mA)

        if not _SKIP_B:
            mB = A.mark()
            state["PS"] = Rot([0, 1, 2, 3, 4, 5])
            wB = A.alloc((8, 384), BF16)
            qb = A.alloc((S_LEN,), BF16)
            kb = A.alloc((S_LEN,), BF16)
            vb = A.alloc((16, 128), BF16)
            ebuf = [A.alloc((2, 512), F32) for _ in range(1)]
            spb = [A.alloc((2, 512), BF16) for _ in range(3)]
            wbuf = [A.alloc((2, 512), BF16) for _ in range(3)]
            sacc = A.alloc((2, 512), F32)
            saccb = [A.alloc((2, 512), BF16) for _ in range(2)]
            ZZ = [psall[:, q * 1024:(q + 1) * 1024].rearrange("p (a c) -> p a c", a=2) for q in range(3)]
            RR = psall[:, 2048:3072].rearrange("p (a c) -> p a c", a=2)
            pcount = 0
            for j in range(4):
                dma("pool", wB[:, :, 0:128], w_in_d[:, :, 672 + j * 128:672 + (j + 1) * 128], [], [("wB", 0)])
                dma("pool", wB[:, :, 128:256], w_in_d[:, :, 1184 + j * 128:1184 + (j + 1) * 128], [], [("wB", 1)])
                dma("pool", wB[:, :, 256:384], w_in_d[:, :, 1696 + j * 128:1696 + (j + 1) * 128], [], [("wB", 2)])
                for tb in range(4):
                    ts = slice(tb * 512, (tb + 1) * 512)
                    b = state["PS"].next()
                    for c in range(8):
                        mm(ps[b][:, :], wB[:, c, 0:128], uT[:, c, ts], c == 0, c == 7,
                           [("wB", 0), ("uT", c, tb)], [("ps", b)])
                    dve_tss(qb[:, ts], ps[b][:, :], 0.125, ALU.mult, [("ps", b)], [("qb", tb)])
                    b = state["PS"].next()
                    for c in range(8):
                        mm(ps[b][:, :], wB[:, c, 128:256], uT[:, c, ts], c == 0, c == 7,
                           [("wB", 1), ("uT", c, tb)], [("ps", b)])
                    dve_copy(kb[:, ts], ps[b][:, :], [("ps", b)], [("kb", tb)])
                for tq in range(4):
                    b = state["PS"].next()
                    for q in range(4):
                        tile = tq * 4 + q
                        for c in range(8):
                            mm(ps[b][:, q * 128:(q + 1) * 128], uT[:, c, tile * 128:(tile + 1) * 128], wB[:, c, 256:384],
                               c == 0, c == 7, [("wB", 2), ("uT", c, tile // 4)], [("ps", b)])
                    dve_copy(vb[:, tq * 4:(tq + 1) * 4, :], ps[b][:, :].rearrange("p (a b) -> p a b", a=4),
                             [("ps", b)], [("vb", tq)])
                items = []
                for i in range(4):
                    for kt in range(4 * i + 3, -1, -1):
                        c0 = max(0, 128 * kt - 512 * i)
                        c0p = max(0, 128 * (kt + 1) - 512 * i)
                        items.append(dict(i=i, kt=kt, c0=c0, c0p=c0p, n=512 - c0, first=(kt == 4 * i + 3),
                                          last=(kt == 0), diag=(kt >= 4 * i), p=pcount))
                        pcount += 1

                def s1(it):
                    i, kt, c0, n = it["i"], it["kt"], it["c0"], it["n"]
                    zq = it["p"] % 3
                    zk = [("ps", 2 * zq), ("ps", 2 * zq + 1)]
                    for _d in range(N_DUMMY_SB):
                        mm(ps[2 * zq + _d % 2][:, c0:512], ones_bf, uT[:, _d, 512 * i + c0:512 * i + 512], True, True,
                           [("cb",)], [("ps", 2 * zq + _d % 2)])
                    for hh in range(2):
                        pb = hh * 64
                        mm(ps[2 * zq + hh][:, c0:512], kb[pb:pb + 64, kt * 128:(kt + 1) * 128],
                           qb[pb:pb + 64, 512 * i + c0:512 * i + 512], True, True,
                           [("kb", kt // 4), ("qb", i)], [("ps", 2 * zq + hh)])
                    ek = 0
                    sk = it["p"] % 3
                    act(ebuf[ek][:, :, 0:n], ZZ[zq][:, :, c0:512], AF.Exp, zk, [("ebuf", ek)])
                    act(spb[sk][:, :, 0:n], ebuf[ek][:, :, 0:n], AF.Ln, [("ebuf", ek)], [("spb", sk)], bias=1.0)
                    if it["diag"]:
                        for hh in range(2):
                            dve_tt(spb[sk][:, hh, 0:128], spb[sk][:, hh, 0:128], msb_bf, ALU.mult,
                                   [("spb", sk), ("cb",)], [("spb", sk)])

                def s2(it):
                    i, kt, c0, n = it["i"], it["kt"], it["c0"], it["n"]
                    zq = it["p"] % 3
                    zk = [("ps", 2 * zq), ("ps", 2 * zq + 1)]
                    sk3 = it["p"] % 3
                    sk = it["p"] % 3
                    pp = it["p"] % 2
                    for hh in range(2):
                        mm(ps[2 * zq + hh][:, c0:512], tri_bf, spb[sk3][:, hh, 0:n], False, it["first"],
                           [("spb", sk3), ("cb",)], [("ps", 2 * zq + hh)])
                    if not it["first"]:
                        for hh in range(2):
                            mm(ps[2 * zq + hh][:, c0:512], neg_bf, saccb[pp][:, hh, c0:512], False, True,
                               [("saccb", pp), ("cb",)], [("ps", 2 * zq + hh)])
                    act(wbuf[sk][:, :, 0:n], ZZ[zq][:, :, c0:512], AF.Exp, zk, [("wbuf", sk)])
                    if it["diag"]:
                        for hh in range(2):
                            dve_tt(wbuf[sk][:, hh, 0:128], wbuf[sk][:, hh, 0:128], msb_bf, ALU.mult,
                                   [("wbuf", sk), ("cb",)], [("wbuf", sk)])
                    if not it["last"]:
                        if it["first"]:
                            S.op("dve", lambda e: e.memset(sacc[:, :, :], 0.0), reads=[], writes=[("sacc",)])
                        dve_tt(sacc[:, :, c0:512], sacc[:, :, c0:512], spb[sk3][:, :, 0:n], ALU.add,
                               [("sacc",), ("spb", sk3)], [("sacc",)])
                        dve_copy(saccb[1 - pp][:, :, :], sacc[:, :, :], [("sacc",)], [("saccb", 1 - pp)])

                def s3(it):
                    i, kt, c0, n = it["i"], it["kt"], it["c0"], it["n"]
                    sk = it["p"] % 3
                    Ob = 6 + i % 2
                    for hh in range(2):
                        pb = hh * 64
                        mm(ps[Ob][pb:pb + 64, c0:512], vb[:, kt, pb:pb + 64], wbuf[sk][:, hh, 0:n], it["first"], it["last"],
                           [("vb", kt // 4), ("wbuf", sk)], [("ps", Ob)])
                    if it["last"]:
                        dve_copy(oT[:, 4 + j, 512 * i:512 * i + 512], ps[Ob][:, :], [("ps", Ob)], [("oT", 4 + j, i)])

                n_it = len(items)
                L1, L2 = 1, 3
                for step in range(n_it + L2):
                    if step < n_it:
                        s1(items[step])
                    if 0 <= step - L1 < n_it:
                        s2(items[step - L1])
                    if 0 <= step - L2 < n_it:
                        s3(items[step - L2])
            S.barrier()
            A.release(mU)

        if not _SKIP_C:
            mC = A.mark()
            state["PS"] = Rot([0, 1, 2, 3, 6, 7])
            wuq = A.alloc((3, 768), BF16)
            wsw = A.alloc((3, 8, 32), BF16)
            wukv = A.alloc((2, 1024), BF16)
            CC = A.alloc((S_LEN,), F32)
            SS = A.alloc((S_LEN,), F32)
            Qh = [A.alloc((S_LEN,), BF16) for _ in range(2)]
            Kh = [A.alloc((S_LEN,), BF16) for _ in range(2)]
            Vh = [A.alloc((16, 128), BF16) for _ in range(2)]
            pbuf = [A.alloc((512,), BF16) for _ in range(6)]
            rec = A.alloc((512,), F32)
            t1 = A.alloc((512,), F32)
            t2 = A.alloc((512,), F32)
            dma("pool", wuq[:, :, :], w_uq_d[:, :, :], [], [("wuq",)])
            wuq4 = w_uq_d.rearrange("p c (h f) -> p c h f", h=8)
            for kc in range(3):
                dma("pool", wsw[:, kc, :, 0:16], wuq4[:, kc, :, 80:96], [], [("wsw", 0)])
                dma("pool", wsw[:, kc, :, 16:32], wuq4[:, kc, :, 64:80], [], [("wsw", 1)])
            dma("pool", wukv[:, :, :], w_ukv_d[:, :, :], [], [("wukv",)])
            dma("sp", CC[:, :], cc_d[:, :], [], [("CC",)])
            dma("sp", SS[:, :], ss_d[:, :], [], [("SS",)])
            for sl in range(2):
                S.op("dve", (lambda sl: lambda e: e.memset(Vh[sl][:, :, 64:128], 1.0))(sl), reads=[], writes=[("Vh1", sl)])
            gcount = 0
            pcount = 0
            scale_a = 96.0 ** -0.5
            for h in range(8):
                sl = h % 2
                j = h // 2
                pb = (h % 2) * 64
                for tb in range(4):
                    ts = slice(tb * 512, (tb + 1) * 512)
                    bA = state["PS"].next()
                    for kc in range(3):
                        mm(ps[bA][0:96, :], wuq[:, kc, h * 96:(h + 1) * 96], cqn[:, kc, ts], kc == 0, kc == 2,
                           [("wuq",), ("cqn", kc, tb)], [("ps", bA)])
                    bB = state["PS"].next()
                    for kc in range(3):
                        mm(ps[bB][64:96, :], wsw[:, kc, h, :], cqn[:, kc, ts], kc == 0, kc == 2,
                           [("wsw", 0), ("wsw", 1), ("cqn", kc, tb)], [("ps", bB)])
                    act_copy(Qh[sl][0:64, ts], ps[bA][0:64, :], [("ps", bA)], [("Qh", sl, tb)])
                    dve_tt(t1[64:96, :], ps[bA][64:96, :], CC[64:96, ts], ALU.mult, [("ps", bA), ("CC",)], [("t1",)])
                    dve_tt(t2[64:96, :], ps[bB][64:96, :], SS[64:96, ts], ALU.mult, [("ps", bB), ("SS",)], [("t2",)])
                    dve_tt(Qh[sl][64:96, ts], t1[64:96, :], t2[64:96, :], ALU.add, [("t1",), ("t2",)], [("Qh", sl, tb)])
                    bK = state["PS"].next()
                    for kc in range(2):
                        mm(ps[bK][0:64, :], wukv[:, kc, h * 128:h * 128 + 64], ckvn[:, kc, ts], kc == 0, kc == 1,
                           [("wukv",), ("ckvn", kc, tb)], [("ps", bK)])
                    act_copy(Kh[sl][0:64, ts], ps[bK][0:64, :], [("ps", bK)], [("Kh", sl, tb)])
                    pool_copy(Kh[sl][64:96, ts], krope[0:32, ts], [("krope", tb)], [("Kh", sl, tb)])
                for half in range(2):
                    b = state["PS"].next()
                    for q in range(8):
                        tile = half * 8 + q
                        for kc in range(2):
                            mm(ps[b][:, q * 64:(q + 1) * 64], ckvn[:, kc, tile * 128:(tile + 1) * 128],
                               wukv[:, kc, h * 128 + 64:h * 128 + 128], kc == 0, kc == 1,
                               [("wukv",), ("ckvn", kc, tile // 4)], [("ps", b)])
                    dve_copy(Vh[sl][:, half * 8:(half + 1) * 8, 0:64], ps[b][:, :].rearrange("p (a b) -> p a b", a=8),
                             [("ps", b)], [("Vh", sl, half)])
                items = []
                for i in range(4):
                    g = gcount
                    gcount += 1
                    for kt in range(0, 4 * i + 4):
                        c0 = max(0, 128 * kt - 512 * i)
                        items.append(dict(i=i, kt=kt, c0=c0, n=512 - c0, first=(kt == 0), last=(kt == 4 * i + 3),
                                          diag=(kt >= 4 * i), g=g, p=pcount))
                        pcount += 1

                def c1(it):
                    i, kt, c0, n = it["i"], it["kt"], it["c0"], it["n"]
                    zb = state["PS"].next()
                    it["zb"] = zb
                    mm(ps[zb][:, c0:512], Kh[sl][0:96, kt * 128:(kt + 1) * 128], Qh[sl][0:96, 512 * i + c0:512 * i + 512],
                       True, True, [("Kh", sl, kt // 4), ("Qh", sl, i)], [("ps", zb)])
                    pk = it["p"] % 6
                    act(pbuf[pk][:, 0:n], ps[zb][:, c0:512], AF.Exp, [("ps", zb)], [("pbuf", pk)], scale=scale_a)
                    if it["diag"]:
                        dve_tt(pbuf[pk][:, 0:128], pbuf[pk][:, 0:128], mmla_bf, ALU.mult, [("pbuf", pk), ("cb",)], [("pbuf", pk)])

                def c2(it):
                    i, kt, c0, n, g = it["i"], it["kt"], it["c0"], it["n"], it["g"]
                    pk = it["p"] % 6
                    Ob = 4 + g % 2
                    mm(ps[Ob][:, c0:512], Vh[sl][:, kt, :], pbuf[pk][:, 0:n], it["first"], it["last"],
                       [("Vh", sl, kt // 8), ("Vh1", sl), ("pbuf", pk)], [("ps", Ob)])
                    if it["last"]:
                        S.op("dve", (lambda Ob: lambda e: e.reciprocal(rec[64:128, :], ps[Ob][64:128, :]))(Ob),
                             reads=[("ps", Ob)], writes=[("rec",)])
                        dve_tt(oT[pb:pb + 64, j, 512 * i:512 * i + 512], ps[Ob][0:64, :], rec[64:128, :], ALU.mult,
                               [("ps", Ob), ("rec",)], [("oT", j, i)])

                n_it = len(items)
                LC = 4
                for step in range(n_it + LC):
                    if step < n_it:
                        c1(items[step])
                    if 0 <= step - LC < n_it:
                        c2(items[step - LC])
            S.barrier()
            A.release(mC)
        state["PS"] = Rot(range(8))
        outproj(w_eo_d, oT)
        S.barrier()
        A.release(m0)

    def mix1():
        S.barrier()
        state["PS"] = Rot([0, 1, 2, 3])
        m0 = A.mark()
        oT = A.alloc((8, S_LEN), BF16)
        m1 = A.mark()
        uT = A.alloc((8, S_LEN), BF16)
        wq1 = [A.alloc((8, 384), BF16) for _ in range(2)]
        q1 = [A.alloc((S_LEN,), BF16) for _ in range(2)]
        k1 = [A.alloc((S_LEN,), BF16) for _ in range(2)]
        v1 = [A.alloc((16, 2, 128), BF16) for _ in range(2)]
        btab = [A.alloc((640,), F32) for _ in range(2)]
        expB = [A.alloc((640,), BF16) for _ in range(2)]
        mask1 = A.alloc((640,), F32)
        nmask1 = A.alloc((640,), F32)
        pbuf = [A.alloc((512,), BF16) for _ in range(6)]
        rec = A.alloc((512,), F32)
        dma("sp", mask1[:, :], mask1_d[:, :], [], [("mask1",)])
        dma("sp", nmask1[:, :], nmask1_d[:, :], [], [("nmask1",)])
        for sl in range(2):
            S.op("dve", (lambda sl: lambda e: e.memset(v1[sl][:, :, :, 64:128], 1.0))(sl), reads=[], writes=[("v11", sl)])
        for tb in range(4):
            norm_main(tb, G_MIX1, uT[:, :, tb * 512:(tb + 1) * 512], lambda c, tb=tb: ("uT", c, tb))
        gcount = 0
        pcount = 0
        for jp in range(8):
            sl = jp % 2
            for part in range(3):
                dma("pool", wq1[sl][:, :, part * 128:(part + 1) * 128],
                    w_qkv_d[:, :, part * 1024 + jp * 128:part * 1024 + (jp + 1) * 128], [], [("wq1", sl, part)])
            for tb in range(4):
                ts = slice(tb * 512, (tb + 1) * 512)
                b = state["PS"].next()
                for c in range(8):
                    mm(ps[b][:, :], wq1[sl][:, c, 0:128], uT[:, c, ts], c == 0, c == 7,
                       [("wq1", sl, 0), ("uT", c, tb)], [("ps", b)])
                act_mul(q1[sl][:, ts], ps[b][:, :], 0.125, [("ps", b)], [("q1", sl, tb)])
                b = state["PS"].next()
                for c in range(8):
                    mm(ps[b][:, :], wq1[sl][:, c, 128:256], uT[:, c, ts], c == 0, c == 7,
                       [("wq1", sl, 1), ("uT", c, tb)], [("ps", b)])
                act_copy(k1[sl][:, ts], ps[b][:, :], [("ps", b)], [("k1", sl, tb)])
            for tq in range(4):
                b = state["PS"].next()
                for q in range(4):
                    tile = tq * 4 + q
                    for c in range(8):
                        mm(ps[b][:, q * 128:(q + 1) * 128], uT[:, c, tile * 128:(tile + 1) * 128], wq1[sl][:, c, 256:384],
                           c == 0, c == 7, [("wq1", sl, 2), ("uT", c, tile // 4)], [("ps", b)])
                dve_copy(v1[sl][:, tq * 4:(tq + 1) * 4, :, 0:64],
                         ps[b][:, :].rearrange("p (a h b) -> p a h b", a=4, h=2), [("ps", b)], [("v1", sl, tq)])
            for hh in range(2):
                h = 2 * jp + hh
                bs = hh
                dma("sp", btab[bs][:, :], btab_d[h], [], [("btab", bs)])
                dve_tt(btab[bs][:, :], btab[bs][:, :], mask1[:, :], ALU.mult, [("btab", bs), ("mask1",)], [("btab", bs)])
                dve_tt(expB[bs][:, :], btab[bs][:, :], nmask1[:, :], ALU.add, [("btab", bs), ("nmask1",)], [("expB", bs)])
            if True:
                items = []
                for i in range(4):
                    kts = list(range(max(0, 4 * i - 4), 4 * i + 4))
                    for kt in kts:
                        for hh in range(2):
                            tlo = max(128 * kt, 512 * i)
                            thi = min(128 * kt + 640, 512 * i + 512)
                            items.append(dict(i=i, kt=kt, tlo=tlo, thi=thi, n=thi - tlo, tl0=tlo - 128 * kt, hh=hh,
                                              first=(kt == kts[0]), last=(kt == kts[-1]), g=hh, p=pcount))
                            pcount += 1

                def d1(it):
                    pb = it["hh"] * 64
                    bs = it["hh"]
                    i, kt, tlo, thi, n = it["i"], it["kt"], it["tlo"], it["thi"], it["n"]
                    zb = state["PS"].next()
                    it["zb"] = zb
                    for _d in range(N_DUMMY_M1):
                        mm(ps[zb][:, 0:n], ones_bf, uT[:, _d, tlo:thi], True, True, [("cb",)], [("ps", zb)])
                    mm(ps[zb][:, 0:n], k1[sl][pb:pb + 64, kt * 128:(kt + 1) * 128], q1[sl][pb:pb + 64, tlo:thi],
                       True, False, [("k1", sl, kt // 4), ("q1", sl, i)], [("ps", zb)])
                    mm(ps[zb][:, 0:n], ident_bf, expB[bs][:, it["tl0"]:it["tl0"] + n],
                       False, True, [("expB", bs), ("cb",)], [("ps", zb)])
                    pk = it["p"] % 6
                    act(pbuf[pk][:, 0:n], ps[zb][:, 0:n], AF.Exp, [("ps", zb)], [("pbuf", pk)])

                def d2(it):
                    hh = it["hh"]
                    pb = hh * 64
                    i, kt, tlo, thi, n, g = it["i"], it["kt"], it["tlo"], it["thi"], it["n"], it["g"]
                    pk = it["p"] % 6
                    Ob = 4 + hh + 2 * (i % 2)
                    mm(ps[Ob][:, tlo - 512 * i:thi - 512 * i], v1[sl][:, kt, hh, :], pbuf[pk][:, 0:n], it["first"], it["last"],
                       [("v1", sl, kt // 4), ("v11", sl), ("pbuf", pk)], [("ps", Ob)])
                    if it["last"]:
                        S.op("dve", (lambda Ob: lambda e: e.reciprocal(rec[64:128, :], ps[Ob][64:128, :]))(Ob),
                             reads=[("ps", Ob)], writes=[("rec",)])
                        dve_tt(oT[pb:pb + 64, jp, 512 * i:512 * i + 512], ps[Ob][0:64, :], rec[64:128, :], ALU.mult,
                               [("ps", Ob), ("rec",)], [("oT", jp, i)])

                n_it = len(items)
                LD = 4
                for step in range(n_it + LD):
                    if step < n_it:
                        d1(items[step])
                    if 0 <= step - LD < n_it:
                        d2(items[step - LD])
        S.barrier()
        A.release(m1)
        state["PS"] = Rot(range(8))
        outproj(w_oo_d, oT)
        S.barrier()
        A.release(m0)

    for st in stages:
        if st == "mix0":
            mix0()
        elif st == "ffn0":
            ffn(0, G_FFN0)
        elif st == "mix1":
            mix1()
        elif st == "ffn1":
            ffn(1, G_FFN1)
    S.barrier()
    state["PS"] = Rot(range(8))
    stage_o = [A.alloc((8, 512), F32) for _ in range(2)]
    for tb in range(4):
        so = stage_o[tb % 2]
        t0 = tb * 512
        if final_norm:
            act(sq[:, :, :], hT[:, :, t0:t0 + 512], AF.Square, [("h", c, tb) for c in range(8)], [("sq",)])
            bank = state["PS"].next()
            for c in range(8):
                mm(ps[bank][:, :], ones_bf, sq[:, c, :], c == 0, c == 7, [("sq",), ("cb",)], [("ps", bank)])
            rstd_from(bank, 1.0 / D)
            for c in range(8):
                S.op("dve", (lambda c, so, t0: lambda e: e.scalar_tensor_tensor(
                    so[:, c, :], hT[:, c, t0:t0 + 512], gvec[:, G_FINAL + c:G_FINAL + c + 1], rstd[:, :],
                    ALU.mult, ALU.mult))(c, so, t0),
                    reads=[("h", c, tb), ("rstd",), ("gvec",)], writes=[("so", tb % 2)])
            tok = dma("sp", outT_v[:, :, t0:t0 + 512], so[:, :, :], [("so", tb % 2)], [("out", tb)])
        else:
            tok = dma("sp", outT_v[:, :, t0:t0 + 512], hT[:, :, t0:t0 + 512], [("h", c, tb) for c in range(8)], [("out", tb)])
        S.final_dma.append(tok)

    S.finalize(nc)
    with nc.Block() as block:
        @block.tensor
        def _(e):
            S.emit("pe", e)

        @block.scalar
        def _(e):
            S.emit("act", e)

        @block.vector
        def _(e):
            S.emit("dve", e)

        @block.gpsimd
        def _(e):
            S.emit("pool", e)

        @block.sync
        def _(e):
            S.emit("sp", e)
    S.close()
    nc._arena_peak = A.peak if False else None
    return nc


def _consts():
    s = np.arange(128)[:, None]
    t = np.arange(128)[None, :]
    cm = np.zeros((128, NCM), np.float32)
    cm[:, C_ONES:C_ONES + 128] = 1.0
    cm[:, C_NEG:C_NEG + 128] = -1.0
    cm[:, C_TRI:C_TRI + 128] = np.where(s >= t, -1.0, 0.0)
    cm[:, C_MSB:C_MSB + 128] = np.where(s < t, 1.0, 0.0)
    cm[:, C_MMLA:C_MMLA + 128] = np.where((s // 64) <= (t // 64), 1.0, 0.0)
    cm[:, C_ID:C_ID + 128] = np.eye(128, dtype=np.float32)
    pos = np.arange(S_LEN, dtype=np.float32)
    inv_freq = (np.float32(10000.0) ** (-np.arange(0, 32, 2, dtype=np.float32) / np.float32(32))).astype(np.float32)
    ang = (pos[:, None] * inv_freq[None, :]).astype(np.float32)
    cos = np.cos(ang).astype(np.float32).T
    sin = np.sin(ang).astype(np.float32).T
    cc = np.concatenate([cos, cos], 0)
    ss = np.concatenate([-sin, sin], 0)
    cc4 = np.ascontiguousarray(np.tile(cc, (4, 1)))
    ss4 = np.ascontiguousarray(np.tile(ss, (4, 1)))
    sl = np.arange(128)[:, None]
    tl = np.arange(640)[None, :]
    d = tl // 64 - sl // 64
    mask1 = ((d >= 0) & (d <= 8)).astype(np.float32)
    bidx = np.clip(tl - sl, -256, 256) + 256
    return cm, cc4, ss4, mask1, bidx


def _gcol(g):
    return np.ascontiguousarray(np.asarray(g, np.float32).reshape(-1, 128).T)


_CACHE = {}


def run(inputs, stages=("mix0", "ffn0", "mix1", "ffn1"), final_norm=True, trace=False):
    f = lambda a: np.ascontiguousarray(np.asarray(a, dtype=np.float32))
    x = f(inputs["x"])
    cm, cc4, ss4, mask1, bidx = _consts()
    gvec = np.concatenate([
        _gcol(inputs["g_mix"][0]), _gcol(inputs["g_ffn"][0]), _gcol(inputs["g_mix"][1]), _gcol(inputs["g_ffn"][1]),
        _gcol(inputs["g_final"]), _gcol(inputs["ev_g_cq"][0]), _gcol(inputs["ev_g_ckv"][0])], axis=1)
    gvec = np.ascontiguousarray(gvec.astype(np.float32))
    assert gvec.shape == (128, NG)
    rb = f(inputs["od_rel_bias"])[0]
    bias_tab = np.ascontiguousarray(rb[:, bidx])
    shared = {
        "ev_w_in": f(inputs["ev_w_in"])[0], "ev_w_uq": f(inputs["ev_w_uq"])[0], "ev_w_ukv": f(inputs["ev_w_ukv"])[0],
        "ev_w_out": f(inputs["ev_w_out"])[0], "od_w_qkv": f(inputs["od_w_qkv"])[0], "od_w_out": f(inputs["od_w_out"])[0],
        "w_gate": f(inputs["w_gate"]), "w_up": f(inputs["w_up"]), "w_down": f(inputs["w_down"]),
        "gvec": gvec, "cmat": cm, "rope_cc": cc4, "rope_ss": ss4, "bias_tab": bias_tab, "mask1": mask1,
        "nmask1": np.ascontiguousarray((mask1 - 1.0) * 30000.0).astype(np.float32),
    }
    key = (tuple(stages), final_norm)
    if key not in _CACHE:
        _CACHE[key] = build(stages, final_norm)
    nc = _CACHE[key]
    in_maps = []
    for b in range(NCORES):
        m = dict(shared)
        m["xT"] = np.ascontiguousarray(x[b].T)
        in_maps.append(m)
    res = run_bass_kernel_spmd(nc, in_maps, core_ids=list(range(NCORES)), **({"trace": True} if trace else {}))
    out = np.stack([np.ascontiguousarray(np.asarray(r["outT"]).T) for r in res.results], axis=0)
    return out.astype(np.float32), res


def kernel(**inputs):
    out, _ = run(inputs)
    return out
```

```python
import numpy as np
import concourse.bass as bass
import concourse.mybir as mybir
from concourse.bass_utils import run_bass_kernel_spmd

F32 = mybir.dt.float32
BF16 = mybir.dt.bfloat16
U8 = mybir.dt.uint8
ALU = mybir.AluOpType
AF = mybir.ActivationFunctionType

import os as _os
_SKIP_B = bool(_os.environ.get("K_SKIP_B"))
N_DUMMY_SB = int(_os.environ.get("K_DUMMY_SB", "3"))
N_DUMMY_M1 = int(_os.environ.get("K_DUMMY_M1", "1"))
_SKIP_C = bool(_os.environ.get("K_SKIP_C"))
S_LEN = 2048
D = 1024
DFF = 2816
EPS = 1e-6
NCORES = 8
ENGS = ("pe", "act", "dve", "pool", "sp")
NDMA = 24
NDMA_POOL = 16
SEM_LIMIT = 3000
NSEM_PER_ENG = 10


DEBUG_NAMES = {}


class Op:
    __slots__ = ("eng", "fn", "deps", "dmadeps", "marked", "sem", "cnt", "idx", "dma", "tag")

    def __init__(self, eng, fn):
        self.eng = eng
        self.fn = fn
        self.deps = {}
        self.dmadeps = {}
        self.marked = False
        self.sem = None
        self.cnt = 0
        self.idx = -1
        self.dma = None


class Sched:
    def __init__(self):
        self.ops = {e: [] for e in ENGS}
        self.last_w = {}
        self.readers = {}
        self.dma_use = [0] * NDMA
        self.dma_rr = 0
        self.dma_rr_pool = 0
        self.pending = {e: [] for e in ENGS}
        self.final_dma = []

    def _add_dep(self, o, tok):
        if tok is None:
            return
        if not isinstance(tok, tuple) and tok.dma is not None:
            tok = tok.dma
        if isinstance(tok, tuple):
            i, v = tok
            if o.dmadeps.get(i, 0) < v:
                o.dmadeps[i] = v
            return
        if tok.eng == o.eng and o.eng in ("pe", "sp"):
            return
        cur = o.deps.get(tok.eng)
        if cur is None or cur.idx < tok.idx:
            o.deps[tok.eng] = tok

    def op(self, eng, fn, reads=(), writes=(), dma=False, raw_same_only=True):
        o = Op(eng, fn)
        o.tag = (eng, tuple(reads), tuple(writes))
        o.idx = len(self.ops[eng])
        for tok in self.pending[eng]:
            self._add_dep(o, tok)
        self.pending[eng] = []
        for k in reads:
            self._add_dep(o, self.last_w.get(k))
        for k in writes:
            self._add_dep(o, self.last_w.get(k))
            for r in self.readers.get(k, ()):
                self._add_dep(o, r)
        tok = o
        if dma:
            if eng == "pool":
                i = self.dma_rr_pool
                self.dma_rr_pool = (self.dma_rr_pool + 1) % NDMA_POOL
            else:
                i = NDMA_POOL + self.dma_rr
                self.dma_rr = (self.dma_rr + 1) % (NDMA - NDMA_POOL)
            if self.dma_use[i] > 0:
                self._add_dep(o, (i, 16 * self.dma_use[i]))
            self.dma_use[i] += 1
            o.dma = (i, 16 * self.dma_use[i])
            tok = o.dma
        for k in reads:
            self.readers.setdefault(k, []).append(tok)
        for k in writes:
            self.last_w[k] = tok
            self.readers[k] = []
        self.ops[eng].append(o)
        return tok

    def barrier(self):
        toks = []
        for e in ENGS:
            if self.ops[e]:
                toks.append(self.ops[e][-1])
        for i in range(NDMA):
            if self.dma_use[i] > 0:
                toks.append((i, 16 * self.dma_use[i]))
        for e in ENGS:
            self.pending[e] = list(toks)
        self.last_w = {}
        self.readers = {}

    def finalize(self, nc):
        for e in ENGS:
            for o in self.ops[e]:
                for d in o.deps.values():
                    d.marked = True
        esems = {}
        self._ctx = []
        for e in ENGS:
            n = sum(1 for o in self.ops[e] if o.marked)
            need = max(1, (n + SEM_LIMIT - 1) // SEM_LIMIT)
            assert need <= NSEM_PER_ENG, (e, n)
            lst = []
            for k in range(need):
                cm = nc.semaphore(f"s_{e}_{k}")
                lst.append(cm.__enter__())
                self._ctx.append(cm)
            esems[e] = lst
            c = 0
            for o in self.ops[e]:
                if o.marked:
                    o.sem = lst[c // SEM_LIMIT]
                    o.cnt = c % SEM_LIMIT + 1
                    c += 1
        dsems = []
        for i in range(NDMA):
            cm = nc.semaphore(f"s_dma_{i}")
            dsems.append(cm.__enter__())
            self._ctx.append(cm)
        self.dsems = dsems

    def emit(self, eng_name, engobj):
        waited = {}
        dwaited = {}
        for o in self.ops[eng_name]:
            for src, d in o.deps.items():
                if waited.get(src, -1) >= d.idx:
                    continue
                engobj.wait_ge(d.sem, d.cnt)
                waited[src] = d.idx
            for i, v in o.dmadeps.items():
                if dwaited.get(i, 0) >= v:
                    continue
                engobj.wait_ge(self.dsems[i], v)
                dwaited[i] = v
            ins = o.fn(engobj)
            try:
                DEBUG_NAMES[ins.ins.name] = o.tag
            except Exception:
                pass
            if o.dma is not None:
                ins.then_inc(self.dsems[o.dma[0]], 16)
            elif o.marked:
                ins.then_inc(o.sem, 1)
        if eng_name == "sp":
            for (i, v) in self.final_dma:
                if dwaited.get(i, 0) < v:
                    engobj.wait_ge(self.dsems[i], v)
                    dwaited[i] = v

    def close(self):
        for cm in reversed(self._ctx):
            cm.__exit__(None, None, None)


class Arena:
    def __init__(self, nc, nbytes):
        self.t = nc.alloc_sbuf_tensor("arena", [128, nbytes], U8)
        self.n = nbytes
        self.off = 0
        self.peak = 0

    def alloc(self, shape, dtype):
        esz = 4 if dtype == F32 else 2
        n = 1
        for s in shape:
            n *= s
        nb = n * esz
        off = (self.off + 63) // 64 * 64
        assert off + nb <= self.n, ("arena overflow", off, nb, self.n)
        self.off = off + nb
        self.peak = max(self.peak, self.off)
        v = self.t[:, off:off + nb].bitcast(dtype)
        if len(shape) == 2:
            v = v.rearrange("p (a b) -> p a b", a=shape[0])
        elif len(shape) == 3:
            v = v.rearrange("p (a b c) -> p a b c", a=shape[0], b=shape[1])
        return v

    def mark(self):
        return self.off

    def release(self, m):
        self.off = m


class Rot:
    def __init__(self, items):
        self.items = list(items)
        self.i = 0

    def next(self):
        v = self.items[self.i % len(self.items)]
        self.i += 1
        return v


G_MIX0, G_FFN0, G_MIX1, G_FFN1, G_FINAL, G_CQ, G_CKV, NG = 0, 8, 16, 24, 32, 40, 43, 45
C_ONES, C_NEG, C_TRI, C_MSB, C_MMLA, C_ID, NCM = 0, 128, 256, 384, 512, 640, 768


def build(stages=("mix0", "ffn0", "mix1", "ffn1"), final_norm=True):
    nc = bass.Bass("TRN2", target_bir_lowering=False)

    def din(name, shape):
        return nc.dram_tensor(name, list(shape), F32, kind="ExternalInput").ap()

    xT_d = din("xT", [D, S_LEN])
    w_in_d = din("ev_w_in", [D, 2208]).rearrange("(c p) n -> p c n", p=128)
    w_uq_d = din("ev_w_uq", [384, 768]).rearrange("(c p) n -> p c n", p=128)
    w_ukv_d = din("ev_w_ukv", [256, 1024]).rearrange("(c p) n -> p c n", p=128)
    w_eo_d = din("ev_w_out", [D, D]).rearrange("(c p) n -> p c n", p=128)
    w_qkv_d = din("od_w_qkv", [D, 3072]).rearrange("(c p) n -> p c n", p=128)
    w_oo_d = din("od_w_out", [D, D]).rearrange("(c p) n -> p c n", p=128)
    w_gate_d = din("w_gate", [2, D, DFF])
    w_up_d = din("w_up", [2, D, DFF])
    w_down_d = din("w_down", [2, DFF, D])
    gvec_d = din("gvec", [128, NG])
    cmat_d = din("cmat", [128, NCM])
    cc_d = din("rope_cc", [128, S_LEN])
    ss_d = din("rope_ss", [128, S_LEN])
    btab_d = din("bias_tab", [16, 128, 640])
    mask1_d = din("mask1", [128, 640])
    nmask1_d = din("nmask1", [128, 640])
    outT_d = nc.dram_tensor("outT", [D, S_LEN], F32, kind="ExternalOutput").ap()
    outT_v = outT_d.rearrange("(c p) t -> p c t", p=128)
    xT_v = xT_d.rearrange("(c p) t -> p c t", p=128)

    S = Sched()
    A = Arena(nc, 211968)
    psall = nc.alloc_psum_tensor("psall", [128, 4096], F32)
    ps = [psall[:, b * 512:(b + 1) * 512] for b in range(8)]

    hT = A.alloc((8, S_LEN), F32)
    cb = A.alloc((NCM,), BF16)
    gvec = A.alloc((NG,), F32)
    sq = A.alloc((8, 512), BF16)
    rstd = A.alloc((512,), F32)
    ones_bf = cb[:, C_ONES:C_ONES + 128]
    neg_bf = cb[:, C_NEG:C_NEG + 128]
    tri_bf = cb[:, C_TRI:C_TRI + 128]
    msb_bf = cb[:, C_MSB:C_MSB + 128]
    mmla_bf = cb[:, C_MMLA:C_MMLA + 128]
    ident_bf = cb[:, C_ID:C_ID + 128]

    PS = Rot(range(8))
    state = {"PS": PS}

    def mm(out, lhsT, rhs, start, stop, reads, writes):
        S.op("pe", lambda e: e.matmul(out, lhsT, rhs, start=start, stop=stop, skip_group_check=True),
             reads=reads, writes=writes)

    def dve_tt(out, in0, in1, op, reads, writes):
        S.op("dve", lambda e: e.tensor_tensor(out, in0, in1, op), reads=reads, writes=writes)

    def dve_tss(out, in_, scalar, op, reads, writes):
        S.op("dve", lambda e: e.tensor_single_scalar(out, in_, scalar, op), reads=reads, writes=writes)

    def dve_copy(out, in_, reads, writes):
        S.op("dve", lambda e: e.tensor_copy(out, in_), reads=reads, writes=writes)

    def act(out, in_, func, reads, writes, bias=0.0, scale=1.0):
        S.op("act", lambda e: e.activation(out, in_, func, bias=bias, scale=scale), reads=reads, writes=writes)

    def act_copy(out, in_, reads, writes):
        S.op("act", lambda e: e.copy(out, in_), reads=reads, writes=writes)

    def act_mul(out, in_, m, reads, writes):
        S.op("act", lambda e: e.mul(out, in_, m), reads=reads, writes=writes)

    def pool_copy(out, in_, reads, writes):
        S.op("pool", lambda e: e.tensor_copy(out, in_), reads=reads, writes=writes)

    def dma(eng, out, in_, reads, writes):
        return S.op(eng, lambda e: e.dma_start(out=out, in_=in_), reads=reads, writes=writes, dma=True)

    dma("pool", cb[:, :], cmat_d[:, :], [], [("cb",)])
    dma("sp", gvec[:, :], gvec_d[:, :], [], [("gvec",)])
    for c in range(8):
        dma("sp", hT[:, c, :], xT_v[:, c, :], [], [("h", c, tb) for tb in range(4)])

    def rstd_from(bank, inv_n):
        S.op("act", lambda e: e.activation(rstd[:, :], ps[bank][:, :], AF.Sqrt, bias=EPS, scale=inv_n),
             reads=[("ps", bank)], writes=[("rstd",)])
        S.op("dve", lambda e: e.reciprocal(rstd[:, :], rstd[:, :]),
             reads=[("rstd",)], writes=[("rstd",)])

    def norm_main(tb, gcol, dst, dkeys):
        t0 = tb * 512
        act(sq[:, :, :], hT[:, :, t0:t0 + 512], AF.Square, [("h", c, tb) for c in range(8)], [("sq",)])
        bank = state["PS"].next()
        for c in range(8):
            mm(ps[bank][:, :], ones_bf, sq[:, c, :], c == 0, c == 7, [("sq",), ("cb",)], [("ps", bank)])
        rstd_from(bank, 1.0 / D)
        for c in range(8):
            S.op("dve", (lambda c: lambda e: e.scalar_tensor_tensor(
                dst[:, c, :], hT[:, c, t0:t0 + 512], gvec[:, gcol + c:gcol + c + 1], rstd[:, :],
                ALU.mult, ALU.mult))(c),
                reads=[("h", c, tb), ("rstd",), ("gvec",)], writes=[dkeys(c)])

    def ffn(l, gcol, trailing_barrier=True):
        S.barrier()
        state["PS"] = Rot(range(8))
        m0 = A.mark()
        uThs = [A.alloc((8, 1024), BF16) for _ in range(2)]
        actT = A.alloc((22, 1024), BF16)
        wgu = [A.alloc((2, 8, 256), BF16) for _ in range(3)]
        wdc = [A.alloc((22, 128), BF16) for _ in range(3)]
        sg = [A.alloc((512,), F32) for _ in range(2)]
        wg_v = w_gate_d[l].rearrange("(c p) f -> p c f", p=128)
        wu_v = w_up_d[l].rearrange("(c p) f -> p c f", p=128)
        wd_v = w_down_d[l].rearrange("(c p) d -> p c d", p=128)
        n_w = 0
        n_d = 0
        n_s = 0
        def ffn_norm(th):
            for tb2 in range(2):
                tb = th * 2 + tb2
                norm_main(tb, gcol, uThs[th][:, :, tb2 * 512:(tb2 + 1) * 512], lambda c, tb2=tb2, th=th: ("uTh", th, c, tb2))

        ffn_norm(0)
        for th in range(2):
            uTh = uThs[th]
            for fg in range(11):
                if th == 0 and fg == 3:
                    ffn_norm(1)
                slot = n_w % 3
                n_w += 1
                buf = wgu[slot]
                dma("pool", buf[:, 0, :, :], wg_v[:, :, fg * 256:(fg + 1) * 256], [], [("wgu", slot, 0)])
                dma("pool", buf[:, 1, :, :], wu_v[:, :, fg * 256:(fg + 1) * 256], [], [("wgu", slot, 1)])
                for fc in range(2):
                    f = fg * 2 + fc
                    for tb2 in range(2):
                        ts = slice(tb2 * 512, (tb2 + 1) * 512)
                        bg = state["PS"].next()
                        bu = state["PS"].next()
                        for c in range(8):
                            mm(ps[bg][:, :], buf[:, 0, c, fc * 128:(fc + 1) * 128], uTh[:, c, ts], c == 0, c == 7,
                               [("wgu", slot, 0), ("uTh", th, c, tb2)], [("ps", bg)])
                        for c in range(8):
                            mm(ps[bu][:, :], buf[:, 1, c, fc * 128:(fc + 1) * 128], uTh[:, c, ts], c == 0, c == 7,
                               [("wgu", slot, 1), ("uTh", th, c, tb2)], [("ps", bu)])
                        sgb = sg[n_s % 2]
                        sk = ("sg", n_s % 2)
                        n_s += 1
                        act(sgb[:, :], ps[bg][:, :], AF.Silu, [("ps", bg)], [sk])
                        dve_tt(actT[:, f, ts], ps[bu][:, :], sgb[:, :], ALU.mult, [("ps", bu), sk], [("actT", f, tb2)])
            for dc in range(8):
                slot = n_d % 3
                n_d += 1
                wb = wdc[slot]
                dma("pool", wb[:, 0:11, :], wd_v[:, 0:11, dc * 128:(dc + 1) * 128], [], [("wdc", slot)])
                dma("pool", wb[:, 11:22, :], wd_v[:, 11:22, dc * 128:(dc + 1) * 128], [], [("wdc", slot)])
                for tb2 in range(2):
                    tb = th * 2 + tb2
                    ts = slice(tb2 * 512, (tb2 + 1) * 512)
                    b = state["PS"].next()
                    for f in range(22):
                        mm(ps[b][:, :], wb[:, f, :], actT[:, f, ts], f == 0, f == 21,
                           [("wdc", slot), ("actT", f, tb2)], [("ps", b)])
                    hs = hT[:, dc, tb * 512:(tb + 1) * 512]
                    dve_tt(hs, ps[b][:, :], hs, ALU.add, [("ps", b), ("h", dc, tb)], [("h", dc, tb)])
        if trailing_barrier:
            S.barrier()
        A.release(m0)

    def outproj(w_d, oT):
        wo = A.alloc((8, 1024), BF16)
        for half in range(2):
            dma("pool", wo[:, :, half * 512:(half + 1) * 512], w_d[:, :, half * 512:(half + 1) * 512], [], [("wo", half)])
        for dc in range(8):
            for tb in range(4):
                b = state["PS"].next()
                for kc in range(8):
                    mm(ps[b][:, :], wo[:, kc, dc * 128:(dc + 1) * 128], oT[:, kc, tb * 512:(tb + 1) * 512],
                       kc == 0, kc == 7, [("wo", dc // 4), ("oT", kc, tb)], [("ps", b)])
                hs = hT[:, dc, tb * 512:(tb + 1) * 512]
                dve_tt(hs, ps[b][:, :], hs, ALU.add, [("ps", b), ("h", dc, tb)], [("h", dc, tb)])

    def mix0():
        S.barrier()
        state["PS"] = Rot(range(8))
        m0 = A.mark()
        cqn = A.alloc((3, S_LEN), BF16)
        ckvn = A.alloc((2, S_LEN), BF16)
        krope = A.alloc((S_LEN,), BF16)
        oT = A.alloc((8, S_LEN), BF16)
        mU = A.mark()
        uT = A.alloc((8, S_LEN), BF16)
        mA = A.mark()
        wA = A.alloc((8, 704), BF16)
        CCb = [A.alloc((512,), F32) for _ in range(2)]
        SSb = [A.alloc((512,), F32) for _ in range(2)]
        t1 = A.alloc((512,), F32)
        t2 = A.alloc((512,), F32)
        dma("pool", wA[:, :, 0:672], w_in_d[:, :, 0:672], [], [("wA", 0)])
        dma("pool", wA[:, :, 672:688], w_in_d[:, :, 656:672], [], [("wA", 1)])
        dma("pool", wA[:, :, 688:704], w_in_d[:, :, 640:656], [], [("wA", 2)])
        norm_main(0, G_MIX0, uT[:, :, 0:512], lambda c: ("uT", c, 0))
        for tb in range(4):
            ts = slice(tb * 512, (tb + 1) * 512)
            if tb + 1 < 4:
                norm_main(tb + 1, G_MIX0, uT[:, :, (tb + 1) * 512:(tb + 2) * 512], lambda c, tb=tb: ("uT", c, tb + 1))
            for (nch, col0, gc, dst, dname, invn) in ((3, 0, G_CQ, cqn, "cqn", 1.0 / 384),
                                                       (2, 384, G_CKV, ckvn, "ckvn", 1.0 / 256)):
                banks = []
                for j in range(nch):
                    b = state["PS"].next()
                    banks.append(b)
                    for c in range(8):
                        mm(ps[b][:, :], wA[:, c, col0 + j * 128:col0 + (j + 1) * 128], uT[:, c, ts], c == 0, c == 7,
                           [("wA", 0), ("uT", c, tb)], [("ps", b)])
                    act(sq[:, j, :], ps[b][:, :], AF.Square, [("ps", b)], [("sq",)])
                bs = state["PS"].next()
                for j in range(nch):
                    mm(ps[bs][:, :], ones_bf, sq[:, j, :], j == 0, j == nch - 1, [("sq",), ("cb",)], [("ps", bs)])
                rstd_from(bs, invn)
                for j in range(nch):
                    b = banks[j]
                    S.op("dve", (lambda j, b, dst, gc, ts: lambda e: e.scalar_tensor_tensor(
                        dst[:, j, ts], ps[b][:, :], gvec[:, gc + j:gc + j + 1], rstd[:, :], ALU.mult, ALU.mult))(j, b, dst, gc, ts),
                        reads=[("ps", b), ("rstd",), ("gvec",)], writes=[(dname, j, tb)])
            b = state["PS"].next()
            for c in range(8):
                mm(ps[b][0:64, :], wA[:, c, 640:704], uT[:, c, ts], c == 0, c == 7,
                   [("wA", 0), ("wA", 1), ("wA", 2), ("uT", c, tb)], [("ps", b)])
            cc = CCb[tb % 2]
            ss = SSb[tb % 2]
            dma("sp", cc[:, :], cc_d[:, ts], [], [("CCb", tb % 2)])
            dma("sp", ss[:, :], ss_d[:, ts], [], [("SSb", tb % 2)])
            dve_tt(t1[0:32, :], ps[b][0:32, :], cc[0:32, :], ALU.mult, [("ps", b), ("CCb", tb % 2)], [("t1",)])
            dve_tt(t2[0:32, :], ps[b][32:64, :], ss[32:64, :], ALU.mult, [("ps", b), ("SSb", tb % 2)], [("t2",)])
            dve_tt(krope[0:32, ts], t1[0:32, :], t2[0:32, :], ALU.add, [("t1",), ("t2",)], [("krope", tb)])
        S.barrier()
        A.release(mA)

        if not _SKIP_B:
            mB = A.mark()
            state["PS"] = Rot([0, 1, 2, 3, 4, 5])
            wB = A.alloc((8, 384), BF16)
            qb = A.alloc((S_LEN,), BF16)
            kb = A.alloc((S_LEN,), BF16)
            vb = A.alloc((16, 128), BF16)
            ebuf = [A.alloc((2, 512), F32) for _ in range(1)]
            spb = [A.alloc((2, 512), BF16) for _ in range(3)]
            wbuf = [A.alloc((2, 512), BF16) for _ in range(3)]
            sacc = A.alloc((2, 512), F32)
            saccb = [A.alloc((2, 512), BF16) for _ in range(2)]
            ZZ = [psall[:, q * 1024:(q + 1) * 1024].rearrange("p (a c) -> p a c", a=2) for q in range(3)]
            RR = psall[:, 2048:3072].rearrange("p (a c) -> p a c", a=2)
            pcount = 0
            for j in range(4):
                dma("pool", wB[:, :, 0:128], w_in_d[:, :, 672 + j * 128:672 + (j + 1) * 128], [], [("wB", 0)])
                dma("pool", wB[:, :, 128:256], w_in_d[:, :, 1184 + j * 128:1184 + (j + 1) * 128], [], [("wB", 1)])
                dma("pool", wB[:, :, 256:384], w_in_d[:, :, 1696 + j * 128:1696 + (j + 1) * 128], [], [("wB", 2)])
                for tb in range(4):
                    ts = slice(tb * 512, (tb + 1) * 512)
                    b = state["PS"].next()
                    for c in range(8):
                        mm(ps[b][:, :], wB[:, c, 0:128], uT[:, c, ts], c == 0, c == 7,
                           [("wB", 0), ("uT", c, tb)], [("ps", b)])
                    dve_tss(qb[:, ts], ps[b][:, :], 0.125, ALU.mult, [("ps", b)], [("qb", tb)])
                    b = state["PS"].next()
                    for c in range(8):
                        mm(ps[b][:, :], wB[:, c, 128:256], uT[:, c, ts], c == 0, c == 7,
                           [("wB", 1), ("uT", c, tb)], [("ps", b)])
                    dve_copy(kb[:, ts], ps[b][:, :], [("ps", b)], [("kb", tb)])
                for tq in range(4):
                    b = state["PS"].next()
                    for q in range(4):
                        tile = tq * 4 + q
                        for c in range(8):
                            mm(ps[b][:, q * 128:(q + 1) * 128], uT[:, c, tile * 128:(tile + 1) * 128], wB[:, c, 256:384],
                               c == 0, c == 7, [("wB", 2), ("uT", c, tile // 4)], [("ps", b)])
                    dve_copy(vb[:, tq * 4:(tq + 1) * 4, :], ps[b][:, :].rearrange("p (a b) -> p a b", a=4),
                             [("ps", b)], [("vb", tq)])
                items = []
                for i in range(4):
                    for kt in range(4 * i + 3, -1, -1):
                        c0 = max(0, 128 * kt - 512 * i)
                        c0p = max(0, 128 * (kt + 1) - 512 * i)
                        items.append(dict(i=i, kt=kt, c0=c0, c0p=c0p, n=512 - c0, first=(kt == 4 * i + 3),
                                          last=(kt == 0), diag=(kt >= 4 * i), p=pcount))
                        pcount += 1

                def s1(it):
                    i, kt, c0, n = it["i"], it["kt"], it["c0"], it["n"]
                    zq = it["p"] % 3
                    zk = [("ps", 2 * zq), ("ps", 2 * zq + 1)]
                    for _d in range(N_DUMMY_SB):
                        mm(ps[2 * zq + _d % 2][:, c0:512], ones_bf, uT[:, _d, 512 * i + c0:512 * i + 512], True, True,
                           [("cb",)], [("ps", 2 * zq + _d % 2)])
                    for hh in range(2):
                        pb = hh * 64
                        mm(ps[2 * zq + hh][:, c0:512], kb[pb:pb + 64, kt * 128:(kt + 1) * 128],
                           qb[pb:pb + 64, 512 * i + c0:512 * i + 512], True, True,
                           [("kb", kt // 4), ("qb", i)], [("ps", 2 * zq + hh)])
                    ek = 0
                    sk = it["p"] % 3
                    act(ebuf[ek][:, :, 0:n], ZZ[zq][:, :, c0:512], AF.Exp, zk, [("ebuf", ek)])
                    act(spb[sk][:, :, 0:n], ebuf[ek][:, :, 0:n], AF.Ln, [("ebuf", ek)], [("spb", sk)], bias=1.0)
                    if it["diag"]:
                        for hh in range(2):
                            dve_tt(spb[sk][:, hh, 0:128], spb[sk][:, hh, 0:128], msb_bf, ALU.mult,
                                   [("spb", sk), ("cb",)], [("spb", sk)])

                def s2(it):
                    i, kt, c0, n = it["i"], it["kt"], it["c0"], it["n"]
                    zq = it["p"] % 3
                    zk = [("ps", 2 * zq), ("ps", 2 * zq + 1)]
                    sk3 = it["p"] % 3
                    sk = it["p"] % 3
                    pp = it["p"] % 2
                    for hh in range(2):
                        mm(ps[2 * zq + hh][:, c0:512], tri_bf, spb[sk3][:, hh, 0:n], False, it["first"],
                           [("spb", sk3), ("cb",)], [("ps", 2 * zq + hh)])
                    if not it["first"]:
                        for hh in range(2):
                            mm(ps[2 * zq + hh][:, c0:512], neg_bf, saccb[pp][:, hh, c0:512], False, True,
                               [("saccb", pp), ("cb",)], [("ps", 2 * zq + hh)])
                    act(wbuf[sk][:, :, 0:n], ZZ[zq][:, :, c0:512], AF.Exp, zk, [("wbuf", sk)])
                    if it["diag"]:
                        for hh in range(2):
                            dve_tt(wbuf[sk][:, hh, 0:128], wbuf[sk][:, hh, 0:128], msb_bf, ALU.mult,
                                   [("wbuf", sk), ("cb",)], [("wbuf", sk)])
                    if not it["last"]:
                        if it["first"]:
                            S.op("dve", lambda e: e.memset(sacc[:, :, :], 0.0), reads=[], writes=[("sacc",)])
                        dve_tt(sacc[:, :, c0:512], sacc[:, :, c0:512], spb[sk3][:, :, 0:n], ALU.add,
                               [("sacc",), ("spb", sk3)], [("sacc",)])
                        dve_copy(saccb[1 - pp][:, :, :], sacc[:, :, :], [("sacc",)], [("saccb", 1 - pp)])

                def s3(it):
                    i, kt, c0, n = it["i"], it["kt"], it["c0"], it["n"]
                    sk = it["p"] % 3
                    Ob = 6 + i % 2
                    for hh in range(2):
                        pb = hh * 64
                        mm(ps[Ob][pb:pb + 64, c0:512], vb[:, kt, pb:pb + 64], wbuf[sk][:, hh, 0:n], it["first"], it["last"],
                           [("vb", kt // 4), ("wbuf", sk)], [("ps", Ob)])
                    if it["last"]:
                        dve_copy(oT[:, 4 + j, 512 * i:512 * i + 512], ps[Ob][:, :], [("ps", Ob)], [("oT", 4 + j, i)])

                n_it = len(items)
                L1, L2 = 1, 3
                for step in range(n_it + L2):
                    if step < n_it:
                        s1(items[step])
                    if 0 <= step - L1 < n_it:
                        s2(items[step - L1])
                    if 0 <= step - L2 < n_it:
                        s3(items[step - L2])
            S.barrier()
            A.release(mU)

        if not _SKIP_C:
            mC = A.mark()
            state["PS"] = Rot([0, 1, 2, 3, 6, 7])
            wuq = A.alloc((3, 768), BF16)
            wsw = A.alloc((3, 8, 32), BF16)
            wukv = A.alloc((2, 1024), BF16)
            CC = A.alloc((S_LEN,), F32)
            SS = A.alloc((S_LEN,), F32)
            Qh = [A.alloc((S_LEN,), BF16) for _ in range(2)]
            Kh = [A.alloc((S_LEN,), BF16) for _ in range(2)]
            Vh = [A.alloc((16, 128), BF16) for _ in range(2)]
            pbuf = [A.alloc((512,), BF16) for _ in range(6)]
            rec = A.alloc((512,), F32)
            t1 = A.alloc((512,), F32)
            t2 = A.alloc((512,), F32)
            dma("pool", wuq[:, :, :], w_uq_d[:, :, :], [], [("wuq",)])
            wuq4 = w_uq_d.rearrange("p c (h f) -> p c h f", h=8)
            for kc in range(3):
                dma("pool", wsw[:, kc, :, 0:16], wuq4[:, kc, :, 80:96], [], [("wsw", 0)])
                dma("pool", wsw[:, kc, :, 16:32], wuq4[:, kc, :, 64:80], [], [("wsw", 1)])
            dma("pool", wukv[:, :, :], w_ukv_d[:, :, :], [], [("wukv",)])
            dma("sp", CC[:, :], cc_d[:, :], [], [("CC",)])
            dma("sp", SS[:, :], ss_d[:, :], [], [("SS",)])
            for sl in range(2):
                S.op("dve", (lambda sl: lambda e: e.memset(Vh[sl][:, :, 64:128], 1.0))(sl), reads=[], writes=[("Vh1", sl)])
            gcount = 0
            pcount = 0
            scale_a = 96.0 ** -0.5
            for h in range(8):
                sl = h % 2
                j = h // 2
                pb = (h % 2) * 64
                for tb in range(4):
                    ts = slice(tb * 512, (tb + 1) * 512)
                    bA = state["PS"].next()
                    for kc in range(3):
                        mm(ps[bA][0:96, :], wuq[:, kc, h * 96:(h + 1) * 96], cqn[:, kc, ts], kc == 0, kc == 2,
                           [("wuq",), ("cqn", kc, tb)], [("ps", bA)])
                    bB = state["PS"].next()
                    for kc in range(3):
                        mm(ps[bB][64:96, :], wsw[:, kc, h, :], cqn[:, kc, ts], kc == 0, kc == 2,
                           [("wsw", 0), ("wsw", 1), ("cqn", kc, tb)], [("ps", bB)])
                    act_copy(Qh[sl][0:64, ts], ps[bA][0:64, :], [("ps", bA)], [("Qh", sl, tb)])
                    dve_tt(t1[64:96, :], ps[bA][64:96, :], CC[64:96, ts], ALU.mult, [("ps", bA), ("CC",)], [("t1",)])
                    dve_tt(t2[64:96, :], ps[bB][64:96, :], SS[64:96, ts], ALU.mult, [("ps", bB), ("SS",)], [("t2",)])
                    dve_tt(Qh[sl][64:96, ts], t1[64:96, :], t2[64:96, :], ALU.add, [("t1",), ("t2",)], [("Qh", sl, tb)])
                    bK = state["PS"].next()
                    for kc in range(2):
                        mm(ps[bK][0:64, :], wukv[:, kc, h * 128:h * 128 + 64], ckvn[:, kc, ts], kc == 0, kc == 1,
                           [("wukv",), ("ckvn", kc, tb)], [("ps", bK)])
                    act_copy(Kh[sl][0:64, ts], ps[bK][0:64, :], [("ps", bK)], [("Kh", sl, tb)])
                    pool_copy(Kh[sl][64:96, ts], krope[0:32, ts], [("krope", tb)], [("Kh", sl, tb)])
                for half in range(2):
                    b = state["PS"].next()
                    for q in range(8):
                        tile = half * 8 + q
                        for kc in range(2):
                            mm(ps[b][:, q * 64:(q + 1) * 64], ckvn[:, kc, tile * 128:(tile + 1) * 128],
                               wukv[:, kc, h * 128 + 64:h * 128 + 128], kc == 0, kc == 1,
                               [("wukv",), ("ckvn", kc, tile // 4)], [("ps", b)])
                    dve_copy(Vh[sl][:, half * 8:(half + 1) * 8, 0:64], ps[b][:, :].rearrange("p (a b) -> p a b", a=8),
                             [("ps", b)], [("Vh", sl, half)])
                items = []
                for i in range(4):
                    g = gcount
                    gcount += 1
                    for kt in range(0, 4 * i + 4):
                        c0 = max(0, 128 * kt - 512 * i)
                        items.append(dict(i=i, kt=kt, c0=c0, n=512 - c0, first=(kt == 0), last=(kt == 4 * i + 3),
                                          diag=(kt >= 4 * i), g=g, p=pcount))
                        pcount += 1

                def c1(it):
                    i, kt, c0, n = it["i"], it["kt"], it["c0"], it["n"]
                    zb = state["PS"].next()
                    it["zb"] = zb
                    mm(ps[zb][:, c0:512], Kh[sl][0:96, kt * 128:(kt + 1) * 128], Qh[sl][0:96, 512 * i + c0:512 * i + 512],
                       True, True, [("Kh", sl, kt // 4), ("Qh", sl, i)], [("ps", zb)])
                    pk = it["p"] % 6
                    act(pbuf[pk][:, 0:n], ps[zb][:, c0:512], AF.Exp, [("ps", zb)], [("pbuf", pk)], scale=scale_a)
                    if it["diag"]:
                        dve_tt(pbuf[pk][:, 0:128], pbuf[pk][:, 0:128], mmla_bf, ALU.mult, [("pbuf", pk), ("cb",)], [("pbuf", pk)])

                def c2(it):
                    i, kt, c0, n, g = it["i"], it["kt"], it["c0"], it["n"], it["g"]
                    pk = it["p"] % 6
                    Ob = 4 + g % 2
                    mm(ps[Ob][:, c0:512], Vh[sl][:, kt, :], pbuf[pk][:, 0:n], it["first"], it["last"],
                       [("Vh", sl, kt // 8), ("Vh1", sl), ("pbuf", pk)], [("ps", Ob)])
                    if it["last"]:
                        S.op("dve", (lambda Ob: lambda e: e.reciprocal(rec[64:128, :], ps[Ob][64:128, :]))(Ob),
                             reads=[("ps", Ob)], writes=[("rec",)])
                        dve_tt(oT[pb:pb + 64, j, 512 * i:512 * i + 512], ps[Ob][0:64, :], rec[64:128, :], ALU.mult,
                               [("ps", Ob), ("rec",)], [("oT", j, i)])

                n_it = len(items)
                LC = 4
                for step in range(n_it + LC):
                    if step < n_it:
                        c1(items[step])
                    if 0 <= step - LC < n_it:
                        c2(items[step - LC])
            S.barrier()
            A.release(mC)
        state["PS"] = Rot(range(8))
        outproj(w_eo_d, oT)
        S.barrier()
        A.release(m0)

    def mix1():
        S.barrier()
        state["PS"] = Rot([0, 1, 2, 3])
        m0 = A.mark()
        oT = A.alloc((8, S_LEN), BF16)
        m1 = A.mark()
        uT = A.alloc((8, S_LEN), BF16)
        wq1 = [A.alloc((8, 384), BF16) for _ in range(2)]
        q1 = [A.alloc((S_LEN,), BF16) for _ in range(2)]
        k1 = [A.alloc((S_LEN,), BF16) for _ in range(2)]
        v1 = [A.alloc((16, 2, 128), BF16) for _ in range(2)]
        btab = [A.alloc((640,), F32) for _ in range(2)]
        expB = [A.alloc((640,), BF16) for _ in range(2)]
        mask1 = A.alloc((640,), F32)
        nmask1 = A.alloc((640,), F32)
        pbuf = [A.alloc((512,), BF16) for _ in range(6)]
        rec = A.alloc((512,), F32)
        dma("sp", mask1[:, :], mask1_d[:, :], [], [("mask1",)])
        dma("sp", nmask1[:, :], nmask1_d[:, :], [], [("nmask1",)])
        for sl in range(2):
            S.op("dve", (lambda sl: lambda e: e.memset(v1[sl][:, :, :, 64:128], 1.0))(sl), reads=[], writes=[("v11", sl)])
        for tb in range(4):
            norm_main(tb, G_MIX1, uT[:, :, tb * 512:(tb + 1) * 512], lambda c, tb=tb: ("uT", c, tb))
        gcount = 0
        pcount = 0
        for jp in range(8):
            sl = jp % 2
            for part in range(3):
                dma("pool", wq1[sl][:, :, part * 128:(part + 1) * 128],
                    w_qkv_d[:, :, part * 1024 + jp * 128:part * 1024 + (jp + 1) * 128], [], [("wq1", sl, part)])
            for tb in range(4):
                ts = slice(tb * 512, (tb + 1) * 512)
                b = state["PS"].next()
                for c in range(8):
                    mm(ps[b][:, :], wq1[sl][:, c, 0:128], uT[:, c, ts], c == 0, c == 7,
                       [("wq1", sl, 0), ("uT", c, tb)], [("ps", b)])
                act_mul(q1[sl][:, ts], ps[b][:, :], 0.125, [("ps", b)], [("q1", sl, tb)])
                b = state["PS"].next()
                for c in range(8):
                    mm(ps[b][:, :], wq1[sl][:, c, 128:256], uT[:, c, ts], c == 0, c == 7,
                       [("wq1", sl, 1), ("uT", c, tb)], [("ps", b)])
                act_copy(k1[sl][:, ts], ps[b][:, :], [("ps", b)], [("k1", sl, tb)])
            for tq in range(4):
                b = state["PS"].next()
                for q in range(4):
                    tile = tq * 4 + q
                    for c in range(8):
                        mm(ps[b][:, q * 128:(q + 1) * 128], uT[:, c, tile * 128:(tile + 1) * 128], wq1[sl][:, c, 256:384],
                           c == 0, c == 7, [("wq1", sl, 2), ("uT", c, tile // 4)], [("ps", b)])
                dve_copy(v1[sl][:, tq * 4:(tq + 1) * 4, :, 0:64],
                         ps[b][:, :].rearrange("p (a h b) -> p a h b", a=4, h=2), [("ps", b)], [("v1", sl, tq)])
            for hh in range(2):
                h = 2 * jp + hh
                bs = hh
                dma("sp", btab[bs][:, :], btab_d[h], [], [("btab", bs)])
                dve_tt(btab[bs][:, :], btab[bs][:, :], mask1[:, :], ALU.mult, [("btab", bs), ("mask1",)], [("btab", bs)])
                dve_tt(expB[bs][:, :], btab[bs][:, :], nmask1[:, :], ALU.add, [("btab", bs), ("nmask1",)], [("expB", bs)])
            if True:
                items = []
                for i in range(4):
                    kts = list(range(max(0, 4 * i - 4), 4 * i + 4))
                    for kt in kts:
                        for hh in range(2):
                            tlo = max(128 * kt, 512 * i)
                            thi = min(128 * kt + 640, 512 * i + 512)
                            items.append(dict(i=i, kt=kt, tlo=tlo, thi=thi, n=thi - tlo, tl0=tlo - 128 * kt, hh=hh,
                                              first=(kt == kts[0]), last=(kt == kts[-1]), g=hh, p=pcount))
                            pcount += 1

                def d1(it):
                    pb = it["hh"] * 64
                    bs = it["hh"]
                    i, kt, tlo, thi, n = it["i"], it["kt"], it["tlo"], it["thi"], it["n"]
                    zb = state["PS"].next()
                    it["zb"] = zb
                    for _d in range(N_DUMMY_M1):
                        mm(ps[zb][:, 0:n], ones_bf, uT[:, _d, tlo:thi], True, True, [("cb",)], [("ps", zb)])
                    mm(ps[zb][:, 0:n], k1[sl][pb:pb + 64, kt * 128:(kt + 1) * 128], q1[sl][pb:pb + 64, tlo:thi],
                       True, False, [("k1", sl, kt // 4), ("q1", sl, i)], [("ps", zb)])
                    mm(ps[zb][:, 0:n], ident_bf, expB[bs][:, it["tl0"]:it["tl0"] + n],
                       False, True, [("expB", bs), ("cb",)], [("ps", zb)])
                    pk = it["p"] % 6
                    act(pbuf[pk][:, 0:n], ps[zb][:, 0:n], AF.Exp, [("ps", zb)], [("pbuf", pk)])

                def d2(it):
                    hh = it["hh"]
                    pb = hh * 64
                    i, kt, tlo, thi, n, g = it["i"], it["kt"], it["tlo"], it["thi"], it["n"], it["g"]
                    pk = it["p"] % 6
                    Ob = 4 + hh + 2 * (i % 2)
                    mm(ps[Ob][:, tlo - 512 * i:thi - 512 * i], v1[sl][:, kt, hh, :], pbuf[pk][:, 0:n], it["first"], it["last"],
                       [("v1", sl, kt // 4), ("v11", sl), ("pbuf", pk)], [("ps", Ob)])
                    if it["last"]:
                        S.op("dve", (lambda Ob: lambda e: e.reciprocal(rec[64:128, :], ps[Ob][64:128, :]))(Ob),
                             reads=[("ps", Ob)], writes=[("rec",)])
                        dve_tt(oT[pb:pb + 64, jp, 512 * i:512 * i + 512], ps[Ob][0:64, :], rec[64:128, :], ALU.mult,
                               [("ps", Ob), ("rec",)], [("oT", jp, i)])

                n_it = len(items)
                LD = 4
                for step in range(n_it + LD):
                    if step < n_it:
                        d1(items[step])
                    if 0 <= step - LD < n_it:
                        d2(items[step - LD])
        S.barrier()
        A.release(m1)
        state["PS"] = Rot(range(8))
        outproj(w_oo_d, oT)
        S.barrier()
        A.release(m0)

    for st in stages:
        if st == "mix0":
            mix0()
        elif st == "ffn0":
            ffn(0, G_FFN0)
        elif st == "mix1":
            mix1()
        elif st == "ffn1":
            ffn(1, G_FFN1, trailing_barrier=(st != stages[-1]))
    if not (stages and stages[-1] == "ffn1"):
        S.barrier()
        state["PS"] = Rot(range(8))
    for tb in range(4):
        t0 = tb * 512
        so = hT[:, :, t0:t0 + 512]
        if final_norm:
            act(sq[:, :, :], hT[:, :, t0:t0 + 512], AF.Square, [("h", c, tb) for c in range(8)], [("sq",)])
            bank = state["PS"].next()
            for c in range(8):
                mm(ps[bank][:, :], ones_bf, sq[:, c, :], c == 0, c == 7, [("sq",), ("cb",)], [("ps", bank)])
            rstd_from(bank, 1.0 / D)
            for c in range(8):
                S.op("dve", (lambda c, so, t0: lambda e: e.scalar_tensor_tensor(
                    so[:, c, :], hT[:, c, t0:t0 + 512], gvec[:, G_FINAL + c:G_FINAL + c + 1], rstd[:, :],
                    ALU.mult, ALU.mult))(c, so, t0),
                    reads=[("h", c, tb), ("rstd",), ("gvec",)], writes=[("h", c, tb)])
            tok = dma("sp", outT_v[:, :, t0:t0 + 512], so, [("h", c, tb) for c in range(8)], [("out", tb)])
        else:
            tok = dma("sp", outT_v[:, :, t0:t0 + 512], hT[:, :, t0:t0 + 512], [("h", c, tb) for c in range(8)], [("out", tb)])
        S.final_dma.append(tok)

    S.finalize(nc)
    with nc.Block() as block:
        @block.tensor
        def _(e):
            S.emit("pe", e)

        @block.scalar
        def _(e):
            S.emit("act", e)

        @block.vector
        def _(e):
            S.emit("dve", e)

        @block.gpsimd
        def _(e):
            S.emit("pool", e)

        @block.sync
        def _(e):
            S.emit("sp", e)
    S.close()
    nc._arena_peak = A.peak if False else None
    return nc


def _consts():
    s = np.arange(128)[:, None]
    t = np.arange(128)[None, :]
    cm = np.zeros((128, NCM), np.float32)
    cm[:, C_ONES:C_ONES + 128] = 1.0
    cm[:, C_NEG:C_NEG + 128] = -1.0
    cm[:, C_TRI:C_TRI + 128] = np.where(s >= t, -1.0, 0.0)
    cm[:, C_MSB:C_MSB + 128] = np.where(s < t, 1.0, 0.0)
    cm[:, C_MMLA:C_MMLA + 128] = np.where((s // 64) <= (t // 64), 1.0, 0.0)
    cm[:, C_ID:C_ID + 128] = np.eye(128, dtype=np.float32)
    pos = np.arange(S_LEN, dtype=np.float32)
    inv_freq = (np.float32(10000.0) ** (-np.arange(0, 32, 2, dtype=np.float32) / np.float32(32))).astype(np.float32)
    ang = (pos[:, None] * inv_freq[None, :]).astype(np.float32)
    cos = np.cos(ang).astype(np.float32).T
    sin = np.sin(ang).astype(np.float32).T
    cc = np.concatenate([cos, cos], 0)
    ss = np.concatenate([-sin, sin], 0)
    cc4 = np.ascontiguousarray(np.tile(cc, (4, 1)))
    ss4 = np.ascontiguousarray(np.tile(ss, (4, 1)))
    sl = np.arange(128)[:, None]
    tl = np.arange(640)[None, :]
    d = tl // 64 - sl // 64
    mask1 = ((d >= 0) & (d <= 8)).astype(np.float32)
    bidx = np.clip(tl - sl, -256, 256) + 256
    return cm, cc4, ss4, mask1, bidx


def _gcol(g):
    return np.ascontiguousarray(np.asarray(g, np.float32).reshape(-1, 128).T)


_CACHE = {}


def run(inputs, stages=("mix0", "ffn0", "mix1", "ffn1"), final_norm=True, trace=False):
    f = lambda a: np.ascontiguousarray(np.asarray(a, dtype=np.float32))
    x = f(inputs["x"])
    cm, cc4, ss4, mask1, bidx = _consts()
    gvec = np.concatenate([
        _gcol(inputs["g_mix"][0]), _gcol(inputs["g_ffn"][0]), _gcol(inputs["g_mix"][1]), _gcol(inputs["g_ffn"][1]),
        _gcol(inputs["g_final"]), _gcol(inputs["ev_g_cq"][0]), _gcol(inputs["ev_g_ckv"][0])], axis=1)
    gvec = np.ascontiguousarray(gvec.astype(np.float32))
    assert gvec.shape == (128, NG)
    rb = f(inputs["od_rel_bias"])[0]
    bias_tab = np.ascontiguousarray(rb[:, bidx])
    shared = {
        "ev_w_in": f(inputs["ev_w_in"])[0], "ev_w_uq": f(inputs["ev_w_uq"])[0], "ev_w_ukv": f(inputs["ev_w_ukv"])[0],
        "ev_w_out": f(inputs["ev_w_out"])[0], "od_w_qkv": f(inputs["od_w_qkv"])[0], "od_w_out": f(inputs["od_w_out"])[0],
        "w_gate": f(inputs["w_gate"]), "w_up": f(inputs["w_up"]), "w_down": f(inputs["w_down"]),
        "gvec": gvec, "cmat": cm, "rope_cc": cc4, "rope_ss": ss4, "bias_tab": bias_tab, "mask1": mask1,
        "nmask1": np.ascontiguousarray((mask1 - 1.0) * 30000.0).astype(np.float32),
    }
    key = (tuple(stages), final_norm)
    if key not in _CACHE:
        _CACHE[key] = build(stages, final_norm)
    nc = _CACHE[key]
    in_maps = []
    for b in range(NCORES):
        m = dict(shared)
        m["xT"] = np.ascontiguousarray(x[b].T)
        in_maps.append(m)
    res = run_bass_kernel_spmd(nc, in_maps, core_ids=list(range(NCORES)), **({"trace": True} if trace else {}))
    out = np.stack([np.ascontiguousarray(np.asarray(r["outT"]).T) for r in res.results], axis=0)
    return out.astype(np.float32), res


def kernel(**inputs):
    out, _ = run(inputs)
    return out
```

```python
import numpy as np
import concourse.bass as bass
import concourse.mybir as mybir
from concourse.bass_utils import run_bass_kernel_spmd

F32 = mybir.dt.float32
BF16 = mybir.dt.bfloat16
U8 = mybir.dt.uint8
ALU = mybir.AluOpType
AF = mybir.ActivationFunctionType

import os as _os
_SKIP_B = bool(_os.environ.get("K_SKIP_B"))
N_DUMMY_SB = int(_os.environ.get("K_DUMMY_SB", "3"))
N_DUMMY_M1 = int(_os.environ.get("K_DUMMY_M1", "1"))
_SKIP_C = bool(_os.environ.get("K_SKIP_C"))
S_LEN = 2048
D = 1024
DFF = 2816
EPS = 1e-6
NCORES = 8
ENGS = ("pe", "act", "dve", "pool", "sp")
NDMA = 24
NDMA_POOL = 16
SEM_LIMIT = 3000
NSEM_PER_ENG = 10


DEBUG_NAMES = {}


class Op:
    __slots__ = ("eng", "fn", "deps", "dmadeps", "marked", "sem", "cnt", "idx", "dma", "tag")

    def __init__(self, eng, fn):
        self.eng = eng
        self.fn = fn
        self.deps = {}
        self.dmadeps = {}
        self.marked = False
        self.sem = None
        self.cnt = 0
        self.idx = -1
        self.dma = None


class Sched:
    def __init__(self):
        self.ops = {e: [] for e in ENGS}
        self.last_w = {}
        self.readers = {}
        self.dma_use = [0] * NDMA
        self.dma_rr = 0
        self.dma_rr_pool = 0
        self.pending = {e: [] for e in ENGS}
        self.final_dma = []

    def _add_dep(self, o, tok):
        if tok is None:
            return
        if not isinstance(tok, tuple) and tok.dma is not None:
            tok = tok.dma
        if isinstance(tok, tuple):
            i, v = tok
            if o.dmadeps.get(i, 0) < v:
                o.dmadeps[i] = v
            return
        if tok.eng == o.eng and o.eng in ("pe", "sp"):
            return
        cur = o.deps.get(tok.eng)
        if cur is None or cur.idx < tok.idx:
            o.deps[tok.eng] = tok

    def op(self, eng, fn, reads=(), writes=(), dma=False, raw_same_only=True):
        o = Op(eng, fn)
        o.tag = (eng, tuple(reads), tuple(writes))
        o.idx = len(self.ops[eng])
        for tok in self.pending[eng]:
            self._add_dep(o, tok)
        self.pending[eng] = []
        for k in reads:
            self._add_dep(o, self.last_w.get(k))
        for k in writes:
            self._add_dep(o, self.last_w.get(k))
            for r in self.readers.get(k, ()):
                self._add_dep(o, r)
        tok = o
        if dma:
            if eng == "pool":
                i = self.dma_rr_pool
                self.dma_rr_pool = (self.dma_rr_pool + 1) % NDMA_POOL
            else:
                i = NDMA_POOL + self.dma_rr
                self.dma_rr = (self.dma_rr + 1) % (NDMA - NDMA_POOL)
            if self.dma_use[i] > 0:
                self._add_dep(o, (i, 16 * self.dma_use[i]))
            self.dma_use[i] += 1
            o.dma = (i, 16 * self.dma_use[i])
            tok = o.dma
        for k in reads:
            self.readers.setdefault(k, []).append(tok)
        for k in writes:
            self.last_w[k] = tok
            self.readers[k] = []
        self.ops[eng].append(o)
        return tok

    def barrier(self):
        toks = []
        for e in ENGS:
            if self.ops[e]:
                toks.append(self.ops[e][-1])
        for i in range(NDMA):
            if self.dma_use[i] > 0:
                toks.append((i, 16 * self.dma_use[i]))
        for e in ENGS:
            self.pending[e] = list(toks)
        self.last_w = {}
        self.readers = {}

    def finalize(self, nc):
        for e in ENGS:
            for o in self.ops[e]:
                for d in o.deps.values():
                    d.marked = True
        esems = {}
        self._ctx = []
        for e in ENGS:
            n = sum(1 for o in self.ops[e] if o.marked)
            need = max(1, (n + SEM_LIMIT - 1) // SEM_LIMIT)
            assert need <= NSEM_PER_ENG, (e, n)
            lst = []
            for k in range(need):
                cm = nc.semaphore(f"s_{e}_{k}")
                lst.append(cm.__enter__())
                self._ctx.append(cm)
            esems[e] = lst
            c = 0
            for o in self.ops[e]:
                if o.marked:
                    o.sem = lst[c // SEM_LIMIT]
                    o.cnt = c % SEM_LIMIT + 1
                    c += 1
        dsems = []
        for i in range(NDMA):
            cm = nc.semaphore(f"s_dma_{i}")
            dsems.append(cm.__enter__())
            self._ctx.append(cm)
        self.dsems = dsems

    def emit(self, eng_name, engobj):
        waited = {}
        dwaited = {}
        for o in self.ops[eng_name]:
            for src, d in o.deps.items():
                if waited.get(src, -1) >= d.idx:
                    continue
                engobj.wait_ge(d.sem, d.cnt)
                waited[src] = d.idx
            for i, v in o.dmadeps.items():
                if dwaited.get(i, 0) >= v:
                    continue
                engobj.wait_ge(self.dsems[i], v)
                dwaited[i] = v
            ins = o.fn(engobj)
            try:
                DEBUG_NAMES[ins.ins.name] = o.tag
            except Exception:
                pass
            if o.dma is not None:
                ins.then_inc(self.dsems[o.dma[0]], 16)
            elif o.marked:
                ins.then_inc(o.sem, 1)
        if eng_name == "sp":
            for (i, v) in self.final_dma:
                if dwaited.get(i, 0) < v:
                    engobj.wait_ge(self.dsems[i], v)
                    dwaited[i] = v

    def close(self):
        for cm in reversed(self._ctx):
            cm.__exit__(None, None, None)


class Arena:
    def __init__(self, nc, nbytes):
        self.t = nc.alloc_sbuf_tensor("arena", [128, nbytes], U8)
        self.n = nbytes
        self.off = 0
        self.peak = 0

    def alloc(self, shape, dtype):
        esz = 4 if dtype == F32 else 2
        n = 1
        for s in shape:
            n *= s
        nb = n * esz
        off = (self.off + 63) // 64 * 64
        assert off + nb <= self.n, ("arena overflow", off, nb, self.n)
        self.off = off + nb
        self.peak = max(self.peak, self.off)
        v = self.t[:, off:off + nb].bitcast(dtype)
        if len(shape) == 2:
            v = v.rearrange("p (a b) -> p a b", a=shape[0])
        elif len(shape) == 3:
            v = v.rearrange("p (a b c) -> p a b c", a=shape[0], b=shape[1])
        return v

    def mark(self):
        return self.off

    def release(self, m):
        self.off = m


class Rot:
    def __init__(self, items):
        self.items = list(items)
        self.i = 0

    def next(self):
        v = self.items[self.i % len(self.items)]
        self.i += 1
        return v


G_MIX0, G_FFN0, G_MIX1, G_FFN1, G_FINAL, G_CQ, G_CKV, NG = 0, 8, 16, 24, 32, 40, 43, 45
C_ONES, C_NEG, C_TRI, C_MSB, C_MMLA, C_ID, NCM = 0, 128, 256, 384, 512, 640, 768


def build(stages=("mix0", "ffn0", "mix1", "ffn1"), final_norm=True):
    nc = bass.Bass("TRN2", target_bir_lowering=False)

    def din(name, shape):
        return nc.dram_tensor(name, list(shape), F32, kind="ExternalInput").ap()

    xT_d = din("xT", [D, S_LEN])
    w_in_d = din("ev_w_in", [D, 2208]).rearrange("(c p) n -> p c n", p=128)
    w_uq_d = din("ev_w_uq", [384, 768]).rearrange("(c p) n -> p c n", p=128)
    w_ukv_d = din("ev_w_ukv", [256, 1024]).rearrange("(c p) n -> p c n", p=128)
    w_eo_d = din("ev_w_out", [D, D]).rearrange("(c p) n -> p c n", p=128)
    w_qkv_d = din("od_w_qkv", [D, 3072]).rearrange("(c p) n -> p c n", p=128)
    w_oo_d = din("od_w_out", [D, D]).rearrange("(c p) n -> p c n", p=128)
    w_gate_d = din("w_gate", [2, D, DFF])
    w_up_d = din("w_up", [2, D, DFF])
    w_down_d = din("w_down", [2, DFF, D])
    gvec_d = din("gvec", [128, NG])
    cmat_d = din("cmat", [128, NCM])
    cc_d = din("rope_cc", [128, S_LEN])
    ss_d = din("rope_ss", [128, S_LEN])
    btab_d = din("bias_tab", [16, 128, 640])
    mask1_d = din("mask1", [128, 640])
    nmask1_d = din("nmask1", [128, 640])
    outT_d = nc.dram_tensor("outT", [D, S_LEN], F32, kind="ExternalOutput").ap()
    outT_v = outT_d.rearrange("(c p) t -> p c t", p=128)
    xT_v = xT_d.rearrange("(c p) t -> p c t", p=128)

    S = Sched()
    A = Arena(nc, 211968)
    psall = nc.alloc_psum_tensor("psall", [128, 4096], F32)
    ps = [psall[:, b * 512:(b + 1) * 512] for b in range(8)]

    hT = A.alloc((8, S_LEN), F32)
    cb = A.alloc((NCM,), BF16)
    gvec = A.alloc((NG,), F32)
    sq = A.alloc((8, 512), BF16)
    rstd = A.alloc((512,), F32)
    ones_bf = cb[:, C_ONES:C_ONES + 128]
    neg_bf = cb[:, C_NEG:C_NEG + 128]
    tri_bf = cb[:, C_TRI:C_TRI + 128]
    msb_bf = cb[:, C_MSB:C_MSB + 128]
    mmla_bf = cb[:, C_MMLA:C_MMLA + 128]
    ident_bf = cb[:, C_ID:C_ID + 128]

    PS = Rot(range(8))
    state = {"PS": PS}

    def mm(out, lhsT, rhs, start, stop, reads, writes):
        S.op("pe", lambda e: e.matmul(out, lhsT, rhs, start=start, stop=stop, skip_group_check=True),
             reads=reads, writes=writes)

    def dve_tt(out, in0, in1, op, reads, writes):
        S.op("dve", lambda e: e.tensor_tensor(out, in0, in1, op), reads=reads, writes=writes)

    def dve_tss(out, in_, scalar, op, reads, writes):
        S.op("dve", lambda e: e.tensor_single_scalar(out, in_, scalar, op), reads=reads, writes=writes)

    def dve_copy(out, in_, reads, writes):
        S.op("dve", lambda e: e.tensor_copy(out, in_), reads=reads, writes=writes)

    def act(out, in_, func, reads, writes, bias=0.0, scale=1.0):
        S.op("act", lambda e: e.activation(out, in_, func, bias=bias, scale=scale), reads=reads, writes=writes)

    def act_copy(out, in_, reads, writes):
        S.op("act", lambda e: e.copy(out, in_), reads=reads, writes=writes)

    def act_mul(out, in_, m, reads, writes):
        S.op("act", lambda e: e.mul(out, in_, m), reads=reads, writes=writes)

    def pool_copy(out, in_, reads, writes):
        S.op("pool", lambda e: e.tensor_copy(out, in_), reads=reads, writes=writes)

    def dma(eng, out, in_, reads, writes):
        return S.op(eng, lambda e: e.dma_start(out=out, in_=in_), reads=reads, writes=writes, dma=True)

    dma("pool", cb[:, :], cmat_d[:, :], [], [("cb",)])
    dma("sp", gvec[:, :], gvec_d[:, :], [], [("gvec",)])
    for c in range(8):
        dma("sp", hT[:, c, :], xT_v[:, c, :], [], [("h", c, tb) for tb in range(4)])

    def rstd_from(bank, inv_n):
        S.op("act", lambda e: e.activation(rstd[:, :], ps[bank][:, :], AF.Sqrt, bias=EPS, scale=inv_n),
             reads=[("ps", bank)], writes=[("rstd",)])
        S.op("dve", lambda e: e.reciprocal(rstd[:, :], rstd[:, :]),
             reads=[("rstd",)], writes=[("rstd",)])

    def norm_main(tb, gcol, dst, dkeys):
        t0 = tb * 512
        act(sq[:, :, :], hT[:, :, t0:t0 + 512], AF.Square, [("h", c, tb) for c in range(8)], [("sq",)])
        bank = state["PS"].next()
        for c in range(8):
            mm(ps[bank][:, :], ones_bf, sq[:, c, :], c == 0, c == 7, [("sq",), ("cb",)], [("ps", bank)])
        rstd_from(bank, 1.0 / D)
        for c in range(8):
            S.op("dve", (lambda c: lambda e: e.scalar_tensor_tensor(
                dst[:, c, :], hT[:, c, t0:t0 + 512], gvec[:, gcol + c:gcol + c + 1], rstd[:, :],
                ALU.mult, ALU.mult))(c),
                reads=[("h", c, tb), ("rstd",), ("gvec",)], writes=[dkeys(c)])

    def ffn(l, gcol, trailing_barrier=True):
        S.barrier()
        state["PS"] = Rot(range(8))
        m0 = A.mark()
        uThs = [A.alloc((8, 1024), BF16) for _ in range(2)]
        actT = A.alloc((22, 1024), BF16)
        wgu = [A.alloc((2, 8, 256), BF16) for _ in range(3)]
        wdc = [A.alloc((22, 128), BF16) for _ in range(3)]
        sg = [A.alloc((512,), F32) for _ in range(2)]
        wg_v = w_gate_d[l].rearrange("(c p) f -> p c f", p=128)
        wu_v = w_up_d[l].rearrange("(c p) f -> p c f", p=128)
        wd_v = w_down_d[l].rearrange("(c p) d -> p c d", p=128)
        n_w = 0
        n_d = 0
        n_s = 0
        def ffn_norm(th):
            for tb2 in range(2):
                tb = th * 2 + tb2
                norm_main(tb, gcol, uThs[th][:, :, tb2 * 512:(tb2 + 1) * 512], lambda c, tb2=tb2, th=th: ("uTh", th, c, tb2))

        ffn_norm(0)
        for th in range(2):
            uTh = uThs[th]
            for fg in range(11):
                if th == 0 and fg == 3:
                    ffn_norm(1)
                slot = n_w % 3
                n_w += 1
                buf = wgu[slot]
                dma("pool", buf[:, 0, :, :], wg_v[:, :, fg * 256:(fg + 1) * 256], [], [("wgu", slot, 0)])
                dma("pool", buf[:, 1, :, :], wu_v[:, :, fg * 256:(fg + 1) * 256], [], [("wgu", slot, 1)])
                for fc in range(2):
                    f = fg * 2 + fc
                    for tb2 in range(2):
                        ts = slice(tb2 * 512, (tb2 + 1) * 512)
                        bg = state["PS"].next()
                        bu = state["PS"].next()
                        for c in range(8):
                            mm(ps[bg][:, :], buf[:, 0, c, fc * 128:(fc + 1) * 128], uTh[:, c, ts], c == 0, c == 7,
                               [("wgu", slot, 0), ("uTh", th, c, tb2)], [("ps", bg)])
                        for c in range(8):
                            mm(ps[bu][:, :], buf[:, 1, c, fc * 128:(fc + 1) * 128], uTh[:, c, ts], c == 0, c == 7,
                               [("wgu", slot, 1), ("uTh", th, c, tb2)], [("ps", bu)])
                        sgb = sg[n_s % 2]
                        sk = ("sg", n_s % 2)
                        n_s += 1
                        act(sgb[:, :], ps[bg][:, :], AF.Silu, [("ps", bg)], [sk])
                        dve_tt(actT[:, f, ts], ps[bu][:, :], sgb[:, :], ALU.mult, [("ps", bu), sk], [("actT", f, tb2)])
            for dc in range(8):
                slot = n_d % 3
                n_d += 1
                wb = wdc[slot]
                dma("pool", wb[:, 0:11, :], wd_v[:, 0:11, dc * 128:(dc + 1) * 128], [], [("wdc", slot)])
                dma("pool", wb[:, 11:22, :], wd_v[:, 11:22, dc * 128:(dc + 1) * 128], [], [("wdc", slot)])
                for tb2 in range(2):
                    tb = th * 2 + tb2
                    ts = slice(tb2 * 512, (tb2 + 1) * 512)
                    b = state["PS"].next()
                    for f in range(22):
                        mm(ps[b][:, :], wb[:, f, :], actT[:, f, ts], f == 0, f == 21,
                           [("wdc", slot), ("actT", f, tb2)], [("ps", b)])
                    hs = hT[:, dc, tb * 512:(tb + 1) * 512]
                    dve_tt(hs, ps[b][:, :], hs, ALU.add, [("ps", b), ("h", dc, tb)], [("h", dc, tb)])
        if trailing_barrier:
            S.barrier()
        A.release(m0)

    def outproj(w_d, oT):
        wo = A.alloc((8, 1024), BF16)
        for half in range(2):
            dma("pool", wo[:, :, half * 512:(half + 1) * 512], w_d[:, :, half * 512:(half + 1) * 512], [], [("wo", half)])
        for dc in range(8):
            for tb in range(4):
                b = state["PS"].next()
                for kc in range(8):
                    mm(ps[b][:, :], wo[:, kc, dc * 128:(dc + 1) * 128], oT[:, kc, tb * 512:(tb + 1) * 512],
                       kc == 0, kc == 7, [("wo", dc // 4), ("oT", kc, tb)], [("ps", b)])
                hs = hT[:, dc, tb * 512:(tb + 1) * 512]
                dve_tt(hs, ps[b][:, :], hs, ALU.add, [("ps", b), ("h", dc, tb)], [("h", dc, tb)])

    def mix0():
        S.barrier()
        state["PS"] = Rot(range(8))
        m0 = A.mark()
        cqn = A.alloc((3, S_LEN), BF16)
        ckvn = A.alloc((2, S_LEN), BF16)
        krope = A.alloc((S_LEN,), BF16)
        oT = A.alloc((8, S_LEN), BF16)
        mU = A.mark()
        uT = A.alloc((8, S_LEN), BF16)
        mA = A.mark()
        wA = A.alloc((8, 704), BF16)
        CCb = [A.alloc((512,), F32) for _ in range(2)]
        SSb = [A.alloc((512,), F32) for _ in range(2)]
        t1 = A.alloc((512,), F32)
        t2 = A.alloc((512,), F32)
        dma("pool", wA[:, :, 0:672], w_in_d[:, :, 0:672], [], [("wA", 0)])
        dma("pool", wA[:, :, 672:688], w_in_d[:, :, 656:672], [], [("wA", 1)])
        dma("pool", wA[:, :, 688:704], w_in_d[:, :, 640:656], [], [("wA", 2)])
        norm_main(0, G_MIX0, uT[:, :, 0:512], lambda c: ("uT", c, 0))
        for tb in range(4):
            ts = slice(tb * 512, (tb + 1) * 512)
            if tb + 1 < 4:
                norm_main(tb + 1, G_MIX0, uT[:, :, (tb + 1) * 512:(tb + 2) * 512], lambda c, tb=tb: ("uT", c, tb + 1))
            for (nch, col0, gc, dst, dname, invn) in ((3, 0, G_CQ, cqn, "cqn", 1.0 / 384),
                                                       (2, 384, G_CKV, ckvn, "ckvn", 1.0 / 256)):
                banks = []
                for j in range(nch):
                    b = state["PS"].next()
                    banks.append(b)
                    for c in range(8):
                        mm(ps[b][:, :], wA[:, c, col0 + j * 128:col0 + (j + 1) * 128], uT[:, c, ts], c == 0, c == 7,
                           [("wA", 0), ("uT", c, tb)], [("ps", b)])
                    act(sq[:, j, :], ps[b][:, :], AF.Square, [("ps", b)], [("sq",)])
                bs = state["PS"].next()
                for j in range(nch):
                    mm(ps[bs][:, :], ones_bf, sq[:, j, :], j == 0, j == nch - 1, [("sq",), ("cb",)], [("ps", bs)])
                rstd_from(bs, invn)
                for j in range(nch):
                    b = banks[j]
                    S.op("dve", (lambda j, b, dst, gc, ts: lambda e: e.scalar_tensor_tensor(
                        dst[:, j, ts], ps[b][:, :], gvec[:, gc + j:gc + j + 1], rstd[:, :], ALU.mult, ALU.mult))(j, b, dst, gc, ts),
                        reads=[("ps", b), ("rstd",), ("gvec",)], writes=[(dname, j, tb)])
            b = state["PS"].next()
            for c in range(8):
                mm(ps[b][0:64, :], wA[:, c, 640:704], uT[:, c, ts], c == 0, c == 7,
                   [("wA", 0), ("wA", 1), ("wA", 2), ("uT", c, tb)], [("ps", b)])
            cc = CCb[tb % 2]
            ss = SSb[tb % 2]
            dma("sp", cc[:, :], cc_d[:, ts], [], [("CCb", tb % 2)])
            dma("sp", ss[:, :], ss_d[:, ts], [], [("SSb", tb % 2)])
            dve_tt(t1[0:32, :], ps[b][0:32, :], cc[0:32, :], ALU.mult, [("ps", b), ("CCb", tb % 2)], [("t1",)])
            dve_tt(t2[0:32, :], ps[b][32:64, :], ss[32:64, :], ALU.mult, [("ps", b), ("SSb", tb % 2)], [("t2",)])
            dve_tt(krope[0:32, ts], t1[0:32, :], t2[0:32, :], ALU.add, [("t1",), ("t2",)], [("krope", tb)])
        S.barrier()
        A.release(mA)

        if not _SKIP_B:
            mB = A.mark()
            state["PS"] = Rot([0, 1, 2, 3, 4, 5])
            wB = A.alloc((8, 384), BF16)
            qb = A.alloc((S_LEN,), BF16)
            kb = A.alloc((S_LEN,), BF16)
            vb = A.alloc((16, 128), BF16)
            ebuf = [A.alloc((2, 512), F32) for _ in range(1)]
            spb = [A.alloc((2, 512), BF16) for _ in range(3)]
            wbuf = [A.alloc((2, 512), BF16) for _ in range(3)]
            sacc = A.alloc((2, 512), F32)
            saccb = [A.alloc((2, 512), BF16) for _ in range(2)]
            ZZ = [psall[:, q * 1024:(q + 1) * 1024].rearrange("p (a c) -> p a c", a=2) for q in range(3)]
            RR = psall[:, 2048:3072].rearrange("p (a c) -> p a c", a=2)
            pcount = 0
            for j in range(4):
                dma("pool", wB[:, :, 0:128], w_in_d[:, :, 672 + j * 128:672 + (j + 1) * 128], [], [("wB", 0)])
                dma("pool", wB[:, :, 128:256], w_in_d[:, :, 1184 + j * 128:1184 + (j + 1) * 128], [], [("wB", 1)])
                dma("pool", wB[:, :, 256:384], w_in_d[:, :, 1696 + j * 128:1696 + (j + 1) * 128], [], [("wB", 2)])
                for tb in range(4):
                    ts = slice(tb * 512, (tb + 1) * 512)
                    b = state["PS"].next()
                    for c in range(8):
                        mm(ps[b][:, :], wB[:, c, 0:128], uT[:, c, ts], c == 0, c == 7,
                           [("wB", 0), ("uT", c, tb)], [("ps", b)])
                    dve_tss(qb[:, ts], ps[b][:, :], 0.125, ALU.mult, [("ps", b)], [("qb", tb)])
                    b = state["PS"].next()
                    for c in range(8):
                        mm(ps[b][:, :], wB[:, c, 128:256], uT[:, c, ts], c == 0, c == 7,
                           [("wB", 1), ("uT", c, tb)], [("ps", b)])
                    dve_copy(kb[:, ts], ps[b][:, :], [("ps", b)], [("kb", tb)])
                for tq in range(4):
                    b = state["PS"].next()
                    for q in range(4):
                        tile = tq * 4 + q
                        for c in range(8):
                            mm(ps[b][:, q * 128:(q + 1) * 128], uT[:, c, tile * 128:(tile + 1) * 128], wB[:, c, 256:384],
                               c == 0, c == 7, [("wB", 2), ("uT", c, tile // 4)], [("ps", b)])
                    dve_copy(vb[:, tq * 4:(tq + 1) * 4, :], ps[b][:, :].rearrange("p (a b) -> p a b", a=4),
                             [("ps", b)], [("vb", tq)])
                items = []
                for i in range(4):
                    for kt in range(4 * i + 3, -1, -1):
                        c0 = max(0, 128 * kt - 512 * i)
                        c0p = max(0, 128 * (kt + 1) - 512 * i)
                        items.append(dict(i=i, kt=kt, c0=c0, c0p=c0p, n=512 - c0, first=(kt == 4 * i + 3),
                                          last=(kt == 0), diag=(kt >= 4 * i), p=pcount))
                        pcount += 1

                def s1(it):
                    i, kt, c0, n = it["i"], it["kt"], it["c0"], it["n"]
                    zq = it["p"] % 3
                    zk = [("ps", 2 * zq), ("ps", 2 * zq + 1)]
                    for _d in range(N_DUMMY_SB):
                        mm(ps[2 * zq + _d % 2][:, c0:512], ones_bf, uT[:, _d, 512 * i + c0:512 * i + 512], True, True,
                           [("cb",)], [("ps", 2 * zq + _d % 2)])
                    for hh in range(2):
                        pb = hh * 64
                        mm(ps[2 * zq + hh][:, c0:512], kb[pb:pb + 64, kt * 128:(kt + 1) * 128],
                           qb[pb:pb + 64, 512 * i + c0:512 * i + 512], True, True,
                           [("kb", kt // 4), ("qb", i)], [("ps", 2 * zq + hh)])
                    ek = 0
                    sk = it["p"] % 3
                    act(ebuf[ek][:, :, 0:n], ZZ[zq][:, :, c0:512], AF.Exp, zk, [("ebuf", ek)])
                    act(spb[sk][:, :, 0:n], ebuf[ek][:, :, 0:n], AF.Ln, [("ebuf", ek)], [("spb", sk)], bias=1.0)
                    if it["diag"]:
                        for hh in range(2):
                            dve_tt(spb[sk][:, hh, 0:128], spb[sk][:, hh, 0:128], msb_bf, ALU.mult,
                                   [("spb", sk), ("cb",)], [("spb", sk)])

                def s2(it):
                    i, kt, c0, n = it["i"], it["kt"], it["c0"], it["n"]
                    zq = it["p"] % 3
                    zk = [("ps", 2 * zq), ("ps", 2 * zq + 1)]
                    sk3 = it["p"] % 3
                    sk = it["p"] % 3
                    pp = it["p"] % 2
                    for hh in range(2):
                        mm(ps[2 * zq + hh][:, c0:512], tri_bf, spb[sk3][:, hh, 0:n], False, it["first"],
                           [("spb", sk3), ("cb",)], [("ps", 2 * zq + hh)])
                    if not it["first"]:
                        c0p = it["c0p"]
                        for hh in range(2):
                            mm(ps[2 * zq + hh][:, c0p:512], neg_bf, saccb[pp][:, hh, c0p:512], False, True,
                               [("saccb", pp), ("cb",)], [("ps", 2 * zq + hh)])
                    act(wbuf[sk][:, :, 0:n], ZZ[zq][:, :, c0:512], AF.Exp, zk, [("wbuf", sk)])
                    if it["diag"]:
                        for hh in range(2):
                            dve_tt(wbuf[sk][:, hh, 0:128], wbuf[sk][:, hh, 0:128], msb_bf, ALU.mult,
                                   [("wbuf", sk), ("cb",)], [("wbuf", sk)])
                    if not it["last"]:
                        c0p = 512 if it["first"] else it["c0p"]
                        if c0p > c0:
                            dve_copy(sacc[:, :, c0:c0p], spb[sk3][:, :, 0:c0p - c0], [("spb", sk3)], [("sacc",)])
                        if c0p < 512:
                            dve_tt(sacc[:, :, c0p:512], sacc[:, :, c0p:512], spb[sk3][:, :, c0p - c0:n], ALU.add,
                                   [("sacc",), ("spb", sk3)], [("sacc",)])
                        dve_copy(saccb[1 - pp][:, :, c0:512], sacc[:, :, c0:512], [("sacc",)], [("saccb", 1 - pp)])

                def s3(it):
                    i, kt, c0, n = it["i"], it["kt"], it["c0"], it["n"]
                    sk = it["p"] % 3
                    Ob = 6 + i % 2
                    for hh in range(2):
                        pb = hh * 64
                        mm(ps[Ob][pb:pb + 64, c0:512], vb[:, kt, pb:pb + 64], wbuf[sk][:, hh, 0:n], it["first"], it["last"],
                           [("vb", kt // 4), ("wbuf", sk)], [("ps", Ob)])
                    if it["last"]:
                        dve_copy(oT[:, 4 + j, 512 * i:512 * i + 512], ps[Ob][:, :], [("ps", Ob)], [("oT", 4 + j, i)])

                n_it = len(items)
                L1, L2 = 1, 3
                for step in range(n_it + L2):
                    if step < n_it:
                        s1(items[step])
                    if 0 <= step - L1 < n_it:
                        s2(items[step - L1])
                    if 0 <= step - L2 < n_it:
                        s3(items[step - L2])
            S.barrier()
            A.release(mU)

        if not _SKIP_C:
            mC = A.mark()
            state["PS"] = Rot([0, 1, 2, 3, 6, 7])
            wuq = A.alloc((3, 768), BF16)
            wsw = A.alloc((3, 8, 32), BF16)
            wukv = A.alloc((2, 1024), BF16)
            CC = A.alloc((S_LEN,), F32)
            SS = A.alloc((S_LEN,), F32)
            Qh = [A.alloc((S_LEN,), BF16) for _ in range(2)]
            Kh = [A.alloc((S_LEN,), BF16) for _ in range(2)]
            Vh = [A.alloc((16, 128), BF16) for _ in range(2)]
            pbuf = [A.alloc((512,), BF16) for _ in range(6)]
            rec = A.alloc((512,), F32)
            t1 = A.alloc((512,), F32)
            t2 = A.alloc((512,), F32)
            dma("pool", wuq[:, :, :], w_uq_d[:, :, :], [], [("wuq",)])
            wuq4 = w_uq_d.rearrange("p c (h f) -> p c h f", h=8)
            for kc in range(3):
                dma("pool", wsw[:, kc, :, 0:16], wuq4[:, kc, :, 80:96], [], [("wsw", 0)])
                dma("pool", wsw[:, kc, :, 16:32], wuq4[:, kc, :, 64:80], [], [("wsw", 1)])
            dma("pool", wukv[:, :, :], w_ukv_d[:, :, :], [], [("wukv",)])
            dma("sp", CC[:, :], cc_d[:, :], [], [("CC",)])
            dma("sp", SS[:, :], ss_d[:, :], [], [("SS",)])
            for sl in range(2):
                S.op("dve", (lambda sl: lambda e: e.memset(Vh[sl][:, :, 64:128], 1.0))(sl), reads=[], writes=[("Vh1", sl)])
            gcount = 0
            pcount = 0
            scale_a = 96.0 ** -0.5
            for h in range(8):
                sl = h % 2
                j = h // 2
                pb = (h % 2) * 64
                for tb in range(4):
                    ts = slice(tb * 512, (tb + 1) * 512)
                    bA = state["PS"].next()
                    for kc in range(3):
                        mm(ps[bA][0:96, :], wuq[:, kc, h * 96:(h + 1) * 96], cqn[:, kc, ts], kc == 0, kc == 2,
                           [("wuq",), ("cqn", kc, tb)], [("ps", bA)])
                    bB = state["PS"].next()
                    for kc in range(3):
                        mm(ps[bB][64:96, :], wsw[:, kc, h, :], cqn[:, kc, ts], kc == 0, kc == 2,
                           [("wsw", 0), ("wsw", 1), ("cqn", kc, tb)], [("ps", bB)])
                    act_copy(Qh[sl][0:64, ts], ps[bA][0:64, :], [("ps", bA)], [("Qh", sl, tb)])
                    dve_tt(t1[64:96, :], ps[bA][64:96, :], CC[64:96, ts], ALU.mult, [("ps", bA), ("CC",)], [("t1",)])
                    dve_tt(t2[64:96, :], ps[bB][64:96, :], SS[64:96, ts], ALU.mult, [("ps", bB), ("SS",)], [("t2",)])
                    dve_tt(Qh[sl][64:96, ts], t1[64:96, :], t2[64:96, :], ALU.add, [("t1",), ("t2",)], [("Qh", sl, tb)])
                    bK = state["PS"].next()
                    for kc in range(2):
                        mm(ps[bK][0:64, :], wukv[:, kc, h * 128:h * 128 + 64], ckvn[:, kc, ts], kc == 0, kc == 1,
                           [("wukv",), ("ckvn", kc, tb)], [("ps", bK)])
                    act_copy(Kh[sl][0:64, ts], ps[bK][0:64, :], [("ps", bK)], [("Kh", sl, tb)])
                    pool_copy(Kh[sl][64:96, ts], krope[0:32, ts], [("krope", tb)], [("Kh", sl, tb)])
                for half in range(2):
                    b = state["PS"].next()
                    for q in range(8):
                        tile = half * 8 + q
                        for kc in range(2):
                            mm(ps[b][:, q * 64:(q + 1) * 64], ckvn[:, kc, tile * 128:(tile + 1) * 128],
                               wukv[:, kc, h * 128 + 64:h * 128 + 128], kc == 0, kc == 1,
                               [("wukv",), ("ckvn", kc, tile // 4)], [("ps", b)])
                    dve_copy(Vh[sl][:, half * 8:(half + 1) * 8, 0:64], ps[b][:, :].rearrange("p (a b) -> p a b", a=8),
                             [("ps", b)], [("Vh", sl, half)])
                items = []
                for i in range(4):
                    g = gcount
                    gcount += 1
                    for kt in range(0, 4 * i + 4):
                        c0 = max(0, 128 * kt - 512 * i)
                        items.append(dict(i=i, kt=kt, c0=c0, n=512 - c0, first=(kt == 0), last=(kt == 4 * i + 3),
                                          diag=(kt >= 4 * i), g=g, p=pcount))
                        pcount += 1

                def c1(it):
                    i, kt, c0, n = it["i"], it["kt"], it["c0"], it["n"]
                    zb = state["PS"].next()
                    it["zb"] = zb
                    mm(ps[zb][:, c0:512], Kh[sl][0:96, kt * 128:(kt + 1) * 128], Qh[sl][0:96, 512 * i + c0:512 * i + 512],
                       True, True, [("Kh", sl, kt // 4), ("Qh", sl, i)], [("ps", zb)])
                    pk = it["p"] % 6
                    act(pbuf[pk][:, 0:n], ps[zb][:, c0:512], AF.Exp, [("ps", zb)], [("pbuf", pk)], scale=scale_a)
                    if it["diag"]:
                        dve_tt(pbuf[pk][:, 0:128], pbuf[pk][:, 0:128], mmla_bf, ALU.mult, [("pbuf", pk), ("cb",)], [("pbuf", pk)])

                def c2(it):
                    i, kt, c0, n, g = it["i"], it["kt"], it["c0"], it["n"], it["g"]
                    pk = it["p"] % 6
                    Ob = 4 + g % 2
                    mm(ps[Ob][:, c0:512], Vh[sl][:, kt, :], pbuf[pk][:, 0:n], it["first"], it["last"],
                       [("Vh", sl, kt // 8), ("Vh1", sl), ("pbuf", pk)], [("ps", Ob)])
                    if it["last"]:
                        S.op("dve", (lambda Ob: lambda e: e.reciprocal(rec[64:128, :], ps[Ob][64:128, :]))(Ob),
                             reads=[("ps", Ob)], writes=[("rec",)])
                        dve_tt(oT[pb:pb + 64, j, 512 * i:512 * i + 512], ps[Ob][0:64, :], rec[64:128, :], ALU.mult,
                               [("ps", Ob), ("rec",)], [("oT", j, i)])

                n_it = len(items)
                LC = 4
                for step in range(n_it + LC):
                    if step < n_it:
                        c1(items[step])
                    if 0 <= step - LC < n_it:
                        c2(items[step - LC])
            S.barrier()
            A.release(mC)
        state["PS"] = Rot(range(8))
        outproj(w_eo_d, oT)
        S.barrier()
        A.release(m0)

    def mix1():
        S.barrier()
        state["PS"] = Rot([0, 1, 2, 3])
        m0 = A.mark()
        oT = A.alloc((8, S_LEN), BF16)
        m1 = A.mark()
        uT = A.alloc((8, S_LEN), BF16)
        wq1 = [A.alloc((8, 384), BF16) for _ in range(2)]
        q1 = [A.alloc((S_LEN,), BF16) for _ in range(2)]
        k1 = [A.alloc((S_LEN,), BF16) for _ in range(2)]
        v1 = [A.alloc((16, 2, 128), BF16) for _ in range(2)]
        btab = [A.alloc((640,), F32) for _ in range(2)]
        expB = [A.alloc((640,), BF16) for _ in range(2)]
        mask1 = A.alloc((640,), F32)
        nmask1 = A.alloc((640,), F32)
        pbuf = [A.alloc((512,), BF16) for _ in range(6)]
        rec = A.alloc((512,), F32)
        dma("sp", mask1[:, :], mask1_d[:, :], [], [("mask1",)])
        dma("sp", nmask1[:, :], nmask1_d[:, :], [], [("nmask1",)])
        for sl in range(2):
            S.op("dve", (lambda sl: lambda e: e.memset(v1[sl][:, :, :, 64:128], 1.0))(sl), reads=[], writes=[("v11", sl)])
        for tb in range(4):
            norm_main(tb, G_MIX1, uT[:, :, tb * 512:(tb + 1) * 512], lambda c, tb=tb: ("uT", c, tb))
        gcount = 0
        pcount = 0
        for jp in range(8):
            sl = jp % 2
            for part in range(3):
                dma("pool", wq1[sl][:, :, part * 128:(part + 1) * 128],
                    w_qkv_d[:, :, part * 1024 + jp * 128:part * 1024 + (jp + 1) * 128], [], [("wq1", sl, part)])
            for tb in range(4):
                ts = slice(tb * 512, (tb + 1) * 512)
                b = state["PS"].next()
                for c in range(8):
                    mm(ps[b][:, :], wq1[sl][:, c, 0:128], uT[:, c, ts], c == 0, c == 7,
                       [("wq1", sl, 0), ("uT", c, tb)], [("ps", b)])
                act_mul(q1[sl][:, ts], ps[b][:, :], 0.125, [("ps", b)], [("q1", sl, tb)])
                b = state["PS"].next()
                for c in range(8):
                    mm(ps[b][:, :], wq1[sl][:, c, 128:256], uT[:, c, ts], c == 0, c == 7,
                       [("wq1", sl, 1), ("uT", c, tb)], [("ps", b)])
                act_copy(k1[sl][:, ts], ps[b][:, :], [("ps", b)], [("k1", sl, tb)])
            for tq in range(4):
                b = state["PS"].next()
                for q in range(4):
                    tile = tq * 4 + q
                    for c in range(8):
                        mm(ps[b][:, q * 128:(q + 1) * 128], uT[:, c, tile * 128:(tile + 1) * 128], wq1[sl][:, c, 256:384],
                           c == 0, c == 7, [("wq1", sl, 2), ("uT", c, tile // 4)], [("ps", b)])
                dve_copy(v1[sl][:, tq * 4:(tq + 1) * 4, :, 0:64],
                         ps[b][:, :].rearrange("p (a h b) -> p a h b", a=4, h=2), [("ps", b)], [("v1", sl, tq)])
            for hh in range(2):
                h = 2 * jp + hh
                bs = hh
                dma("sp", btab[bs][:, :], btab_d[h], [], [("btab", bs)])
                dve_tt(btab[bs][:, :], btab[bs][:, :], mask1[:, :], ALU.mult, [("btab", bs), ("mask1",)], [("btab", bs)])
                dve_tt(expB[bs][:, :], btab[bs][:, :], nmask1[:, :], ALU.add, [("btab", bs), ("nmask1",)], [("expB", bs)])
            if True:
                items = []
                for i in range(4):
                    kts = list(range(max(0, 4 * i - 4), 4 * i + 4))
                    for kt in kts:
                        for hh in range(2):
                            tlo = max(128 * kt, 512 * i)
                            thi = min(128 * kt + 640, 512 * i + 512)
                            items.append(dict(i=i, kt=kt, tlo=tlo, thi=thi, n=thi - tlo, tl0=tlo - 128 * kt, hh=hh,
                                              first=(kt == kts[0]), last=(kt == kts[-1]), g=hh, p=pcount))
                            pcount += 1

                def d1(it):
                    pb = it["hh"] * 64
                    bs = it["hh"]
                    i, kt, tlo, thi, n = it["i"], it["kt"], it["tlo"], it["thi"], it["n"]
                    zb = state["PS"].next()
                    it["zb"] = zb
                    for _d in range(N_DUMMY_M1):
                        mm(ps[zb][:, 0:n], ones_bf, uT[:, _d, tlo:thi], True, True, [("cb",)], [("ps", zb)])
                    mm(ps[zb][:, 0:n], k1[sl][pb:pb + 64, kt * 128:(kt + 1) * 128], q1[sl][pb:pb + 64, tlo:thi],
                       True, False, [("k1", sl, kt // 4), ("q1", sl, i)], [("ps", zb)])
                    mm(ps[zb][:, 0:n], ident_bf, expB[bs][:, it["tl0"]:it["tl0"] + n],
                       False, True, [("expB", bs), ("cb",)], [("ps", zb)])
                    pk = it["p"] % 6
                    act(pbuf[pk][:, 0:n], ps[zb][:, 0:n], AF.Exp, [("ps", zb)], [("pbuf", pk)])

                def d2(it):
                    hh = it["hh"]
                    pb = hh * 64
                    i, kt, tlo, thi, n, g = it["i"], it["kt"], it["tlo"], it["thi"], it["n"], it["g"]
                    pk = it["p"] % 6
                    Ob = 4 + hh + 2 * (i % 2)
                    mm(ps[Ob][:, tlo - 512 * i:thi - 512 * i], v1[sl][:, kt, hh, :], pbuf[pk][:, 0:n], it["first"], it["last"],
                       [("v1", sl, kt // 4), ("v11", sl), ("pbuf", pk)], [("ps", Ob)])
                    if it["last"]:
                        S.op("dve", (lambda Ob: lambda e: e.reciprocal(rec[64:128, :], ps[Ob][64:128, :]))(Ob),
                             reads=[("ps", Ob)], writes=[("rec",)])
                        dve_tt(oT[pb:pb + 64, jp, 512 * i:512 * i + 512], ps[Ob][0:64, :], rec[64:128, :], ALU.mult,
                               [("ps", Ob), ("rec",)], [("oT", jp, i)])

                n_it = len(items)
                LD = 4
                for step in range(n_it + LD):
                    if step < n_it:
                        d1(items[step])
                    if 0 <= step - LD < n_it:
                        d2(items[step - LD])
        S.barrier()
        A.release(m1)
        state["PS"] = Rot(range(8))
        outproj(w_oo_d, oT)
        S.barrier()
        A.release(m0)

    for st in stages:
        if st == "mix0":
            mix0()
        elif st == "ffn0":
            ffn(0, G_FFN0)
        elif st == "mix1":
            mix1()
        elif st == "ffn1":
            ffn(1, G_FFN1, trailing_barrier=(st != stages[-1]))
    if not (stages and stages[-1] == "ffn1"):
        S.barrier()
        state["PS"] = Rot(range(8))
    for tb in range(4):
        t0 = tb * 512
        so = hT[:, :, t0:t0 + 512]
        if final_norm:
            act(sq[:, :, :], hT[:, :, t0:t0 + 512], AF.Square, [("h", c, tb) for c in range(8)], [("sq",)])
            bank = state["PS"].next()
            for c in range(8):
                mm(ps[bank][:, :], ones_bf, sq[:, c, :], c == 0, c == 7, [("sq",), ("cb",)], [("ps", bank)])
            rstd_from(bank, 1.0 / D)
            for c in range(8):
                S.op("dve", (lambda c, so, t0: lambda e: e.scalar_tensor_tensor(
                    so[:, c, :], hT[:, c, t0:t0 + 512], gvec[:, G_FINAL + c:G_FINAL + c + 1], rstd[:, :],
                    ALU.mult, ALU.mult))(c, so, t0),
                    reads=[("h", c, tb), ("rstd",), ("gvec",)], writes=[("h", c, tb)])
            tok = dma("sp", outT_v[:, :, t0:t0 + 512], so, [("h", c, tb) for c in range(8)], [("out", tb)])
        else:
            tok = dma("sp", outT_v[:, :, t0:t0 + 512], hT[:, :, t0:t0 + 512], [("h", c, tb) for c in range(8)], [("out", tb)])
        S.final_dma.append(tok)

    S.finalize(nc)
    with nc.Block() as block:
        @block.tensor
        def _(e):
            S.emit("pe", e)

        @block.scalar
        def _(e):
            S.emit("act", e)

        @block.vector
        def _(e):
            S.emit("dve", e)

        @block.gpsimd
        def _(e):
            S.emit("pool", e)

        @block.sync
        def _(e):
            S.emit("sp", e)
    S.close()
    nc._arena_peak = A.peak if False else None
    return nc


def _consts():
    s = np.arange(128)[:, None]
    t = np.arange(128)[None, :]
    cm = np.zeros((128, NCM), np.float32)
    cm[:, C_ONES:C_ONES + 128] = 1.0
    cm[:, C_NEG:C_NEG + 128] = -1.0
    cm[:, C_TRI:C_TRI + 128] = np.where(s >= t, -1.0, 0.0)
    cm[:, C_MSB:C_MSB + 128] = np.where(s < t, 1.0, 0.0)
    cm[:, C_MMLA:C_MMLA + 128] = np.where((s // 64) <= (t // 64), 1.0, 0.0)
    cm[:, C_ID:C_ID + 128] = np.eye(128, dtype=np.float32)
    pos = np.arange(S_LEN, dtype=np.float32)
    inv_freq = (np.float32(10000.0) ** (-np.arange(0, 32, 2, dtype=np.float32) / np.float32(32))).astype(np.float32)
    ang = (pos[:, None] * inv_freq[None, :]).astype(np.float32)
    cos = np.cos(ang).astype(np.float32).T
    sin = np.sin(ang).astype(np.float32).T
    cc = np.concatenate([cos, cos], 0)
    ss = np.concatenate([-sin, sin], 0)
    cc4 = np.ascontiguousarray(np.tile(cc, (4, 1)))
    ss4 = np.ascontiguousarray(np.tile(ss, (4, 1)))
    sl = np.arange(128)[:, None]
    tl = np.arange(640)[None, :]
    d = tl // 64 - sl // 64
    mask1 = ((d >= 0) & (d <= 8)).astype(np.float32)
    bidx = np.clip(tl - sl, -256, 256) + 256
    return cm, cc4, ss4, mask1, bidx


def _gcol(g):
    return np.ascontiguousarray(np.asarray(g, np.float32).reshape(-1, 128).T)


_CACHE = {}


def run(inputs, stages=("mix0", "ffn0", "mix1", "ffn1"), final_norm=True, trace=False):
    f = lambda a: np.ascontiguousarray(np.asarray(a, dtype=np.float32))
    x = f(inputs["x"])
    cm, cc4, ss4, mask1, bidx = _consts()
    gvec = np.concatenate([
        _gcol(inputs["g_mix"][0]), _gcol(inputs["g_ffn"][0]), _gcol(inputs["g_mix"][1]), _gcol(inputs["g_ffn"][1]),
        _gcol(inputs["g_final"]), _gcol(inputs["ev_g_cq"][0]), _gcol(inputs["ev_g_ckv"][0])], axis=1)
    gvec = np.ascontiguousarray(gvec.astype(np.float32))
    assert gvec.shape == (128, NG)
    rb = f(inputs["od_rel_bias"])[0]
    bias_tab = np.ascontiguousarray(rb[:, bidx])
    shared = {
        "ev_w_in": f(inputs["ev_w_in"])[0], "ev_w_uq": f(inputs["ev_w_uq"])[0], "ev_w_ukv": f(inputs["ev_w_ukv"])[0],
        "ev_w_out": f(inputs["ev_w_out"])[0], "od_w_qkv": f(inputs["od_w_qkv"])[0], "od_w_out": f(inputs["od_w_out"])[0],
        "w_gate": f(inputs["w_gate"]), "w_up": f(inputs["w_up"]), "w_down": f(inputs["w_down"]),
        "gvec": gvec, "cmat": cm, "rope_cc": cc4, "rope_ss": ss4, "bias_tab": bias_tab, "mask1": mask1,
        "nmask1": np.ascontiguousarray((mask1 - 1.0) * 30000.0).astype(np.float32),
    }
    key = (tuple(stages), final_norm)
    if key not in _CACHE:
        _CACHE[key] = build(stages, final_norm)
    nc = _CACHE[key]
    in_maps = []
    for b in range(NCORES):
        m = dict(shared)
        m["xT"] = np.ascontiguousarray(x[b].T)
        in_maps.append(m)
    res = run_bass_kernel_spmd(nc, in_maps, core_ids=list(range(NCORES)), **({"trace": True} if trace else {}))
    out = np.stack([np.ascontiguousarray(np.asarray(r["outT"]).T) for r in res.results], axis=0)
    return out.astype(np.float32), res


def kernel(**inputs):
    out, _ = run(inputs)
    return out
```

```python
import numpy as np
import concourse.bass as bass
import concourse.mybir as mybir
from concourse.bass_utils import run_bass_kernel_spmd

F32 = mybir.dt.float32
BF16 = mybir.dt.bfloat16
U8 = mybir.dt.uint8
ALU = mybir.AluOpType
AF = mybir.ActivationFunctionType

import os as _os
_SKIP_B = bool(_os.environ.get("K_SKIP_B"))
N_DUMMY_SB = int(_os.environ.get("K_DUMMY_SB", "3"))
N_DUMMY_M1 = int(_os.environ.get("K_DUMMY_M1", "1"))
_SKIP_C = bool(_os.environ.get("K_SKIP_C"))
S_LEN = 2048
D = 1024
DFF = 2816
EPS = 1e-6
NCORES = 8
ENGS = ("pe", "act", "dve", "pool", "sp")
NDMA = 24
NDMA_POOL = 16
SEM_LIMIT = 3000
NSEM_PER_ENG = 10


DEBUG_NAMES = {}


class Op:
    __slots__ = ("eng", "fn", "deps", "dmadeps", "marked", "sem", "cnt", "idx", "dma", "tag")

    def __init__(self, eng, fn):
        self.eng = eng
        self.fn = fn
        self.deps = {}
        self.dmadeps = {}
        self.marked = False
        self.sem = None
        self.cnt = 0
        self.idx = -1
        self.dma = None


class Sched:
    def __init__(self):
        self.ops = {e: [] for e in ENGS}
        self.last_w = {}
        self.readers = {}
        self.dma_use = [0] * NDMA
        self.dma_rr = 0
        self.dma_rr_pool = 0
        self.pending = {e: [] for e in ENGS}
        self.final_dma = []

    def _add_dep(self, o, tok):
        if tok is None:
            return
        if not isinstance(tok, tuple) and tok.dma is not None:
            tok = tok.dma
        if isinstance(tok, tuple):
            i, v = tok
            if o.dmadeps.get(i, 0) < v:
                o.dmadeps[i] = v
            return
        if tok.eng == o.eng and o.eng in ("pe", "sp"):
            return
        cur = o.deps.get(tok.eng)
        if cur is None or cur.idx < tok.idx:
            o.deps[tok.eng] = tok

    def op(self, eng, fn, reads=(), writes=(), dma=False, raw_same_only=True):
        o = Op(eng, fn)
        o.tag = (eng, tuple(reads), tuple(writes))
        o.idx = len(self.ops[eng])
        for tok in self.pending[eng]:
            self._add_dep(o, tok)
        self.pending[eng] = []
        for k in reads:
            self._add_dep(o, self.last_w.get(k))
        for k in writes:
            self._add_dep(o, self.last_w.get(k))
            for r in self.readers.get(k, ()):
                self._add_dep(o, r)
        tok = o
        if dma:
            if eng == "pool":
                i = self.dma_rr_pool
                self.dma_rr_pool = (self.dma_rr_pool + 1) % NDMA_POOL
            else:
                i = NDMA_POOL + self.dma_rr
                self.dma_rr = (self.dma_rr + 1) % (NDMA - NDMA_POOL)
            if self.dma_use[i] > 0:
                self._add_dep(o, (i, 16 * self.dma_use[i]))
            self.dma_use[i] += 1
            o.dma = (i, 16 * self.dma_use[i])
            tok = o.dma
        for k in reads:
            self.readers.setdefault(k, []).append(tok)
        for k in writes:
            self.last_w[k] = tok
            self.readers[k] = []
        self.ops[eng].append(o)
        return tok

    def barrier(self):
        toks = []
        for e in ENGS:
            if self.ops[e]:
                toks.append(self.ops[e][-1])
        for i in range(NDMA):
            if self.dma_use[i] > 0:
                toks.append((i, 16 * self.dma_use[i]))
        for e in ENGS:
            self.pending[e] = list(toks)
        self.last_w = {}
        self.readers = {}

    def finalize(self, nc):
        for e in ENGS:
            for o in self.ops[e]:
                for d in o.deps.values():
                    d.marked = True
        esems = {}
        self._ctx = []
        for e in ENGS:
            n = sum(1 for o in self.ops[e] if o.marked)
            need = max(1, (n + SEM_LIMIT - 1) // SEM_LIMIT)
            assert need <= NSEM_PER_ENG, (e, n)
            lst = []
            for k in range(need):
                cm = nc.semaphore(f"s_{e}_{k}")
                lst.append(cm.__enter__())
                self._ctx.append(cm)
            esems[e] = lst
            c = 0
            for o in self.ops[e]:
                if o.marked:
                    o.sem = lst[c // SEM_LIMIT]
                    o.cnt = c % SEM_LIMIT + 1
                    c += 1
        dsems = []
        for i in range(NDMA):
            cm = nc.semaphore(f"s_dma_{i}")
            dsems.append(cm.__enter__())
            self._ctx.append(cm)
        self.dsems = dsems

    def emit(self, eng_name, engobj):
        waited = {}
        dwaited = {}
        for o in self.ops[eng_name]:
            for src, d in o.deps.items():
                if waited.get(src, -1) >= d.idx:
                    continue
                engobj.wait_ge(d.sem, d.cnt)
                waited[src] = d.idx
            for i, v in o.dmadeps.items():
                if dwaited.get(i, 0) >= v:
                    continue
                engobj.wait_ge(self.dsems[i], v)
                dwaited[i] = v
            ins = o.fn(engobj)
            try:
                DEBUG_NAMES[ins.ins.name] = o.tag
            except Exception:
                pass
            if o.dma is not None:
                ins.then_inc(self.dsems[o.dma[0]], 16)
            elif o.marked:
                ins.then_inc(o.sem, 1)
        if eng_name == "sp":
            for (i, v) in self.final_dma:
                if dwaited.get(i, 0) < v:
                    engobj.wait_ge(self.dsems[i], v)
                    dwaited[i] = v

    def close(self):
        for cm in reversed(self._ctx):
            cm.__exit__(None, None, None)


class Arena:
    def __init__(self, nc, nbytes):
        self.t = nc.alloc_sbuf_tensor("arena", [128, nbytes], U8)
        self.n = nbytes
        self.off = 0
        self.peak = 0

    def alloc(self, shape, dtype):
        esz = 4 if dtype == F32 else 2
        n = 1
        for s in shape:
            n *= s
        nb = n * esz
        off = (self.off + 63) // 64 * 64
        assert off + nb <= self.n, ("arena overflow", off, nb, self.n)
        self.off = off + nb
        self.peak = max(self.peak, self.off)
        v = self.t[:, off:off + nb].bitcast(dtype)
        if len(shape) == 2:
            v = v.rearrange("p (a b) -> p a b", a=shape[0])
        elif len(shape) == 3:
            v = v.rearrange("p (a b c) -> p a b c", a=shape[0], b=shape[1])
        return v

    def mark(self):
        return self.off

    def release(self, m):
        self.off = m


class Rot:
    def __init__(self, items):
        self.items = list(items)
        self.i = 0

    def next(self):
        v = self.items[self.i % len(self.items)]
        self.i += 1
        return v


G_MIX0, G_FFN0, G_MIX1, G_FFN1, G_FINAL, G_CQ, G_CKV, NG = 0, 8, 16, 24, 32, 40, 43, 45
C_ONES, C_NEG, C_TRI, C_MSB, C_MMLA, C_ID, NCM = 0, 128, 256, 384, 512, 640, 768


def build(stages=("mix0", "ffn0", "mix1", "ffn1"), final_norm=True):
    nc = bass.Bass("TRN2", target_bir_lowering=False)

    def din(name, shape):
        return nc.dram_tensor(name, list(shape), F32, kind="ExternalInput").ap()

    xT_d = din("xT", [D, S_LEN])
    w_in_d = din("ev_w_in", [D, 2208]).rearrange("(c p) n -> p c n", p=128)
    w_uq_d = din("ev_w_uq", [384, 768]).rearrange("(c p) n -> p c n", p=128)
    w_ukv_d = din("ev_w_ukv", [256, 1024]).rearrange("(c p) n -> p c n", p=128)
    w_eo_d = din("ev_w_out", [D, D]).rearrange("(c p) n -> p c n", p=128)
    w_qkv_d = din("od_w_qkv", [D, 3072]).rearrange("(c p) n -> p c n", p=128)
    w_oo_d = din("od_w_out", [D, D]).rearrange("(c p) n -> p c n", p=128)
    w_gate_d = din("w_gate", [2, D, DFF])
    w_up_d = din("w_up", [2, D, DFF])
    w_down_d = din("w_down", [2, DFF, D])
    gvec_d = din("gvec", [128, NG])
    cmat_d = din("cmat", [128, NCM])
    cc_d = din("rope_cc", [128, S_LEN])
    ss_d = din("rope_ss", [128, S_LEN])
    btab_d = din("bias_tab", [16, 128, 640])
    mask1_d = din("mask1", [128, 640])
    nmask1_d = din("nmask1", [128, 640])
    outT_d = nc.dram_tensor("outT", [D, S_LEN], F32, kind="ExternalOutput").ap()
    outT_v = outT_d.rearrange("(c p) t -> p c t", p=128)
    xT_v = xT_d.rearrange("(c p) t -> p c t", p=128)

    S = Sched()
    A = Arena(nc, 211968)
    psall = nc.alloc_psum_tensor("psall", [128, 4096], F32)
    ps = [psall[:, b * 512:(b + 1) * 512] for b in range(8)]

    hT = A.alloc((8, S_LEN), F32)
    cb = A.alloc((NCM,), BF16)
    gvec = A.alloc((NG,), F32)
    sq = A.alloc((8, 512), BF16)
    rstd = A.alloc((512,), F32)
    ones_bf = cb[:, C_ONES:C_ONES + 128]
    neg_bf = cb[:, C_NEG:C_NEG + 128]
    tri_bf = cb[:, C_TRI:C_TRI + 128]
    msb_bf = cb[:, C_MSB:C_MSB + 128]
    mmla_bf = cb[:, C_MMLA:C_MMLA + 128]
    ident_bf = cb[:, C_ID:C_ID + 128]

    PS = Rot(range(8))
    state = {"PS": PS}

    def mm(out, lhsT, rhs, start, stop, reads, writes):
        S.op("pe", lambda e: e.matmul(out, lhsT, rhs, start=start, stop=stop, skip_group_check=True),
             reads=reads, writes=writes)

    def dve_tt(out, in0, in1, op, reads, writes):
        S.op("dve", lambda e: e.tensor_tensor(out, in0, in1, op), reads=reads, writes=writes)

    def dve_tss(out, in_, scalar, op, reads, writes):
        S.op("dve", lambda e: e.tensor_single_scalar(out, in_, scalar, op), reads=reads, writes=writes)

    def dve_copy(out, in_, reads, writes):
        S.op("dve", lambda e: e.tensor_copy(out, in_), reads=reads, writes=writes)

    def act(out, in_, func, reads, writes, bias=0.0, scale=1.0):
        S.op("act", lambda e: e.activation(out, in_, func, bias=bias, scale=scale), reads=reads, writes=writes)

    def act_copy(out, in_, reads, writes):
        S.op("act", lambda e: e.copy(out, in_), reads=reads, writes=writes)

    def act_mul(out, in_, m, reads, writes):
        S.op("act", lambda e: e.mul(out, in_, m), reads=reads, writes=writes)

    def pool_copy(out, in_, reads, writes):
        S.op("pool", lambda e: e.tensor_copy(out, in_), reads=reads, writes=writes)

    def dma(eng, out, in_, reads, writes):
        return S.op(eng, lambda e: e.dma_start(out=out, in_=in_), reads=reads, writes=writes, dma=True)

    dma("pool", cb[:, :], cmat_d[:, :], [], [("cb",)])
    dma("sp", gvec[:, :], gvec_d[:, :], [], [("gvec",)])
    for tb in range(4):
        dma("sp", hT[:, :, tb * 512:(tb + 1) * 512], xT_v[:, :, tb * 512:(tb + 1) * 512], [],
            [("h", c, tb) for c in range(8)])

    def rstd_from(bank, inv_n):
        S.op("act", lambda e: e.activation(rstd[:, :], ps[bank][:, :], AF.Sqrt, bias=EPS, scale=inv_n),
             reads=[("ps", bank)], writes=[("rstd",)])
        S.op("dve", lambda e: e.reciprocal(rstd[:, :], rstd[:, :]),
             reads=[("rstd",)], writes=[("rstd",)])

    def norm_main(tb, gcol, dst, dkeys):
        t0 = tb * 512
        act(sq[:, :, :], hT[:, :, t0:t0 + 512], AF.Square, [("h", c, tb) for c in range(8)], [("sq",)])
        bank = state["PS"].next()
        for c in range(8):
            mm(ps[bank][:, :], ones_bf, sq[:, c, :], c == 0, c == 7, [("sq",), ("cb",)], [("ps", bank)])
        rstd_from(bank, 1.0 / D)
        for c in range(8):
            S.op("dve", (lambda c: lambda e: e.scalar_tensor_tensor(
                dst[:, c, :], hT[:, c, t0:t0 + 512], gvec[:, gcol + c:gcol + c + 1], rstd[:, :],
                ALU.mult, ALU.mult))(c),
                reads=[("h", c, tb), ("rstd",), ("gvec",)], writes=[dkeys(c)])

    def ffn(l, gcol, trailing_barrier=True):
        S.barrier()
        state["PS"] = Rot(range(8))
        m0 = A.mark()
        uThs = [A.alloc((8, 1024), BF16) for _ in range(2)]
        actT = A.alloc((22, 1024), BF16)
        wgu = [A.alloc((2, 8, 256), BF16) for _ in range(3)]
        wdc = [A.alloc((22, 128), BF16) for _ in range(3)]
        sg = [A.alloc((512,), F32) for _ in range(2)]
        wg_v = w_gate_d[l].rearrange("(c p) f -> p c f", p=128)
        wu_v = w_up_d[l].rearrange("(c p) f -> p c f", p=128)
        wd_v = w_down_d[l].rearrange("(c p) d -> p c d", p=128)
        n_w = 0
        n_d = 0
        n_s = 0
        def ffn_norm(th):
            for tb2 in range(2):
                tb = th * 2 + tb2
                norm_main(tb, gcol, uThs[th][:, :, tb2 * 512:(tb2 + 1) * 512], lambda c, tb2=tb2, th=th: ("uTh", th, c, tb2))

        ffn_norm(0)
        for th in range(2):
            uTh = uThs[th]
            for fg in range(11):
                if th == 0 and fg == 3:
                    ffn_norm(1)
                slot = n_w % 3
                n_w += 1
                buf = wgu[slot]
                dma("pool", buf[:, 0, :, :], wg_v[:, :, fg * 256:(fg + 1) * 256], [], [("wgu", slot, 0)])
                dma("pool", buf[:, 1, :, :], wu_v[:, :, fg * 256:(fg + 1) * 256], [], [("wgu", slot, 1)])
                for fc in range(2):
                    f = fg * 2 + fc
                    for tb2 in range(2):
                        ts = slice(tb2 * 512, (tb2 + 1) * 512)
                        bg = state["PS"].next()
                        bu = state["PS"].next()
                        for c in range(8):
                            mm(ps[bg][:, :], buf[:, 0, c, fc * 128:(fc + 1) * 128], uTh[:, c, ts], c == 0, c == 7,
                               [("wgu", slot, 0), ("uTh", th, c, tb2)], [("ps", bg)])
                        for c in range(8):
                            mm(ps[bu][:, :], buf[:, 1, c, fc * 128:(fc + 1) * 128], uTh[:, c, ts], c == 0, c == 7,
                               [("wgu", slot, 1), ("uTh", th, c, tb2)], [("ps", bu)])
                        sgb = sg[n_s % 2]
                        sk = ("sg", n_s % 2)
                        n_s += 1
                        act(sgb[:, :], ps[bg][:, :], AF.Silu, [("ps", bg)], [sk])
                        dve_tt(actT[:, f, ts], ps[bu][:, :], sgb[:, :], ALU.mult, [("ps", bu), sk], [("actT", f, tb2)])
            for dc in range(8):
                slot = n_d % 3
                n_d += 1
                wb = wdc[slot]
                dma("pool", wb[:, 0:11, :], wd_v[:, 0:11, dc * 128:(dc + 1) * 128], [], [("wdc", slot)])
                dma("pool", wb[:, 11:22, :], wd_v[:, 11:22, dc * 128:(dc + 1) * 128], [], [("wdc", slot)])
                for tb2 in range(2):
                    tb = th * 2 + tb2
                    ts = slice(tb2 * 512, (tb2 + 1) * 512)
                    b = state["PS"].next()
                    for f in range(22):
                        mm(ps[b][:, :], wb[:, f, :], actT[:, f, ts], f == 0, f == 21,
                           [("wdc", slot), ("actT", f, tb2)], [("ps", b)])
                    hs = hT[:, dc, tb * 512:(tb + 1) * 512]
                    dve_tt(hs, ps[b][:, :], hs, ALU.add, [("ps", b), ("h", dc, tb)], [("h", dc, tb)])
        if trailing_barrier:
            S.barrier()
        A.release(m0)

    def outproj(w_d, oT):
        wo = A.alloc((8, 1024), BF16)
        for half in range(2):
            dma("pool", wo[:, :, half * 512:(half + 1) * 512], w_d[:, :, half * 512:(half + 1) * 512], [], [("wo", half)])
        for dc in range(8):
            for tb in range(4):
                b = state["PS"].next()
                for kc in range(8):
                    mm(ps[b][:, :], wo[:, kc, dc * 128:(dc + 1) * 128], oT[:, kc, tb * 512:(tb + 1) * 512],
                       kc == 0, kc == 7, [("wo", dc // 4), ("oT", kc, tb)], [("ps", b)])
                hs = hT[:, dc, tb * 512:(tb + 1) * 512]
                dve_tt(hs, ps[b][:, :], hs, ALU.add, [("ps", b), ("h", dc, tb)], [("h", dc, tb)])

    def mix0():
        S.barrier()
        state["PS"] = Rot(range(8))
        m0 = A.mark()
        cqn = A.alloc((3, S_LEN), BF16)
        ckvn = A.alloc((2, S_LEN), BF16)
        krope = A.alloc((S_LEN,), BF16)
        oT = A.alloc((8, S_LEN), BF16)
        mU = A.mark()
        uT = A.alloc((8, S_LEN), BF16)
        mA = A.mark()
        wA = A.alloc((8, 704), BF16)
        CCb = [A.alloc((512,), F32) for _ in range(2)]
        SSb = [A.alloc((512,), F32) for _ in range(2)]
        t1 = A.alloc((512,), F32)
        t2 = A.alloc((512,), F32)
        dma("pool", wA[:, :, 0:672], w_in_d[:, :, 0:672], [], [("wA", 0)])
        dma("pool", wA[:, :, 672:688], w_in_d[:, :, 656:672], [], [("wA", 1)])
        dma("pool", wA[:, :, 688:704], w_in_d[:, :, 640:656], [], [("wA", 2)])
        norm_main(0, G_MIX0, uT[:, :, 0:512], lambda c: ("uT", c, 0))
        for tb in range(4):
            ts = slice(tb * 512, (tb + 1) * 512)
            if tb + 1 < 4:
                norm_main(tb + 1, G_MIX0, uT[:, :, (tb + 1) * 512:(tb + 2) * 512], lambda c, tb=tb: ("uT", c, tb + 1))
            for (nch, col0, gc, dst, dname, invn) in ((3, 0, G_CQ, cqn, "cqn", 1.0 / 384),
                                                       (2, 384, G_CKV, ckvn, "ckvn", 1.0 / 256)):
                banks = []
                for j in range(nch):
                    b = state["PS"].next()
                    banks.append(b)
                    for c in range(8):
                        mm(ps[b][:, :], wA[:, c, col0 + j * 128:col0 + (j + 1) * 128], uT[:, c, ts], c == 0, c == 7,
                           [("wA", 0), ("uT", c, tb)], [("ps", b)])
                    act(sq[:, j, :], ps[b][:, :], AF.Square, [("ps", b)], [("sq",)])
                bs = state["PS"].next()
                for j in range(nch):
                    mm(ps[bs][:, :], ones_bf, sq[:, j, :], j == 0, j == nch - 1, [("sq",), ("cb",)], [("ps", bs)])
                rstd_from(bs, invn)
                for j in range(nch):
                    b = banks[j]
                    S.op("dve", (lambda j, b, dst, gc, ts: lambda e: e.scalar_tensor_tensor(
                        dst[:, j, ts], ps[b][:, :], gvec[:, gc + j:gc + j + 1], rstd[:, :], ALU.mult, ALU.mult))(j, b, dst, gc, ts),
                        reads=[("ps", b), ("rstd",), ("gvec",)], writes=[(dname, j, tb)])
            b = state["PS"].next()
            for c in range(8):
                mm(ps[b][0:64, :], wA[:, c, 640:704], uT[:, c, ts], c == 0, c == 7,
                   [("wA", 0), ("wA", 1), ("wA", 2), ("uT", c, tb)], [("ps", b)])
            cc = CCb[tb % 2]
            ss = SSb[tb % 2]
            dma("sp", cc[:, :], cc_d[:, ts], [], [("CCb", tb % 2)])
            dma("sp", ss[:, :], ss_d[:, ts], [], [("SSb", tb % 2)])
            dve_tt(t1[0:32, :], ps[b][0:32, :], cc[0:32, :], ALU.mult, [("ps", b), ("CCb", tb % 2)], [("t1",)])
            dve_tt(t2[0:32, :], ps[b][32:64, :], ss[32:64, :], ALU.mult, [("ps", b), ("SSb", tb % 2)], [("t2",)])
            dve_tt(krope[0:32, ts], t1[0:32, :], t2[0:32, :], ALU.add, [("t1",), ("t2",)], [("krope", tb)])
        S.barrier()
        A.release(mA)

        if not _SKIP_B:
            mB = A.mark()
            state["PS"] = Rot([0, 1, 2, 3, 4, 5])
            wB = A.alloc((8, 384), BF16)
            qb = A.alloc((S_LEN,), BF16)
            kb = A.alloc((S_LEN,), BF16)
            vb = A.alloc((16, 128), BF16)
            ebuf = [A.alloc((2, 512), F32) for _ in range(1)]
            spb = [A.alloc((2, 512), BF16) for _ in range(3)]
            wbuf = [A.alloc((2, 512), BF16) for _ in range(3)]
            sacc = A.alloc((2, 512), F32)
            saccb = [A.alloc((2, 512), BF16) for _ in range(2)]
            ZZ = [psall[:, q * 1024:(q + 1) * 1024].rearrange("p (a c) -> p a c", a=2) for q in range(3)]
            RR = psall[:, 2048:3072].rearrange("p (a c) -> p a c", a=2)
            pcount = 0
            for j in range(4):
                dma("pool", wB[:, :, 0:128], w_in_d[:, :, 672 + j * 128:672 + (j + 1) * 128], [], [("wB", 0)])
                dma("pool", wB[:, :, 128:256], w_in_d[:, :, 1184 + j * 128:1184 + (j + 1) * 128], [], [("wB", 1)])
                dma("pool", wB[:, :, 256:384], w_in_d[:, :, 1696 + j * 128:1696 + (j + 1) * 128], [], [("wB", 2)])
                for tb in range(4):
                    ts = slice(tb * 512, (tb + 1) * 512)
                    b = state["PS"].next()
                    for c in range(8):
                        mm(ps[b][:, :], wB[:, c, 0:128], uT[:, c, ts], c == 0, c == 7,
                           [("wB", 0), ("uT", c, tb)], [("ps", b)])
                    dve_tss(qb[:, ts], ps[b][:, :], 0.125, ALU.mult, [("ps", b)], [("qb", tb)])
                    b = state["PS"].next()
                    for c in range(8):
                        mm(ps[b][:, :], wB[:, c, 128:256], uT[:, c, ts], c == 0, c == 7,
                           [("wB", 1), ("uT", c, tb)], [("ps", b)])
                    dve_copy(kb[:, ts], ps[b][:, :], [("ps", b)], [("kb", tb)])
                for tq in range(4):
                    b = state["PS"].next()
                    for q in range(4):
                        tile = tq * 4 + q
                        for c in range(8):
                            mm(ps[b][:, q * 128:(q + 1) * 128], uT[:, c, tile * 128:(tile + 1) * 128], wB[:, c, 256:384],
                               c == 0, c == 7, [("wB", 2), ("uT", c, tile // 4)], [("ps", b)])
                    dve_copy(vb[:, tq * 4:(tq + 1) * 4, :], ps[b][:, :].rearrange("p (a b) -> p a b", a=4),
                             [("ps", b)], [("vb", tq)])
                items = []
                for i in range(4):
                    for kt in range(4 * i + 3, -1, -1):
                        c0 = max(0, 128 * kt - 512 * i)
                        c0p = max(0, 128 * (kt + 1) - 512 * i)
                        items.append(dict(i=i, kt=kt, c0=c0, c0p=c0p, n=512 - c0, first=(kt == 4 * i + 3),
                                          last=(kt == 0), diag=(kt >= 4 * i), p=pcount))
                        pcount += 1

                def s1(it):
                    i, kt, c0, n = it["i"], it["kt"], it["c0"], it["n"]
                    zq = it["p"] % 3
                    zk = [("ps", 2 * zq), ("ps", 2 * zq + 1)]
                    for _d in range(N_DUMMY_SB):
                        mm(ps[2 * zq + _d % 2][:, c0:512], ones_bf, uT[:, _d, 512 * i + c0:512 * i + 512], True, True,
                           [("cb",)], [("ps", 2 * zq + _d % 2)])
                    for hh in range(2):
                        pb = hh * 64
                        mm(ps[2 * zq + hh][:, c0:512], kb[pb:pb + 64, kt * 128:(kt + 1) * 128],
                           qb[pb:pb + 64, 512 * i + c0:512 * i + 512], True, True,
                           [("kb", kt // 4), ("qb", i)], [("ps", 2 * zq + hh)])
                    ek = 0
                    sk = it["p"] % 3
                    act(ebuf[ek][:, :, 0:n], ZZ[zq][:, :, c0:512], AF.Exp, zk, [("ebuf", ek)])
                    act(spb[sk][:, :, 0:n], ebuf[ek][:, :, 0:n], AF.Ln, [("ebuf", ek)], [("spb", sk)], bias=1.0)
                    if it["diag"]:
                        for hh in range(2):
                            dve_tt(spb[sk][:, hh, 0:128], spb[sk][:, hh, 0:128], msb_bf, ALU.mult,
                                   [("spb", sk), ("cb",)], [("spb", sk)])

                def s2(it):
                    i, kt, c0, n = it["i"], it["kt"], it["c0"], it["n"]
                    zq = it["p"] % 3
                    zk = [("ps", 2 * zq), ("ps", 2 * zq + 1)]
                    sk3 = it["p"] % 3
                    sk = it["p"] % 3
                    pp = it["p"] % 2
                    for hh in range(2):
                        mm(ps[2 * zq + hh][:, c0:512], tri_bf, spb[sk3][:, hh, 0:n], False, it["first"],
                           [("spb", sk3), ("cb",)], [("ps", 2 * zq + hh)])
                    if not it["first"]:
                        c0p = it["c0p"]
                        for hh in range(2):
                            mm(ps[2 * zq + hh][:, c0p:512], neg_bf, saccb[pp][:, hh, c0p:512], False, True,
                               [("saccb", pp), ("cb",)], [("ps", 2 * zq + hh)])
                    act(wbuf[sk][:, :, 0:n], ZZ[zq][:, :, c0:512], AF.Exp, zk, [("wbuf", sk)])
                    if it["diag"]:
                        for hh in range(2):
                            dve_tt(wbuf[sk][:, hh, 0:128], wbuf[sk][:, hh, 0:128], msb_bf, ALU.mult,
                                   [("wbuf", sk), ("cb",)], [("wbuf", sk)])
                    if not it["last"]:
                        c0p = 512 if it["first"] else it["c0p"]
                        if c0p > c0:
                            dve_copy(sacc[:, :, c0:c0p], spb[sk3][:, :, 0:c0p - c0], [("spb", sk3)], [("sacc",)])
                        if c0p < 512:
                            dve_tt(sacc[:, :, c0p:512], sacc[:, :, c0p:512], spb[sk3][:, :, c0p - c0:n], ALU.add,
                                   [("sacc",), ("spb", sk3)], [("sacc",)])
                        dve_copy(saccb[1 - pp][:, :, c0:512], sacc[:, :, c0:512], [("sacc",)], [("saccb", 1 - pp)])

                def s3(it):
                    i, kt, c0, n = it["i"], it["kt"], it["c0"], it["n"]
                    sk = it["p"] % 3
                    Ob = 6 + i % 2
                    for hh in range(2):
                        pb = hh * 64
                        mm(ps[Ob][pb:pb + 64, c0:512], vb[:, kt, pb:pb + 64], wbuf[sk][:, hh, 0:n], it["first"], it["last"],
                           [("vb", kt // 4), ("wbuf", sk)], [("ps", Ob)])
                    if it["last"]:
                        dve_copy(oT[:, 4 + j, 512 * i:512 * i + 512], ps[Ob][:, :], [("ps", Ob)], [("oT", 4 + j, i)])

                n_it = len(items)
                L1, L2 = 1, 3
                for step in range(n_it + L2):
                    if step < n_it:
                        s1(items[step])
                    if 0 <= step - L1 < n_it:
                        s2(items[step - L1])
                    if 0 <= step - L2 < n_it:
                        s3(items[step - L2])
            S.barrier()
            A.release(mU)

        if not _SKIP_C:
            mC = A.mark()
            state["PS"] = Rot([0, 1, 2, 3, 6, 7])
            wuq = A.alloc((3, 768), BF16)
            wsw = A.alloc((3, 8, 32), BF16)
            wukv = A.alloc((2, 1024), BF16)
            CC = A.alloc((S_LEN,), F32)
            SS = A.alloc((S_LEN,), F32)
            Qh = [A.alloc((S_LEN,), BF16) for _ in range(2)]
            Kh = [A.alloc((S_LEN,), BF16) for _ in range(2)]
            Vh = [A.alloc((16, 128), BF16) for _ in range(2)]
            pbuf = [A.alloc((512,), BF16) for _ in range(6)]
            rec = A.alloc((512,), F32)
            t1 = A.alloc((512,), F32)
            t2 = A.alloc((512,), F32)
            dma("pool", wuq[:, :, :], w_uq_d[:, :, :], [], [("wuq",)])
            wuq4 = w_uq_d.rearrange("p c (h f) -> p c h f", h=8)
            for kc in range(3):
                dma("pool", wsw[:, kc, :, 0:16], wuq4[:, kc, :, 80:96], [], [("wsw", 0)])
                dma("pool", wsw[:, kc, :, 16:32], wuq4[:, kc, :, 64:80], [], [("wsw", 1)])
            dma("pool", wukv[:, :, :], w_ukv_d[:, :, :], [], [("wukv",)])
            dma("sp", CC[:, :], cc_d[:, :], [], [("CC",)])
            dma("sp", SS[:, :], ss_d[:, :], [], [("SS",)])
            for sl in range(2):
                S.op("dve", (lambda sl: lambda e: e.memset(Vh[sl][:, :, 64:128], 1.0))(sl), reads=[], writes=[("Vh1", sl)])
            gcount = 0
            pcount = 0
            scale_a = 96.0 ** -0.5
            for h in range(8):
                sl = h % 2
                j = h // 2
                pb = (h % 2) * 64
                for tb in range(4):
                    ts = slice(tb * 512, (tb + 1) * 512)
                    bA = state["PS"].next()
                    for kc in range(3):
                        mm(ps[bA][0:96, :], wuq[:, kc, h * 96:(h + 1) * 96], cqn[:, kc, ts], kc == 0, kc == 2,
                           [("wuq",), ("cqn", kc, tb)], [("ps", bA)])
                    bB = state["PS"].next()
                    for kc in range(3):
                        mm(ps[bB][64:96, :], wsw[:, kc, h, :], cqn[:, kc, ts], kc == 0, kc == 2,
                           [("wsw", 0), ("wsw", 1), ("cqn", kc, tb)], [("ps", bB)])
                    act_copy(Qh[sl][0:64, ts], ps[bA][0:64, :], [("ps", bA)], [("Qh", sl, tb)])
                    dve_tt(t1[64:96, :], ps[bA][64:96, :], CC[64:96, ts], ALU.mult, [("ps", bA), ("CC",)], [("t1",)])
                    dve_tt(t2[64:96, :], ps[bB][64:96, :], SS[64:96, ts], ALU.mult, [("ps", bB), ("SS",)], [("t2",)])
                    dve_tt(Qh[sl][64:96, ts], t1[64:96, :], t2[64:96, :], ALU.add, [("t1",), ("t2",)], [("Qh", sl, tb)])
                    bK = state["PS"].next()
                    for kc in range(2):
                        mm(ps[bK][0:64, :], wukv[:, kc, h * 128:h * 128 + 64], ckvn[:, kc, ts], kc == 0, kc == 1,
                           [("wukv",), ("ckvn", kc, tb)], [("ps", bK)])
                    act_copy(Kh[sl][0:64, ts], ps[bK][0:64, :], [("ps", bK)], [("Kh", sl, tb)])
                    pool_copy(Kh[sl][64:96, ts], krope[0:32, ts], [("krope", tb)], [("Kh", sl, tb)])
                for half in range(2):
                    b = state["PS"].next()
                    for q in range(8):
                        tile = half * 8 + q
                        for kc in range(2):
                            mm(ps[b][:, q * 64:(q + 1) * 64], ckvn[:, kc, tile * 128:(tile + 1) * 128],
                               wukv[:, kc, h * 128 + 64:h * 128 + 128], kc == 0, kc == 1,
                               [("wukv",), ("ckvn", kc, tile // 4)], [("ps", b)])
                    act_copy(Vh[sl][:, half * 8:(half + 1) * 8, 0:64], ps[b][:, :].rearrange("p (a b) -> p a b", a=8),
                             [("ps", b)], [("Vh", sl, half)])
                items = []
                for i in range(4):
                    g = gcount
                    gcount += 1
                    for kt in range(0, 4 * i + 4):
                        c0 = max(0, 128 * kt - 512 * i)
                        items.append(dict(i=i, kt=kt, c0=c0, n=512 - c0, first=(kt == 0), last=(kt == 4 * i + 3),
                                          diag=(kt >= 4 * i), g=g, p=pcount))
                        pcount += 1

                def c1(it):
                    i, kt, c0, n = it["i"], it["kt"], it["c0"], it["n"]
                    zb = state["PS"].next()
                    it["zb"] = zb
                    mm(ps[zb][:, c0:512], Kh[sl][0:96, kt * 128:(kt + 1) * 128], Qh[sl][0:96, 512 * i + c0:512 * i + 512],
                       True, True, [("Kh", sl, kt // 4), ("Qh", sl, i)], [("ps", zb)])
                    pk = it["p"] % 6
                    act(pbuf[pk][:, 0:n], ps[zb][:, c0:512], AF.Exp, [("ps", zb)], [("pbuf", pk)], scale=scale_a)
                    if it["diag"]:
                        dve_tt(pbuf[pk][:, 0:128], pbuf[pk][:, 0:128], mmla_bf, ALU.mult, [("pbuf", pk), ("cb",)], [("pbuf", pk)])

                def c2(it):
                    i, kt, c0, n, g = it["i"], it["kt"], it["c0"], it["n"], it["g"]
                    pk = it["p"] % 6
                    Ob = 4 + g % 2
                    mm(ps[Ob][:, c0:512], Vh[sl][:, kt, :], pbuf[pk][:, 0:n], it["first"], it["last"],
                       [("Vh", sl, kt // 8), ("Vh1", sl), ("pbuf", pk)], [("ps", Ob)])
                    if it["last"]:
                        S.op("dve", (lambda Ob: lambda e: e.reciprocal(rec[64:128, :], ps[Ob][64:128, :]))(Ob),
                             reads=[("ps", Ob)], writes=[("rec",)])
                        dve_tt(oT[pb:pb + 64, j, 512 * i:512 * i + 512], ps[Ob][0:64, :], rec[64:128, :], ALU.mult,
                               [("ps", Ob), ("rec",)], [("oT", j, i)])

                n_it = len(items)
                LC = 4
                for step in range(n_it + LC):
                    if step < n_it:
                        c1(items[step])
                    if 0 <= step - LC < n_it:
                        c2(items[step - LC])
            S.barrier()
            A.release(mC)
        state["PS"] = Rot(range(8))
        outproj(w_eo_d, oT)
        S.barrier()
        A.release(m0)

    def mix1():
        S.barrier()
        state["PS"] = Rot([0, 1, 2, 3])
        m0 = A.mark()
        oT = A.alloc((8, S_LEN), BF16)
        m1 = A.mark()
        uT = A.alloc((8, S_LEN), BF16)
        wq1 = [A.alloc((8, 384), BF16) for _ in range(2)]
        q1 = [A.alloc((S_LEN,), BF16) for _ in range(2)]
        k1 = [A.alloc((S_LEN,), BF16) for _ in range(2)]
        v1 = [A.alloc((16, 2, 128), BF16) for _ in range(2)]
        btab = [A.alloc((640,), F32) for _ in range(2)]
        expB = [A.alloc((640,), BF16) for _ in range(2)]
        mask1 = A.alloc((640,), F32)
        nmask1 = A.alloc((640,), F32)
        pbuf = [A.alloc((512,), BF16) for _ in range(6)]
        rec = A.alloc((512,), F32)
        dma("sp", mask1[:, :], mask1_d[:, :], [], [("mask1",)])
        dma("sp", nmask1[:, :], nmask1_d[:, :], [], [("nmask1",)])
        for sl in range(2):
            S.op("dve", (lambda sl: lambda e: e.memset(v1[sl][:, :, :, 64:128], 1.0))(sl), reads=[], writes=[("v11", sl)])
        for tb in range(4):
            norm_main(tb, G_MIX1, uT[:, :, tb * 512:(tb + 1) * 512], lambda c, tb=tb: ("uT", c, tb))
        gcount = 0
        pcount = 0
        for jp in range(8):
            sl = jp % 2
            for part in range(3):
                dma("pool", wq1[sl][:, :, part * 128:(part + 1) * 128],
                    w_qkv_d[:, :, part * 1024 + jp * 128:part * 1024 + (jp + 1) * 128], [], [("wq1", sl, part)])
            for tb in range(4):
                ts = slice(tb * 512, (tb + 1) * 512)
                b = state["PS"].next()
                for c in range(8):
                    mm(ps[b][:, :], wq1[sl][:, c, 0:128], uT[:, c, ts], c == 0, c == 7,
                       [("wq1", sl, 0), ("uT", c, tb)], [("ps", b)])
                act_mul(q1[sl][:, ts], ps[b][:, :], 0.125, [("ps", b)], [("q1", sl, tb)])
                b = state["PS"].next()
                for c in range(8):
                    mm(ps[b][:, :], wq1[sl][:, c, 128:256], uT[:, c, ts], c == 0, c == 7,
                       [("wq1", sl, 1), ("uT", c, tb)], [("ps", b)])
                act_copy(k1[sl][:, ts], ps[b][:, :], [("ps", b)], [("k1", sl, tb)])
            for tq in range(4):
                b = state["PS"].next()
                for q in range(4):
                    tile = tq * 4 + q
                    for c in range(8):
                        mm(ps[b][:, q * 128:(q + 1) * 128], uT[:, c, tile * 128:(tile + 1) * 128], wq1[sl][:, c, 256:384],
                           c == 0, c == 7, [("wq1", sl, 2), ("uT", c, tile // 4)], [("ps", b)])
                act_copy(v1[sl][:, tq * 4:(tq + 1) * 4, :, 0:64],
                         ps[b][:, :].rearrange("p (a h b) -> p a h b", a=4, h=2), [("ps", b)], [("v1", sl, tq)])
            for hh in range(2):
                h = 2 * jp + hh
                bs = hh
                dma("sp", btab[bs][:, :], btab_d[h], [], [("btab", bs)])
                dve_tt(btab[bs][:, :], btab[bs][:, :], mask1[:, :], ALU.mult, [("btab", bs), ("mask1",)], [("btab", bs)])
                dve_tt(expB[bs][:, :], btab[bs][:, :], nmask1[:, :], ALU.add, [("btab", bs), ("nmask1",)], [("expB", bs)])
            if True:
                items = []
                for i in range(4):
                    kts = list(range(max(0, 4 * i - 4), 4 * i + 4))
                    for kt in kts:
                        for hh in range(2):
                            tlo = max(128 * kt, 512 * i)
                            thi = min(128 * kt + 640, 512 * i + 512)
                            items.append(dict(i=i, kt=kt, tlo=tlo, thi=thi, n=thi - tlo, tl0=tlo - 128 * kt, hh=hh,
                                              first=(kt == kts[0]), last=(kt == kts[-1]), g=hh, p=pcount))
                            pcount += 1

                def d1(it):
                    pb = it["hh"] * 64
                    bs = it["hh"]
                    i, kt, tlo, thi, n = it["i"], it["kt"], it["tlo"], it["thi"], it["n"]
                    zb = state["PS"].next()
                    it["zb"] = zb
                    for _d in range(N_DUMMY_M1):
                        mm(ps[zb][:, 0:n], ones_bf, uT[:, _d, tlo:thi], True, True, [("cb",)], [("ps", zb)])
                    mm(ps[zb][:, 0:n], k1[sl][pb:pb + 64, kt * 128:(kt + 1) * 128], q1[sl][pb:pb + 64, tlo:thi],
                       True, False, [("k1", sl, kt // 4), ("q1", sl, i)], [("ps", zb)])
                    mm(ps[zb][:, 0:n], ident_bf, expB[bs][:, it["tl0"]:it["tl0"] + n],
                       False, True, [("expB", bs), ("cb",)], [("ps", zb)])
                    pk = it["p"] % 6
                    act(pbuf[pk][:, 0:n], ps[zb][:, 0:n], AF.Exp, [("ps", zb)], [("pbuf", pk)])

                def d2(it):
                    hh = it["hh"]
                    pb = hh * 64
                    i, kt, tlo, thi, n, g = it["i"], it["kt"], it["tlo"], it["thi"], it["n"], it["g"]
                    pk = it["p"] % 6
                    Ob = 4 + hh + 2 * (i % 2)
                    mm(ps[Ob][:, tlo - 512 * i:thi - 512 * i], v1[sl][:, kt, hh, :], pbuf[pk][:, 0:n], it["first"], it["last"],
                       [("v1", sl, kt // 4), ("v11", sl), ("pbuf", pk)], [("ps", Ob)])
                    if it["last"]:
                        S.op("dve", (lambda Ob: lambda e: e.reciprocal(rec[64:128, :], ps[Ob][64:128, :]))(Ob),
                             reads=[("ps", Ob)], writes=[("rec",)])
                        dve_tt(oT[pb:pb + 64, jp, 512 * i:512 * i + 512], ps[Ob][0:64, :], rec[64:128, :], ALU.mult,
                               [("ps", Ob), ("rec",)], [("oT", jp, i)])

                n_it = len(items)
                LD = 4
                for step in range(n_it + LD):
                    if step < n_it:
                        d1(items[step])
                    if 0 <= step - LD < n_it:
                        d2(items[step - LD])
        S.barrier()
        A.release(m1)
        state["PS"] = Rot(range(8))
        outproj(w_oo_d, oT)
        S.barrier()
        A.release(m0)

    for st in stages:
        if st == "mix0":
            mix0()
        elif st == "ffn0":
            ffn(0, G_FFN0)
        elif st == "mix1":
            mix1()
        elif st == "ffn1":
            ffn(1, G_FFN1, trailing_barrier=(st != stages[-1]))
    if not (stages and stages[-1] == "ffn1"):
        S.barrier()
        state["PS"] = Rot(range(8))
    for tb in range(4):
        t0 = tb * 512
        so = hT[:, :, t0:t0 + 512]
        if final_norm:
            act(sq[:, :, :], hT[:, :, t0:t0 + 512], AF.Square, [("h", c, tb) for c in range(8)], [("sq",)])
            bank = state["PS"].next()
            for c in range(8):
                mm(ps[bank][:, :], ones_bf, sq[:, c, :], c == 0, c == 7, [("sq",), ("cb",)], [("ps", bank)])
            rstd_from(bank, 1.0 / D)
            for c in range(8):
                S.op("dve", (lambda c, so, t0: lambda e: e.scalar_tensor_tensor(
                    so[:, c, :], hT[:, c, t0:t0 + 512], gvec[:, G_FINAL + c:G_FINAL + c + 1], rstd[:, :],
                    ALU.mult, ALU.mult))(c, so, t0),
                    reads=[("h", c, tb), ("rstd",), ("gvec",)], writes=[("h", c, tb)])
            tok = dma("sp", outT_v[:, :, t0:t0 + 512], so, [("h", c, tb) for c in range(8)], [("out", tb)])
        else:
            tok = dma("sp", outT_v[:, :, t0:t0 + 512], hT[:, :, t0:t0 + 512], [("h", c, tb) for c in range(8)], [("out", tb)])
        S.final_dma.append(tok)

    S.finalize(nc)
    with nc.Block() as block:
        @block.tensor
        def _(e):
            S.emit("pe", e)

        @block.scalar
        def _(e):
            S.emit("act", e)

        @block.vector
        def _(e):
            S.emit("dve", e)

        @block.gpsimd
        def _(e):
            S.emit("pool", e)

        @block.sync
        def _(e):
            S.emit("sp", e)
    S.close()
    nc._arena_peak = A.peak if False else None
    return nc


def _consts():
    s = np.arange(128)[:, None]
    t = np.arange(128)[None, :]
    cm = np.zeros((128, NCM), np.float32)
    cm[:, C_ONES:C_ONES + 128] = 1.0
    cm[:, C_NEG:C_NEG + 128] = -1.0
    cm[:, C_TRI:C_TRI + 128] = np.where(s >= t, -1.0, 0.0)
    cm[:, C_MSB:C_MSB + 128] = np.where(s < t, 1.0, 0.0)
    cm[:, C_MMLA:C_MMLA + 128] = np.where((s // 64) <= (t // 64), 1.0, 0.0)
    cm[:, C_ID:C_ID + 128] = np.eye(128, dtype=np.float32)
    pos = np.arange(S_LEN, dtype=np.float32)
    inv_freq = (np.float32(10000.0) ** (-np.arange(0, 32, 2, dtype=np.float32) / np.float32(32))).astype(np.float32)
    ang = (pos[:, None] * inv_freq[None, :]).astype(np.float32)
    cos = np.cos(ang).astype(np.float32).T
    sin = np.sin(ang).astype(np.float32).T
    cc = np.concatenate([cos, cos], 0)
    ss = np.concatenate([-sin, sin], 0)
    cc4 = np.ascontiguousarray(np.tile(cc, (4, 1)))
    ss4 = np.ascontiguousarray(np.tile(ss, (4, 1)))
    sl = np.arange(128)[:, None]
    tl = np.arange(640)[None, :]
    d = tl // 64 - sl // 64
    mask1 = ((d >= 0) & (d <= 8)).astype(np.float32)
    bidx = np.clip(tl - sl, -256, 256) + 256
    return cm, cc4, ss4, mask1, bidx


def _gcol(g):
    return np.ascontiguousarray(np.asarray(g, np.float32).reshape(-1, 128).T)


_CACHE = {}


def run(inputs, stages=("mix0", "ffn0", "mix1", "ffn1"), final_norm=True, trace=False):
    f = lambda a: np.ascontiguousarray(np.asarray(a, dtype=np.float32))
    x = f(inputs["x"])
    cm, cc4, ss4, mask1, bidx = _consts()
    gvec = np.concatenate([
        _gcol(inputs["g_mix"][0]), _gcol(inputs["g_ffn"][0]), _gcol(inputs["g_mix"][1]), _gcol(inputs["g_ffn"][1]),
        _gcol(inputs["g_final"]), _gcol(inputs["ev_g_cq"][0]), _gcol(inputs["ev_g_ckv"][0])], axis=1)
    gvec = np.ascontiguousarray(gvec.astype(np.float32))
    assert gvec.shape == (128, NG)
    rb = f(inputs["od_rel_bias"])[0]
    bias_tab = np.ascontiguousarray(rb[:, bidx])
    shared = {
        "ev_w_in": f(inputs["ev_w_in"])[0], "ev_w_uq": f(inputs["ev_w_uq"])[0], "ev_w_ukv": f(inputs["ev_w_ukv"])[0],
        "ev_w_out": f(inputs["ev_w_out"])[0], "od_w_qkv": f(inputs["od_w_qkv"])[0], "od_w_out": f(inputs["od_w_out"])[0],
        "w_gate": f(inputs["w_gate"]), "w_up": f(inputs["w_up"]), "w_down": f(inputs["w_down"]),
        "gvec": gvec, "cmat": cm, "rope_cc": cc4, "rope_ss": ss4, "bias_tab": bias_tab, "mask1": mask1,
        "nmask1": np.ascontiguousarray((mask1 - 1.0) * 30000.0).astype(np.float32),
    }
    key = (tuple(stages), final_norm)
    if key not in _CACHE:
        _CACHE[key] = build(stages, final_norm)
    nc = _CACHE[key]
    in_maps = []
    for b in range(NCORES):
        m = dict(shared)
        m["xT"] = np.ascontiguousarray(x[b].T)
        in_maps.append(m)
    res = run_bass_kernel_spmd(nc, in_maps, core_ids=list(range(NCORES)), **({"trace": True} if trace else {}))
    out = np.stack([np.ascontiguousarray(np.asarray(r["outT"]).T) for r in res.results], axis=0)
    return out.astype(np.float32), res


def kernel(**inputs):
    out, _ = run(inputs)
    return out
```
